# Optimizing a Trainium2 kernel written in Bass

```python
import jax, jax.numpy as jnp
from jax import lax
import numpy as np

D_MODEL = 1024
BATCH = 2
SEQ = 8192
DEPTH = 1

N_GROUPS = 3
HEADS_PER_GROUP = 4
HEAD_DIM = 128
DILATED_CONFIGS = ((128, 1), (512, 4), (2048, 16))
ATTN_WIDTH = N_GROUPS * HEADS_PER_GROUP * HEAD_DIM
ATTN_OUT = HEADS_PER_GROUP * HEAD_DIM
CONV_WIDTH = D_MODEL
CONV_K = 3
D_FF = 2816
EPS = 1e-6
N_MOD = 9
IN_SIZES = (ATTN_WIDTH, ATTN_WIDTH, ATTN_WIDTH, CONV_WIDTH, CONV_WIDTH, CONV_WIDTH, D_MODEL, D_MODEL)
IN_WIDTH = sum(IN_SIZES)
IN_SPLITS = tuple(int(s) for s in np.cumsum(IN_SIZES)[:-1])

kernel_name = "hybrid_dilated_attn_shortconv_macaron_adaln"


def rms_norm(x, g):
    x32 = x.astype(jnp.float32)
    y = x32 * lax.rsqrt(jnp.mean(x32 * x32, axis=-1, keepdims=True) + EPS)
    return (y * g.astype(jnp.float32)).astype(x.dtype)


def modulate(h, shift, scale):
    return h * (1.0 + scale[:, None, :]) + shift[:, None, :]


def swiglu(h, w_gate, w_up, w_down):
    return (jax.nn.silu(h @ w_gate) * (h @ w_up)) @ w_down


def dilated_band_attention(q, k, v, window, dilation):
    B, S, H, E = q.shape
    nw = window // dilation
    L = S // dilation
    nb = -(-L // nw)
    Lp = nb * nw

    def to_blocks(t):
        t = t.reshape(B, L, dilation, H, E).transpose(0, 2, 3, 1, 4)
        t = jnp.pad(t, ((0, 0), (0, 0), (0, 0), (0, Lp - L), (0, 0)))
        return t.reshape(B, dilation, H, nb, nw, E)

    def with_prev(t):
        prev = jnp.pad(t[:, :, :, :-1], ((0, 0), (0, 0), (0, 0), (1, 0), (0, 0), (0, 0)))
        return jnp.concatenate([prev, t], axis=4)

    qb = to_blocks(q)
    kc = with_prev(to_blocks(k))
    vc = with_prev(to_blocks(v))
    s = jnp.einsum('brhnqe,brhnke->brhnqk', qb, kc) * (E ** -0.5)
    qi = jnp.arange(nw)
    kj = jnp.arange(2 * nw)
    rel = nw + qi[:, None] - kj[None, :]
    band = (rel >= 0) & (rel <= nw)
    key_pos = jnp.arange(nb)[:, None] * nw - nw + kj[None, :]
    mask = band[None, :, :] & (key_pos >= 0)[:, None, :]
    s = jnp.where(mask, s, -jnp.inf)
    m = jnp.max(s, axis=-1, keepdims=True)
    p = jnp.exp(s - m)
    denom = jnp.sum(p, axis=-1, keepdims=True)
    o = jnp.einsum('brhnqk,brhnke->brhnqe', p, vc) / denom
    lse = (m + jnp.log(denom))[..., 0]
    o = o.reshape(B, dilation, H, Lp, E)[:, :, :, :L].transpose(0, 3, 1, 2, 4).reshape(B, S, H, E)
    lse = lse.reshape(B, dilation, H, Lp)[:, :, :, :L].transpose(0, 3, 1, 2).reshape(B, S, H)
    return o, lse


def hybrid_mixer(h, w_in, q_norm, k_norm, conv_w, w_attn_branch, w_conv_branch, w_out):
    B, S, _ = h.shape
    proj = h @ w_in
    q, k, v, u, b_gate, c_gate, g_attn, g_conv = jnp.split(proj, IN_SPLITS, axis=-1)
    q = rms_norm(q.reshape(B, S, N_GROUPS, HEADS_PER_GROUP, HEAD_DIM), q_norm).astype(jnp.float32)
    k = rms_norm(k.reshape(B, S, N_GROUPS, HEADS_PER_GROUP, HEAD_DIM), k_norm).astype(jnp.float32)
    v = v.reshape(B, S, N_GROUPS, HEADS_PER_GROUP, HEAD_DIM).astype(jnp.float32)
    outs, lses = [], []
    for g, (window, dilation) in enumerate(DILATED_CONFIGS):
        o_g, lse_g = dilated_band_attention(q[:, :, g], k[:, :, g], v[:, :, g], window, dilation)
        outs.append(o_g)
        lses.append(lse_g)
    weights = jax.nn.softmax(jnp.stack(lses, axis=0), axis=0)
    o = jnp.einsum('gbsh,gbshe->bshe', weights, jnp.stack(outs, axis=0))
    y_attn = o.reshape(B, S, ATTN_OUT).astype(h.dtype) @ w_attn_branch
    xc = c_gate * u
    xp = jnp.pad(xc, ((0, 0), (CONV_K - 1, 0), (0, 0)))
    conv = xp[:, 0:S] * conv_w[0]
    for j in range(1, CONV_K):
        conv = conv + xp[:, j:j + S] * conv_w[j]
    y_conv = (b_gate * conv) @ w_conv_branch
    merged = jax.nn.sigmoid(g_attn) * y_attn + jax.nn.sigmoid(g_conv) * y_conv
    return merged @ w_out


def setup_inputs(seed: int = 0) -> dict:
    key = jax.random.key(seed)
    ks = jax.random.split(key, 24)
    f32 = jnp.float32
    L, D = DEPTH, D_MODEL

    def nrm(k, shape, scale):
        return jax.random.normal(k, shape, f32) * scale

    return {
        'x': nrm(ks[0], (BATCH, SEQ, D), 1.0),
        'c': nrm(ks[1], (BATCH, D), 1.0),
        'w_ada': nrm(ks[2], (L, D, N_MOD * D), 0.5 * D ** -0.5),
        'b_ada': nrm(ks[3], (L, N_MOD * D), 0.01),
        'norm_ffn1': 1.0 + nrm(ks[4], (L, D), 0.01),
        'ffn1_w_gate': nrm(ks[5], (L, D, D_FF), D ** -0.5),
        'ffn1_w_up': nrm(ks[6], (L, D, D_FF), D ** -0.5),
        'ffn1_w_down': nrm(ks[7], (L, D_FF, D), D_FF ** -0.5),
        'norm_mix': 1.0 + nrm(ks[8], (L, D), 0.01),
        'w_in': nrm(ks[9], (L, D, IN_WIDTH), D ** -0.5),
        'q_norm': 1.0 + nrm(ks[10], (L, HEAD_DIM), 0.01),
        'k_norm': 1.0 + nrm(ks[11], (L, HEAD_DIM), 0.01),
        'conv_w': nrm(ks[12], (L, CONV_K, CONV_WIDTH), CONV_K ** -0.5),
        'w_attn_branch': nrm(ks[13], (L, ATTN_OUT, D), ATTN_OUT ** -0.5),
        'w_conv_branch': nrm(ks[14], (L, CONV_WIDTH, D), CONV_WIDTH ** -0.5),
        'w_out': nrm(ks[15], (L, D, D), D ** -0.5),
        'norm_ffn2': 1.0 + nrm(ks[16], (L, D), 0.01),
        'ffn2_w_gate': nrm(ks[17], (L, D, D_FF), D ** -0.5),
        'ffn2_w_up': nrm(ks[18], (L, D, D_FF), D ** -0.5),
        'ffn2_w_down': nrm(ks[19], (L, D_FF, D), D_FF ** -0.5),
    }


def reference(x, c, w_ada, b_ada, norm_ffn1, ffn1_w_gate, ffn1_w_up, ffn1_w_down,
              norm_mix, w_in, q_norm, k_norm, conv_w, w_attn_branch, w_conv_branch, w_out,
              norm_ffn2, ffn2_w_gate, ffn2_w_up, ffn2_w_down):
    c_act = jax.nn.silu(c)
    for l in range(DEPTH):
        mod = c_act @ w_ada[l] + b_ada[l]
        sh1, sc1, gt1, sh2, sc2, gt2, sh3, sc3, gt3 = jnp.split(mod, N_MOD, axis=-1)
        h = modulate(rms_norm(x, norm_ffn1[l]), sh1, sc1)
        x = x + 0.5 * gt1[:, None, :] * swiglu(h, ffn1_w_gate[l], ffn1_w_up[l], ffn1_w_down[l])
        h = modulate(rms_norm(x, norm_mix[l]), sh2, sc2)
        x = x + gt2[:, None, :] * hybrid_mixer(h, w_in[l], q_norm[l], k_norm[l], conv_w[l],
                                               w_attn_branch[l], w_conv_branch[l], w_out[l])
        h = modulate(rms_norm(x, norm_ffn2[l]), sh3, sc3)
        x = x + 0.5 * gt3[:, None, :] * swiglu(h, ffn2_w_gate[l], ffn2_w_up[l], ffn2_w_down[l])
    return x
```

```python
from contextlib import ExitStack
import numpy as np
import ml_dtypes
import concourse.bass as bass
import concourse.mybir as mybir
from concourse.bass_utils import run_bass_kernel_spmd

F32 = mybir.dt.float32
BF16 = mybir.dt.bfloat16
AF = mybir.ActivationFunctionType
ALU = mybir.AluOpType
AX = mybir.AxisListType

NEG = -30000.0
EPS = 1e-6
D = 1024
KC = 8
DFF = 2816
NJ = 22
T = 2048
NT = 16
INW = 9728


class Op:
    __slots__ = ("eng", "fn", "deps", "chan", "chan_val", "needs_inc", "inc_val")

    def __init__(self, eng, fn, chan):
        self.eng = eng
        self.fn = fn
        self.chan = chan
        self.chan_val = 0
        self.deps = ()
        self.needs_inc = False
        self.inc_val = 0


class _Rec:
    def __init__(self):
        self.call = None

    def __getattr__(self, name):
        def f(*a, **kw):
            self.call = (name, a, kw)
        return f


class Prog:
    ENGS = ("pe", "act", "dve", "pool", "sp")
    BLK = {"pe": "tensor", "act": "scalar", "dve": "vector", "pool": "gpsimd", "sp": "sync"}

    def __init__(self):
        self.ops = []
        self.last_write = {}
        self.readers = {}
        self.chan_count = {}
        self.last_on_eng = {}
        self.last_on_chan = {}
        self.par_epoch = {}
        self.epoch_base = {}
        self.new_epoch = set()

    def add(self, eng, fn, reads=(), writes=(), chan=None, extra_deps=(), pwrites=()):
        if fn is not None:
            rec = _Rec()
            fn(rec)
            assert rec.call is not None
            fn = rec.call
        op = Op(eng, fn, chan)
        deps = set(extra_deps)
        for r in reads:
            deps.update(self.last_write.get(r, ()))
            if isinstance(r, tuple) and r[0] == "ps":
                deps.update(o for o in self.readers.get(r, ()) if o.eng != eng)
        for w in writes:
            deps.update(self.last_write.get(w, ()))
            deps.update(self.readers.get(w, ()))
        for w in pwrites:
            rd = self.readers.get(w, ())
            if rd or not self.par_epoch.get(w, False):
                base = set(rd) | set(self.last_write.get(w, ()))
                self.epoch_base[w] = base
                self.new_epoch.add(w)
            deps.update(self.epoch_base.get(w, ()))
        if eng == "pe":
            deps = {d for d in deps if not (d.eng == "pe" and d.chan is None)}
        deps.discard(op)
        for r in reads:
            self.readers.setdefault(r, []).append(op)
        for w in writes:
            self.last_write[w] = [op]
            self.readers[w] = []
            self.par_epoch[w] = False
        for w in pwrites:
            if w in self.new_epoch:
                self.new_epoch.discard(w)
                self.last_write[w] = [op]
                self.readers[w] = []
                self.par_epoch[w] = True
            else:
                self.last_write[w].append(op)
        if chan is not None:
            n = self.chan_count.get(chan, 0) + 1
            self.chan_count[chan] = n
            op.chan_val = 16 * n
            self.last_on_chan[chan] = op
        elif fn is not None:
            self.last_on_eng[eng] = op
        for d in deps:
            if d.chan is None:
                d.needs_inc = True
        op.deps = tuple(deps)
        self.ops.append(op)
        return op

    def barrier(self):
        deps = list(self.last_on_eng.values()) + list(self.last_on_chan.values())
        for e in self.ENGS:
            self.add(e, None, extra_deps=deps)

    def emit(self, nc):
        cnt = {e: 0 for e in self.ENGS}
        for op in self.ops:
            if op.chan is None and op.needs_inc:
                cnt[op.eng] += 1
                op.inc_val = cnt[op.eng]
        with ExitStack() as st:
            sem_eng = {e: st.enter_context(nc.semaphore("s_" + e)) for e in self.ENGS}
            sem_chan = {c: st.enter_context(nc.semaphore("c_%d" % i))
                        for i, c in enumerate(self.chan_count)}
            block = st.enter_context(nc.Block())
            for e in self.ENGS:
                ops_e = [op for op in self.ops if op.eng == e]
                if not ops_e:
                    continue

                def body(engine, ops_e=ops_e, e=e):
                    waited = {}
                    for op in ops_e:
                        need = {}
                        for d in op.deps:
                            if d.chan is not None:
                                key = ("c", d.chan)
                                s, v = sem_chan[d.chan], d.chan_val
                            else:
                                key = ("e", d.eng)
                                s, v = sem_eng[d.eng], d.inc_val
                            if need.get(key, (None, 0))[1] < v:
                                need[key] = (s, v)
                        for key, (s, v) in need.items():
                            if waited.get(key, 0) < v:
                                engine.wait_ge(s, v)
                                waited[key] = v
                        if op.fn is None:
                            continue
                        name, a, kw = op.fn
                        ins = getattr(engine, name)(*a, **kw)
                        if op.chan is not None:
                            ins.then_inc(sem_chan[op.chan], 16)
                        elif op.needs_inc:
                            ins.then_inc(sem_eng[e], 1)

                getattr(block, self.BLK[e])(body)


def build_nc(debug_stop=None):
    nc = bass.Bass("TRN2", target_bir_lowering=False)

    def din(name, shape, dt=F32):
        return nc.dram_tensor(name, list(shape), dt, kind="ExternalInput").ap()

    xo_d = din("xo", [128, NT, D])
    xh_d = din("xh", [128, NT, D])
    cT_d = din("cT", [128, KC])
    hm_d = din("hm", [128, 2])
    wada_d = din("w_ada", [D, 9 * D])
    bada_d = din("b_ada", [9 * D])
    gcol_d = din("gcols", [128, 3 * KC])
    qk_d = din("qkn", [128, 2])
    cw_d = din("cw", [128, KC, 3])
    bcol_d = din("bcol", [128, 9 * KC])
    f1g_d = din("ffn1_w_gate", [D, DFF]); f1u_d = din("ffn1_w_up", [D, DFF]); f1d_d = din("ffn1_w_down", [DFF, D])
    f2g_d = din("ffn2_w_gate", [D, DFF]); f2u_d = din("ffn2_w_up", [D, DFF]); f2d_d = din("ffn2_w_down", [DFF, D])
    win_d = din("w_in", [D, INW])
    wab_d = din("w_attn_branch", [512, D])
    wcb_d = din("w_conv_branch", [D, D])
    wout_d = din("w_out", [D, D])
    ident_d = din("ident", [128, 128], BF16)
    identf_d = din("identf", [128, 128], F32)
    masks_d = din("masks", [128, 6, 128], BF16)
    out_d = nc.dram_tensor("out", [128, NT, D], F32, kind="ExternalOutput").ap()
    x1s_d = nc.dram_tensor("x1s", [128, NT, D], F32).ap()
    h2s_d = nc.dram_tensor("h2s", [128, KC, T], BF16).ap()

    P = Prog()
    st = ExitStack()
    sb = lambda name, shape, dt: st.enter_context(nc.sbuf_tensor(name, list(shape), dt))
    A = sb("A", [128, 32768], BF16)
    B = sb("B", [128, 16384], BF16)
    C = sb("C", [128, 16384], BF16)
    Dw = sb("Dw", [128, 24576], BF16)
    E = sb("E", [128, 8192], BF16)
    ident = sb("ident_s", [128, 128], BF16)
    identf = sb("identf_s", [128, 128], F32)
    masks = sb("masks_s", [128, 8, 128], BF16)
    masksh = sb("masksh_s", [128, 3, 128], BF16)
    ones_bf = sb("ones_s", [128, 128], BF16)
    gtrows = sb("gtrows", [128, 3, D], BF16)
    cols = sb("cols", [128, 64], F32)
    gcols = sb("gcols_s", [128, 3 * KC], F32)
    qk = sb("qk_s", [128, 4], F32)
    cw = sb("cw_s", [128, KC, 3], F32)
    hm = sb("hm_s", [128, 2], F32)
    cact = sb("cact", [128, KC], F32)
    cactb = sb("cactb", [128, KC], BF16)
    bcol = sb("bcol_s", [128, 9 * KC], F32)
    crep = sb("crep", [128, KC, 128], BF16)
    ssb = sb("ssb", [128, 6 * NT], F32)
    rstd = sb("rstd", [128, 6 * NT], F32)
    mhalf = sb("mhalf", [128, NT], F32)
    pss = [st.enter_context(nc.psum_tensor("ps%d" % i, [128, 512], F32)) for i in range(8)]

    def psb(i):
        return pss[i][:].bitcast(BF16)

    xbuf = A[:].bitcast(F32).rearrange("p (n d) -> p n d", n=NT)
    hT = B[:].rearrange("p (k t) -> p k t", k=KC)

    cload = []
    def cl(dst, src, key):
        cload.append(key)
        P.add("sp", lambda e: e.dma_start(out=dst, in_=src), writes=[key], chan="const")
    cl(ident[:], ident_d, "ident"); cl(identf[:], identf_d, "identf")
    cl(masks[:, 0:6, :], masks_d, "masks"); cl(gcols[:], gcol_d, "gcols")
    cl(qk[:, 0:2], qk_d, "qk"); cl(cw[:], cw_d, "cw"); cl(hm[:], hm_d, "hm"); cl(cact[:], cT_d, "cact"); cl(bcol[:], bcol_d, "bcol")
    P.add("dve", lambda e: e.memset(ones_bf[:], 1.0), reads=cload, writes=cload + ["ones"])
    P.add("dve", lambda e: e.memset(mhalf[:], -0.5), writes=["mhalf"])
    for g in range(3):
        P.add("dve", lambda e, g=g: e.tensor_scalar(out=masksh[:, g, :], in0=masks[:, 2 * g, :],
                                                    scalar1=hm[:, 0:1], scalar2=None, op0=ALU.add),
              reads=["masks", "hm"], writes=["masksh"])
    P.add("dve", lambda e: e.tensor_scalar(out=qk[:, 2:3], in0=qk[:, 1:2], scalar1=float(128 ** -0.5),
                                           scalar2=None, op0=ALU.mult), reads=["qk"], writes=["qk"])
    P.add("act", lambda e: e.activation(out=cact[:], in_=cact[:], func=AF.Silu), reads=["cact"], writes=["cact"])
    P.add("dve", lambda e: e.tensor_copy(out=cactb[:], in_=cact[:]), reads=["cact"], writes=["cactb"])
    for k in range(KC):
        P.add("dve", lambda e, k=k: e.tensor_scalar(out=crep[:, k, :], in0=ones_bf[:], scalar1=cact[:, k:k + 1],
                                                    scalar2=None, op0=ALU.mult),
              reads=["cact", "ones"], writes=["crep"])

    def slot_views(s):
        base = Dw[:, s * 12288:(s + 1) * 12288] if s < 2 else C[:, 0:12288]
        wg = base[:, 0:4096].rearrange("p (k n) -> p k n", k=KC)
        wu = base[:, 4096:8192].rearrange("p (k n) -> p k n", k=KC)
        wd = base[:, 8192:12288].rearrange("p (j n) -> p j n", j=4)
        wa = base[:, 0:8192].rearrange("p (k n) -> p k n", k=KC)
        return wg, wu, wd, wa
    SLOTS = [slot_views(s) for s in range(3)]

    hid = [E[:, i * 2048:(i + 1) * 2048].rearrange("p (j t) -> p j t", j=4) for i in range(2)]
    sgb = [E[:, 4096 + i * 1024:4096 + (i + 1) * 1024].bitcast(F32) for i in range(2)]
    xnb = [E[:, 6144:7168], E[:, 7168:8192], C[:, 13312:14336], C[:, 14336:15360]]
    junk = C[:, 12288:13312]
    brow = C[0:1, 15360:16384]
    PIECES = [(0, 4), (4, 4), (8, 2), (10, 4), (14, 4), (18, 4)]

    pending_loads = []
    piece_ctr = [0]

    def q_piece(wg_d, wu_d, wd_d, j0, nj):
        def mk():
            idx = piece_ctr[0]; piece_ctr[0] += 1
            s = idx % 3
            wg, wu, wd, _ = SLOTS[s]
            nc_ = nj * 128
            P.add("pool", lambda e: e.dma_start(out=wg[:, :, 0:nc_],
                  in_=wg_d[:, j0 * 128:j0 * 128 + nc_].rearrange("(k p) n -> p k n", p=128)),
                  writes=[("wg", s)], chan=("wg", s))
            P.add("pool", lambda e: e.dma_start(out=wu[:, :, 0:nc_],
                  in_=wu_d[:, j0 * 128:j0 * 128 + nc_].rearrange("(k p) n -> p k n", p=128)),
                  writes=[("wu", s)], chan=("wu", s))
            P.add("pool", lambda e: e.dma_start(out=wd[:, 0:nj, :],
                  in_=wd_d[j0 * 128:j0 * 128 + nc_, :].rearrange("(j p) n -> p j n", p=128)),
                  writes=[("wd", s)], chan=("wd", s))
            return s
        pending_loads.append(mk)

    def q_mod(j):
        def mk():
            idx = piece_ctr[0]; piece_ctr[0] += 1
            s = idx % 3
            wa = SLOTS[s][3]
            P.add("pool", lambda e: e.dma_start(
                out=wa, in_=wada_d[:, j * D:(j + 1) * D].rearrange("(k p) n -> p k n", p=128)),
                writes=[("wg", s), ("wu", s)], chan=("wg", s))
            if j % 3 == 2:
                P.add("pool", lambda e: e.dma_start(out=brow, in_=bada_d[j * D:(j + 1) * D].rearrange("(o n) -> o n", o=1)),
                      writes=["brow"], chan="brow")
            return s
        pending_loads.append(mk)

    loaded = []

    def issue_load():
        if pending_loads:
            loaded.append(pending_loads.pop(0)())

    def mod_block(j, issue=True):
        s = loaded.pop(0)
        wa = SLOTS[s][3]
        role, li = j % 3, j // 3
        if role == 2:
            for hf in range(2):
                pb = 6 + hf
                for k in range(KC):
                    P.add("pe", lambda e, k=k: e.matmul(pss[pb][:], lhsT=crep[:, k, :], rhs=wa[:, k, hf * 512:(hf + 1) * 512],
                                                        start=(k == 0), stop=False),
                          reads=["crep", ("wg", s), ("wu", s)], writes=[("ps", pb)])
                P.add("pe", lambda e: e.matmul(pss[pb][:], lhsT=ones_bf[0:1, :], rhs=brow[:, hf * 512:(hf + 1) * 512],
                                               start=False, stop=True),
                      reads=["ones", "brow"], writes=[("ps", pb)])
                P.add("dve", lambda e: e.tensor_scalar(out=gtrows[:, li, hf * 512:(hf + 1) * 512], in0=pss[pb][:], scalar1=0.5,
                                                       scalar2=None, op0=ALU.mult),
                      reads=[("ps", pb)], writes=[("gt", li)])
        else:
            pb = 7
            for kc in range(KC):
                for k in range(KC):
                    P.add("pe", lambda e, k=k, kc=kc: e.matmul(pss[pb][:, kc:kc + 1], lhsT=wa[:, k, kc * 128:(kc + 1) * 128],
                                                               rhs=cactb[:, k:k + 1], start=(k == 0), stop=(k == KC - 1)),
                          reads=["cactb", ("wg", s), ("wu", s)], writes=[("ps", pb)])
            if role == 0:
                P.add("dve", lambda e: e.tensor_tensor(out=cols[:, 16 * li + 8:16 * li + 16], in0=pss[pb][:, 0:KC],
                                                       in1=bcol[:, j * KC:(j + 1) * KC], op=ALU.add),
                      reads=[("ps", pb), "bcol"], writes=["cols"])
            else:
                P.add("dve", lambda e: e.tensor_tensor(out=cols[:, 48:56], in0=pss[pb][:, 0:KC],
                                                       in1=bcol[:, j * KC:(j + 1) * KC], op=ALU.add),
                      reads=[("ps", pb), "bcol"], writes=["cols"])
                P.add("dve", lambda e: e.scalar_tensor_tensor(
                    out=cols[:, 16 * li:16 * li + 8], in0=cols[:, 48:56], scalar=1.0, in1=gcols[:, li * KC:(li + 1) * KC],
                    op0=ALU.add, op1=ALU.mult), reads=["cols", "gcols"], writes=["cols"])
        if issue:
            issue_load()

    def load_x(src_d, tg, after=()):
        P.add("sp", lambda e: e.dma_start(out=xbuf[:, 4 * tg:4 * tg + 4, :], in_=src_d[:, 4 * tg:4 * tg + 4, :]),
              reads=list(after), writes=[("x", 4 * tg + i) for i in range(4)], chan=("xg", tg))

    tctr = [0]

    ng_state = {}

    def norm_front(tg, off=0):
        c0 = off + 4 * tg
        for i in range(4):
            n = 4 * tg + i
            P.add("act", lambda e, n=n, i=i: e.activation(out=junk, in_=xbuf[:, n, :], func=AF.Square,
                                                          accum_out=ssb[:, c0 + i:c0 + i + 1]),
                  reads=[("x", n), "ssb"], writes=[("ss", off + n), "junk"])
        P.add("dve", lambda e: e.tensor_scalar(out=rstd[:, c0:c0 + 4], in0=ssb[:, c0:c0 + 4],
                                               scalar1=1.0 / D, scalar2=EPS, op0=ALU.mult, op1=ALU.add),
              reads=["ssb"] + [("ss", off + 4 * tg + i) for i in range(4)], writes=[("rs", off + tg)])
        P.add("pool", lambda e: e.tensor_tensor(out=rstd[:, c0:c0 + 4], in0=rstd[:, c0:c0 + 4],
                                                in1=mhalf[:, 0:4], op=ALU.pow),
              reads=[("rs", off + tg), "mhalf"], writes=[("rs", off + tg)])
        tiles = []
        for i in range(4):
            n = 4 * tg + i
            xi = tctr[0] % 4
            pb = 6 + (tctr[0] % 2); tctr[0] += 1
            xb = xnb[xi]
            tiles.append((n, i, xi, pb, xb))
            P.add("pool", lambda e, n=n, i=i, xb=xb: e.tensor_scalar(out=xb, in0=xbuf[:, n, :], scalar1=rstd[:, c0 + i:c0 + i + 1],
                                                                     scalar2=0.0, op0=ALU.mult, op1=ALU.add),
                  reads=[("x", n), ("rs", off + tg)], writes=[("xn", xi)])
        ng_state[(off, tg)] = tiles

    def norm_back(tg, acol, scol, off=0):
        tiles = ng_state.pop((off, tg))

        def transp(t):
            n, i, xi, pb, xb = t
            for c in range(KC):
                P.add("pe", lambda e, c=c: e.transpose(
                    out=psb(pb)[:, c * 128:(c + 1) * 128], in_=xb[:, c * 128:(c + 1) * 128], identity=ident[:]),
                    reads=[("xn", xi), "ident"], writes=[("ps", pb)])

        def evac(t):
            n, i, xi, pb, xb = t
            for c in range(KC):
                if pb == 6:
                    P.add("dve", lambda e, c=c: e.tensor_scalar(
                        out=hT[:, c, n * 128:(n + 1) * 128], in0=psb(pb)[:, c * 128:(c + 1) * 128],
                        scalar1=cols[:, acol + c:acol + c + 1], scalar2=cols[:, scol + c:scol + c + 1],
                        op0=ALU.mult, op1=ALU.add), reads=[("ps", pb), "cols"], pwrites=[("hT", n)])
                else:
                    P.add("act", lambda e, c=c: e.activation(
                        out=hT[:, c, n * 128:(n + 1) * 128], in_=psb(pb)[:, c * 128:(c + 1) * 128],
                        func=AF.Identity, scale=cols[:, acol + c:acol + c + 1], bias=cols[:, scol + c:scol + c + 1]),
                        reads=[("ps", pb), "cols"], pwrites=[("hT", n)])

        transp(tiles[0]); transp(tiles[1])
        evac(tiles[0]); transp(tiles[2])
        evac(tiles[1]); transp(tiles[3])
        evac(tiles[2]); evac(tiles[3])

    mctr = [0]
    actr = [0]

    def unit_gu(s, nj, tg, hb, js):
        wg, wu, wd, _ = SLOTS[s]
        hd = hid[hb]
        for j in js:
            pg = j % 2
            for k in range(KC):
                P.add("pe", lambda e, j=j, k=k: e.matmul(
                    pss[pg][:], lhsT=wg[:, k, j * 128:(j + 1) * 128], rhs=hT[:, k, tg * 512:(tg + 1) * 512],
                    start=(k == 0), stop=(k == KC - 1)),
                    reads=[("wg", s)] + [("hT", 4 * tg + i) for i in range(4)], writes=[("ps", pg)])
            for k in range(KC):
                P.add("pe", lambda e, j=j, k=k: e.matmul(
                    pss[2 + pg][:], lhsT=wu[:, k, j * 128:(j + 1) * 128], rhs=hT[:, k, tg * 512:(tg + 1) * 512],
                    start=(k == 0), stop=(k == KC - 1)),
                    reads=[("wu", s)] + [("hT", 4 * tg + i) for i in range(4)], writes=[("ps", 2 + pg)])
            P.add("act", lambda e: e.activation(out=sgb[pg], in_=pss[pg][:], func=AF.Silu),
                  reads=[("ps", pg)], writes=[("sg", pg)])
            P.add("dve", lambda e, j=j: e.tensor_tensor(out=hd[:, j, :], in0=pss[2 + pg][:], in1=sgb[pg], op=ALU.mult),
                  reads=[("ps", 2 + pg), ("sg", pg)], pwrites=[("hid", hb)])

    def unit_down(s, nj, tg, hb):
        wg, wu, wd, _ = SLOTS[s]
        hd = hid[hb]
        for t in range(4):
            n = 4 * tg + t
            for hf in range(2):
                pa = 4 + (actr[0] % 4); actr[0] += 1
                for j in range(nj):
                    P.add("pe", lambda e, j=j: e.matmul(
                        pss[pa][:], lhsT=hd[:, j, t * 128:(t + 1) * 128], rhs=wd[:, j, hf * 512:(hf + 1) * 512],
                        start=(j == 0), stop=(j == nj - 1)),
                        reads=[("hid", hb), ("wd", s)], writes=[("ps", pa)])
                P.add("dve", lambda e: e.tensor_tensor(
                    out=xbuf[:, n, hf * 512:(hf + 1) * 512], in0=pss[pa][:],
                    in1=xbuf[:, n, hf * 512:(hf + 1) * 512], op=ALU.add),
                    reads=[("ps", pa), ("x", n)], writes=[("x", n)])

    def new_hb():
        hb = mctr[0] % 2; mctr[0] += 1
        return hb

    def ffn_unit(s, nj, tg, mid=None):
        hb = new_hb()
        unit_gu(s, nj, tg, hb, range(nj))
        if mid is not None:
            mid()
        unit_down(s, nj, tg, hb)

    def ffn_pass(gi, pre_f, pre_b, post_f, post_b, extras, first_mid=None, nxt=None, head=None):
        if head is None:
            pre_f(0); pre_b(0); pre_f(1)
        deferred = None
        for q, (j0, nj) in enumerate(PIECES):
            s = loaded.pop(0)
            wd = SLOTS[s][2]

            def scale_wd(s=s, wd=wd, nj=nj):
                for j in range(nj):
                    P.add("dve", lambda e, j=j: e.tensor_tensor(out=wd[:, j, :], in0=wd[:, j, :], in1=gtrows[:, gi, :], op=ALU.mult),
                          reads=[("wd", s), ("gt", gi)], writes=[("wd", s)])
            mid0 = None
            if q == 0 and first_mid is not None:
                def mid0(scale_wd=scale_wd):
                    first_mid()
                    scale_wd()
            else:
                scale_wd()
            last = (q == len(PIECES) - 1)
            hbs = [None] * 4
            for tg in range(4):
                if q == 0:
                    ffn_unit(s, nj, tg, mid=(mid0 if tg == 0 else None))
                else:
                    if tg == 0:
                        hbs[0] = new_hb()
                        unit_gu(s, nj, 0, hbs[0], range(nj))
                    if tg < 3:
                        hbs[tg + 1] = new_hb()
                        unit_gu(s, nj, tg + 1, hbs[tg + 1], [0])
                    unit_down(s, nj, tg, hbs[tg])
                    if tg < 3:
                        unit_gu(s, nj, tg + 1, hbs[tg + 1], range(1, nj))
                if q == 0:
                    if tg == 0 and head is not None:
                        head()
                    if tg + 1 < 4:
                        pre_b(tg + 1)
                    if tg + 2 < 4:
                        pre_f(tg + 2)
                if last:
                    if tg >= 1:
                        post_b(tg - 1)
                    post_f(tg)
                    if nxt is not None and tg == 2:
                        nxt[0](0)
                    if nxt is not None and tg == 3:
                        nxt[1](0)
                        nxt[0](1)
            issue_load()
            if q == 0 and first_mid is not None:
                issue_load()
            for fn in extras.get(q, ()):
                fn()
        if nxt is not None:
            return lambda: post_b(3)
        post_b(3)
        return None

    def ring(s):
        base = Dw[:, s * 3072:(s + 1) * 3072]
        return [base[:, i * 1024:(i + 1) * 1024].rearrange("p (k n) -> p k n", k=KC) for i in range(3)]
    RING = [ring(s) for s in range(3)]
    wab = Dw[:, 9216:13312].rearrange("p (h n) -> p h n", h=4)
    wcb = Dw[:, 13312:21504].rearrange("p (k n) -> p k n", k=KC)
    wout = E[:, 0:8192].rearrange("p (k n) -> p k n", k=KC)

    win_v = win_d.rearrange("(k p) n -> p k n", p=128)
    rctr = [0]
    ring_extra = []
    rloads = []

    def queue_ring(colsets):
        def mk():
            s = rctr[0] % 3; rctr[0] += 1
            for i, c0 in enumerate(colsets):
                P.add("pool", lambda e, i=i, c0=c0, s=s: e.dma_start(out=RING[s][i], in_=win_v[:, :, c0:c0 + 128]),
                      writes=[("rg", s, i)] + ring_extra, chan=("rg", s, i))
            return s
        rloads.append(mk)
    rloaded = []

    def issue_ring():
        if rloads:
            rloaded.append(rloads.pop(0)())

    order = [(g, h) for h in range(4) for g in range(3)]
    for (g, h) in order:
        idx = g * 4 + h
        queue_ring([idx * 128, 1536 + idx * 128, 3072 + idx * 128])
    for fc in range(KC):
        queue_ring([4608 + fc * 128, 6656 + fc * 128, 5632 + fc * 128])
    for oc in range(KC):
        queue_ring([7680 + oc * 128, 8704 + oc * 128])

    F1 = (f1g_d, f1u_d, f1d_d)
    for j in (0, 1):
        q_mod(j)
    for q, (j0, nj) in enumerate(PIECES):
        q_piece(*F1, j0, nj)
        if q <= 2:
            q_mod(2 + q)
    for q, (j0, nj) in enumerate(PIECES):
        q_piece(*F1, j0, nj)
        if q < 4:
            q_mod(5 + q)

    load_x(xh_d, 0)
    for _ in range(3):
        issue_load()
    for tg in range(1, 4):
        load_x(xh_d, tg, after=[("wg", 0)])
    P.add("dve", lambda e: e.memset(ssb[:], 0.0), writes=["ssb"])
    for j in (0, 1):
        mod_block(j)

    def post_halo_b(tg):
        norm_back(tg, 16, 24, 16)
        P.add("sp", lambda e: e.dma_start(out=h2s_d[:, :, tg * 512:(tg + 1) * 512], in_=hT[:, :, tg * 512:(tg + 1) * 512]),
              reads=[("hT", 4 * tg + i) for i in range(4)], writes=[("h2s", tg)], chan=("h2s", tg))
        load_x(xo_d, tg)

    own_pre = (lambda tg: norm_front(tg, 32), lambda tg: norm_back(tg, 0, 8, 32))
    tail = ffn_pass(0, lambda tg: norm_front(tg, 0), lambda tg: norm_back(tg, 0, 8, 0),
                    lambda tg: norm_front(tg, 16), post_halo_b,
                    {1: [lambda: mod_block(3)], 2: [lambda: mod_block(4)]},
                    first_mid=lambda: mod_block(2, issue=False))

    def early_ring():
        ring_extra.extend([("wg", 0), ("wu", 0), ("wd", 0)])
        for _ in range(3):
            issue_ring()
        del ring_extra[:]

    def post_own_f(tg):
        P.add("sp", lambda e: e.dma_start(out=x1s_d[:, 4 * tg:4 * tg + 4, :], in_=xbuf[:, 4 * tg:4 * tg + 4, :]),
              reads=[("x", 4 * tg + i) for i in range(4)], writes=[("x1s", tg)], chan=("x1s", tg))
        norm_front(tg, 48)

    ffn_pass(0, own_pre[0], own_pre[1],
             post_own_f, lambda tg: norm_back(tg, 16, 24, 48),
             {0: [lambda: mod_block(5)], 1: [lambda: mod_block(6)], 2: [lambda: mod_block(7)], 3: [lambda: mod_block(8), early_ring]},
             head=None)
    P.barrier()

    h2o = hT
    h2h = C[:].rearrange("p (k t) -> p k t", k=KC)
    for tg in range(4):
        P.add("sp", lambda e, tg=tg: e.dma_start(out=h2h[:, :, tg * 512:(tg + 1) * 512], in_=h2s_d[:, :, tg * 512:(tg + 1) * 512]),
              reads=[("h2s", tg)], pwrites=["h2h"], chan=("h2h", tg))
    oT = A[:, 0:8192].rearrange("p (h t) -> p h t", h=4)
    QTg = A[:, 8192:10240].rearrange("p (b i) -> p b i", b=16)
    KTg = A[:, 10240:14336].rearrange("p (b i) -> p b i", b=32)
    VTg = A[:, 14336:18432].rearrange("p (b i) -> p b i", b=32)
    Vb = A[:, 18432:22528].rearrange("p (b i) -> p b i", b=32)
    num = A[:, 22528:26624].bitcast(F32)
    den = A[:, 26624:30720].bitcast(F32)
    PT = [A[:, 30720 + i * 512:30720 + (i + 1) * 512] for i in range(2)]
    PT4 = PT + [E[:, 5120:5632], E[:, 5632:6144]]
    sqb = [A[:, 31744 + i * 512:31744 + (i + 1) * 512] for i in range(2)]
    rrb = [E[:, i * 1024:(i + 1) * 1024].bitcast(F32) for i in range(2)]

    P.add("pool", lambda e: e.dma_start(out=wab, in_=wab_d.rearrange("(h p) n -> p h n", p=128)),
          writes=["wab"], chan="wab")

    def dst_src(buf, base_blk, g, tb):
        flat = buf.rearrange("p b i -> p (b i)")
        o = base_blk * 128
        if g == 2:
            return flat[:, o + tb * 512:o + (tb + 1) * 512], (lambda ap: ap)
        if g == 1:
            dst = flat[:, o:o + 2048].rearrange("p (r q a) -> p r q a", r=4, a=4)[:, :, :, tb]
            return dst, (lambda ap: ap.rearrange("p (r q) -> p r q", r=4))
        dst = flat[:, o:o + 2048].rearrange("p (q a) -> p a q", a=16)[:, 4 * tb:4 * tb + 4, :]
        return dst, (lambda ap: ap.rearrange("p (a q) -> p a q", a=4))

    pctr = [0]
    qctr = [0]
    tctr2 = [0]
    ptmp = [E[:, 2048 + i * 512:2048 + (i + 1) * 512] for i in range(6)]
    pending_fin = []

    def proj_block(w, rhs_fn, ncols, kind, gcol, dst, shape_fn, contig, src="h2o"):
        pb = pctr[0] % 3; pctr[0] += 1
        for k in range(KC):
            P.add("pe", lambda e, k=k: e.matmul(pss[pb][:, 0:ncols], lhsT=w[0][:, k, :], rhs=rhs_fn(k),
                                                start=(k == 0), stop=(k == KC - 1)),
                  reads=[w[1], src], writes=[("ps", pb)])
        key = {"q": "QTg", "k": "KTg", "v": "VTg"}[kind]

        def store(src_fn, reads, eng_kind):
            if contig:
                out_ap, wkey, pw = dst, None, [key]
            else:
                ti = tctr2[0] % 6; tctr2[0] += 1
                out_ap, wkey, pw = ptmp[ti][:, 0:ncols], ("ptmp", ti), []
            src_fn(out_ap if contig else out_ap, reads, [wkey] if wkey else [], pw)
            if not contig:
                P.add("pool", lambda e: e.tensor_copy(out=dst, in_=shape_fn(ptmp[ti][:, 0:ncols])),
                      reads=[("ptmp", ti)], pwrites=[key])

        if kind == "v":
            def src(out_ap, reads, wr, pw):
                P.add("dve", lambda e: e.tensor_copy(out=out_ap, in_=pss[pb][:, 0:ncols]),
                      reads=[("ps", pb)], writes=wr, pwrites=pw)
            store(src, None, None)
            while pending_fin:
                pending_fin.pop(0)()
            return
        i2 = qctr[0] % 2; qctr[0] += 1
        sq = sqb[i2]; rr = rrb[i2]
        P.add("act", lambda e: e.activation(out=sq[:, 0:ncols], in_=pss[pb][:, 0:ncols], func=AF.Square),
              reads=[("ps", pb)], writes=[("sq", i2)])

        def fin():
            P.add("pe", lambda e: e.matmul(pss[3 + i2][:, 0:ncols], lhsT=ones_bf[:], rhs=sq[:, 0:ncols], start=True, stop=True),
                  reads=[("sq", i2), "ones"], writes=[("ps", 3 + i2)])
            P.add("act", lambda e: e.activation(out=rr[:, 0:ncols], in_=pss[3 + i2][:, 0:ncols], func=AF.Ln,
                                                scale=(1.0 if kind == "q" else 1.0 / 128),
                                                bias=(cols[:, 57:58] if kind == "q" else cols[:, 56:57])),
                  reads=[("ps", 3 + i2), "cols"], writes=[("rr", i2)])
            P.add("act", lambda e: e.activation(out=rr[:, 0:ncols], in_=rr[:, 0:ncols], func=AF.Exp, scale=-0.5),
                  reads=[("rr", i2)], writes=[("rr", i2)])

            def src(out_ap, reads, wr, pw):
                P.add("dve", lambda e: e.scalar_tensor_tensor(out=out_ap, in0=pss[pb][:, 0:ncols], scalar=gcol,
                                                              in1=rr[:, 0:ncols], op0=ALU.mult, op1=ALU.mult),
                      reads=[("ps", pb), ("rr", i2), "qk"], writes=wr, pwrites=pw)
            store(src, None, None)
        prev = list(pending_fin)
        del pending_fin[:]
        pending_fin.append(fin)
        for f in prev:
            f()

    def flush_fin():
        while pending_fin:
            pending_fin.pop(0)()

    P.add("dve", lambda e: e.memset(cols[:, 56:57], EPS), writes=["cols"])
    P.add("dve", lambda e: e.memset(cols[:, 57:58], 128 * EPS), writes=["cols"])

    h2h4 = h2h.rearrange("p k (n q) -> p k n q", n=16)
    sctr = [0]
    for (g, h) in order:
        s = rloaded.pop(0)
        wq = (RING[s][0], ("rg", s, 0)); wk = (RING[s][1], ("rg", s, 1)); wv = (RING[s][2], ("rg", s, 2))
        for tb in range(4):
            rf = lambda k, tb=tb: h2o[:, k, tb * 512:(tb + 1) * 512]
            d_, sf = dst_src(QTg, 0, g, tb)
            proj_block(wq, rf, 512, "q", qk[:, 0:1], d_, sf, g == 2)
            d_, sf = dst_src(KTg, 16, g, tb)
            proj_block(wk, rf, 512, "k", qk[:, 1:2], d_, sf, g == 2)
            d_, sf = dst_src(VTg, 16, g, tb)
            proj_block(wv, rf, 512, "v", None, d_, sf, g == 2)
        if g == 2:
            for tb in range(4):
                rf = lambda k, tb=tb: h2h[:, k, tb * 512:(tb + 1) * 512]
                d_, sf = dst_src(KTg, 0, 2, tb)
                proj_block(wk, rf, 512, "k", qk[:, 1:2], d_, sf, True, src="h2h")
                d_, sf = dst_src(VTg, 0, 2, tb)
                proj_block(wv, rf, 512, "v", None, d_, sf, True, src="h2h")
            nhalo = 16
        elif g == 1:
            rf = lambda k: h2h4[:, k, :, 96:128]
            sf = lambda ap: ap.rearrange("p (a rb) -> p a rb", a=4)
            dv = lambda buf: buf.rearrange("p b i -> p (b i)")[:, 0:512].rearrange("p (rb a) -> p a rb", a=4)
            proj_block(wk, rf, 512, "k", qk[:, 1:2], dv(KTg), sf, False, src="h2h")
            proj_block(wv, rf, 512, "v", None, dv(VTg), sf, False, src="h2h")
            nhalo = 4
        else:
            rf = lambda k: h2h4[:, k, :, 120:128]
            sf = lambda ap: ap.rearrange("p (a b) -> p a b", a=16)
            dv = lambda buf: buf[:, 0, :].rearrange("p (b a) -> p a b", a=16)
            proj_block(wk, rf, 128, "k", qk[:, 1:2], dv(KTg), sf, False, src="h2h")
            proj_block(wv, rf, 128, "v", None, dv(VTg), sf, False, src="h2h")
            nhalo = 1
        flush_fin()
        issue_ring()
        grps = [list(range(i0, min(i0 + 8, nhalo))) for i0 in range(0, nhalo, 8)] + [list(range(16, 24)), list(range(24, 32))]
        for grp in grps:
            pb = 6 + (sctr[0] % 2); sctr[0] += 1
            for ii, blk in enumerate(grp):
                P.add("pe", lambda e, ii=ii, blk=blk, pb=pb: e.transpose(
                    out=psb(pb)[:, ii * 128:(ii + 1) * 128], in_=VTg[:, blk, :], identity=ident[:]),
                    reads=["VTg", "ident"], writes=[("ps", pb)])
            b0 = grp[0]; nb = len(grp)
            P.add("dve", lambda e, b0=b0, nb=nb, pb=pb: e.tensor_copy(
                out=Vb[:, b0:b0 + nb, :], in_=psb(pb)[:, 0:nb * 128].rearrange("p (b i) -> p b i", b=nb)),
                reads=[("ps", pb)], pwrites=["Vb"])
        def batch_info(qb4):
            qblks = [4 * qb4 + i for i in range(4)]
            info = []
            for qb in qblks:
                cur = 16 + qb
                if g == 2:
                    prev, halo = qb, True
                elif g == 1:
                    r4, n1 = qb // 4, qb % 4
                    prev, halo = (r4, True) if n1 == 0 else (16 + qb - 1, False)
                else:
                    prev, halo = (0, True) if qb == 0 else (16 + qb - 1, False)
                info.append((prev, halo, cur))
            return qblks, info

        def scores(qb4):
            qblks, info = batch_info(qb4)
            par = qb4 % 2
            for kbi in range(2):
                ps_s = 4 + 2 * par + kbi
                pt = PT4[2 * par + kbi]
                for bi, qb in enumerate(qblks):
                    prev, halo, cur = info[bi]
                    kb = prev if kbi == 0 else cur
                    if kbi == 0:
                        mk_ap = masksh[:, g, :] if halo else masks[:, 2 * g, :]
                    else:
                        mk_ap = masks[:, 2 * g + 1, :]
                    P.add("pe", lambda e, bi=bi, qb=qb, kb=kb: e.matmul(
                        pss[ps_s][:, bi * 128:(bi + 1) * 128], lhsT=KTg[:, kb, :], rhs=QTg[:, qb, :], start=True, stop=False),
                        reads=["KTg", "QTg"], writes=[("ps", ps_s)])
                    P.add("pe", lambda e, bi=bi, mk_ap=mk_ap: e.matmul(
                        pss[ps_s][:, bi * 128:(bi + 1) * 128], lhsT=ident[:], rhs=mk_ap, start=False, stop=True),
                        reads=["ident", "masks", "masksh"], writes=[("ps", ps_s)])
                P.add("act", lambda e: e.activation(out=pt, in_=pss[ps_s][:], func=AF.Exp),
                      reads=[("ps", ps_s)], writes=[("PT", 2 * par + kbi)])

        def pv(qb4):
            qblks, info = batch_info(qb4)
            par = qb4 % 2
            pn = 0 + par; pd = 2 + par
            for bi, qb in enumerate(qblks):
                prev, halo, cur = info[bi]
                for kbi in range(2):
                    kb = prev if kbi == 0 else cur
                    pt = PT4[2 * par + kbi]
                    P.add("pe", lambda e, bi=bi, kb=kb, kbi=kbi, pt=pt: e.matmul(
                        pss[pn][:, bi * 128:(bi + 1) * 128], lhsT=Vb[:, kb, :], rhs=pt[:, bi * 128:(bi + 1) * 128],
                        start=(kbi == 0), stop=(kbi == 1)),
                        reads=["Vb", ("PT", 2 * par + kbi)], writes=[("ps", pn)])
                for kbi in range(2):
                    pt = PT4[2 * par + kbi]
                    P.add("pe", lambda e, bi=bi, kbi=kbi, pt=pt: e.matmul(
                        pss[pd][:, bi * 128:(bi + 1) * 128], lhsT=ones_bf[:], rhs=pt[:, bi * 128:(bi + 1) * 128],
                        start=(kbi == 0), stop=(kbi == 1)),
                        reads=["ones", ("PT", 2 * par + kbi)], writes=[("ps", pd)])
            def canon(buf):
                if g == 2:
                    return buf[:, qb4 * 512:(qb4 + 1) * 512], (lambda ap: ap)
                if g == 1:
                    v = buf.rearrange("p (a r q) -> p r a q", a=4, r=4)[:, qb4]
                    return v, (lambda ap: ap.rearrange("p (q a) -> p a q", a=4))
                v = buf.rearrange("p (a q) -> p a q", a=16)[:, :, 32 * qb4:32 * qb4 + 32]
                return v, (lambda ap: ap.rearrange("p (q a) -> p a q", a=16))
            dn, shp = canon(num)
            dd, shp2 = canon(den)
            if g == 0:
                P.add("dve", lambda e: e.tensor_copy(out=dn, in_=shp(pss[pn][:])), reads=[("ps", pn)], writes=["num"])
                P.add("dve", lambda e: e.tensor_copy(out=dd, in_=shp2(pss[pd][:])), reads=[("ps", pd)], writes=["den"])
            else:
                P.add("dve", lambda e: e.tensor_tensor(out=dn, in0=shp(pss[pn][:]), in1=dn, op=ALU.add),
                      reads=[("ps", pn), "num"], writes=["num"])
                P.add("dve", lambda e: e.tensor_tensor(out=dd, in0=shp2(pss[pd][:]), in1=dd, op=ALU.add),
                      reads=[("ps", pd), "den"], writes=["den"])

        for qb4 in range(4):
            scores(qb4)
            if qb4 >= 1:
                pv(qb4 - 1)
        pv(3)
        if g == 2:
            P.add("act", lambda e: e.activation(out=den, in_=den, func=AF.Ln), reads=["den"], writes=["den"])
            P.add("act", lambda e: e.activation(out=den, in_=den, func=AF.Exp, scale=-1.0), reads=["den"], writes=["den"])
            P.add("dve", lambda e, h=h: e.tensor_tensor(out=oT[:, h, :], in0=num, in1=den, op=ALU.mult),
                  reads=["num", "den"], writes=["oT"])
    P.barrier()

    cvT = A[:, 8192:24576].rearrange("p (k t) -> p k t", k=KC)
    xc = A[:, 24576:28704].bitcast(F32).rearrange("p (n q) -> p n q", n=16)
    caccs = [E[:, i * 4096:(i + 1) * 4096].bitcast(F32).rearrange("p (n q) -> p n q", n=16) for i in range(2)]
    ctmp = [A[:, 28704 + i * 1024:28704 + (i + 1) * 1024].bitcast(F32) for i in range(2)]
    P.add("pool", lambda e: e.dma_start(out=wcb, in_=wcb_d.rearrange("(k p) n -> p k n", p=128)),
          writes=["wcb"], chan="wcb")
    h2h_t = h2h.rearrange("p k (n q) -> p k n q", n=16)[:, :, 14:16, 127]
    cctr = [0]

    def conv_uc(fc, s):
        wu_ = RING[s][0]; wc_ = RING[s][1]
        for tb in range(4):
            i2 = cctr[0] % 2; cctr[0] += 1
            pu = 0 + i2; pc = 2 + i2
            for k in range(KC):
                P.add("pe", lambda e, k=k: e.matmul(pss[pu][:], lhsT=wu_[:, k, :], rhs=h2o[:, k, tb * 512:(tb + 1) * 512],
                                                    start=(k == 0), stop=(k == KC - 1)),
                      reads=[("rg", s, 0), "h2o"], writes=[("ps", pu)])
            for k in range(KC):
                P.add("pe", lambda e, k=k: e.matmul(pss[pc][:], lhsT=wc_[:, k, :], rhs=h2o[:, k, tb * 512:(tb + 1) * 512],
                                                    start=(k == 0), stop=(k == KC - 1)),
                      reads=[("rg", s, 1), "h2o"], writes=[("ps", pc)])
            P.add("act", lambda e: e.activation(out=ctmp[i2], in_=pss[pc][:], func=AF.Copy),
                  reads=[("ps", pc)], writes=[("ctmp", i2)])
            P.add("dve", lambda e: e.tensor_tensor(
                out=xc[:, 4 * tb:4 * tb + 4, 1:129], in0=pss[pu][:].rearrange("p (n q) -> p n q", n=4),
                in1=ctmp[i2].rearrange("p (n q) -> p n q", n=4), op=ALU.mult),
                reads=[("ps", pu), ("ctmp", i2)], pwrites=["xc"])
        for k in range(KC):
            P.add("pe", lambda e, k=k: e.matmul(pss[6][:, 0:2], lhsT=wu_[:, k, :], rhs=h2h_t[:, k, :],
                                                start=(k == 0), stop=(k == KC - 1)),
                  reads=[("rg", s, 0), "h2h"], writes=[("ps", 6)])
        for k in range(KC):
            P.add("pe", lambda e, k=k: e.matmul(pss[7][:, 0:2], lhsT=wc_[:, k, :], rhs=h2h_t[:, k, :],
                                                start=(k == 0), stop=(k == KC - 1)),
                  reads=[("rg", s, 1), "h2h"], writes=[("ps", 7)])
        P.add("act", lambda e: e.activation(out=cols[:, 58:60], in_=pss[7][:, 0:2], func=AF.Copy),
              reads=[("ps", 7)], writes=["cols"])
        P.add("dve", lambda e: e.scalar_tensor_tensor(out=xc[:, 14:16, 0], in0=pss[6][:, 0:2], scalar=hm[:, 1:2],
                                                      in1=cols[:, 58:60], op0=ALU.mult, op1=ALU.mult),
              reads=[("ps", 6), "cols", "hm"], writes=["xc"])

    def conv_taps(fc):
        cacc = caccs[fc % 2]; ck = ("cacc", fc % 2)
        w0 = cw[:, fc, 0:1]; w1 = cw[:, fc, 1:2]; w2 = cw[:, fc, 2:3]
        P.add("dve", lambda e: e.tensor_scalar(out=cacc[:, :, :], in0=xc[:, :, 1:129], scalar1=w2, scalar2=None, op0=ALU.mult),
              reads=["xc", "cw"], writes=[ck])
        P.add("dve", lambda e: e.scalar_tensor_tensor(out=cacc[:, 1:16, :], in0=xc[:, 0:15, 1:129], scalar=w1,
                                                      in1=cacc[:, 1:16, :], op0=ALU.mult, op1=ALU.add),
              reads=["xc", "cw", ck], writes=[ck])
        P.add("dve", lambda e: e.scalar_tensor_tensor(out=cacc[:, 0, :], in0=xc[:, 15, 0:128], scalar=w1,
                                                      in1=cacc[:, 0, :], op0=ALU.mult, op1=ALU.add),
              reads=["xc", "cw", ck], writes=[ck])
        P.add("dve", lambda e: e.scalar_tensor_tensor(out=cacc[:, 2:16, :], in0=xc[:, 0:14, 1:129], scalar=w0,
                                                      in1=cacc[:, 2:16, :], op0=ALU.mult, op1=ALU.add),
              reads=["xc", "cw", ck], writes=[ck])
        P.add("dve", lambda e: e.scalar_tensor_tensor(out=cacc[:, 0:2, :], in0=xc[:, 14:16, 0:128], scalar=w0,
                                                      in1=cacc[:, 0:2, :], op0=ALU.mult, op1=ALU.add),
              reads=["xc", "cw", ck], writes=[ck])

    def conv_b(fc, s):
        wb_ = RING[s][2]
        cacc = caccs[fc % 2]; ck = ("cacc", fc % 2)
        for tb in range(4):
            pbk = 4 + tb
            for k in range(KC):
                P.add("pe", lambda e, k=k: e.matmul(pss[pbk][:], lhsT=wb_[:, k, :], rhs=h2o[:, k, tb * 512:(tb + 1) * 512],
                                                    start=(k == 0), stop=(k == KC - 1)),
                      reads=[("rg", s, 2), "h2o"], writes=[("ps", pbk)])
            P.add("dve", lambda e: e.tensor_tensor(
                out=cvT[:, fc, tb * 512:(tb + 1) * 512], in0=pss[pbk][:],
                in1=cacc[:, 4 * tb:4 * tb + 4, :].rearrange("p n q -> p (n q)"), op=ALU.mult),
                reads=[("ps", pbk), ck], pwrites=["cvT"])

    cslots = {}
    for fc in range(KC):
        cslots[fc] = rloaded.pop(0)
        conv_uc(fc, cslots[fc])
        conv_taps(fc)
        if fc >= 1:
            conv_b(fc - 1, cslots[fc - 1])
            issue_ring()
    conv_b(KC - 1, cslots[KC - 1])
    issue_ring()
    P.barrier()

    wout_st = A[:, 24576:32768].rearrange("p (k n) -> p k n", k=KC)
    P.add("pool", lambda e: e.dma_start(out=wout_st, in_=wout_d.rearrange("(k p) n -> p k n", p=128)),
          writes=["wout_st"], chan="wout")
    mT = C[:].rearrange("p (k t) -> p k t", k=KC)
    tab = [E[:, i * 1024:(i + 1) * 1024].bitcast(F32) for i in range(4)]
    m12 = [E[:, 4096 + i * 1024:4096 + (i + 1) * 1024].bitcast(F32) for i in range(4)]
    gctr = [0]
    for oc in range(KC):
        s = rloaded.pop(0)
        wga = RING[s][0]; wgc = RING[s][1]
        for tb in range(4):
            i2 = gctr[0] % 2; gctr[0] += 1
            pya, pyc, pga, pgc = 4 * i2, 4 * i2 + 1, 4 * i2 + 2, 4 * i2 + 3
            tk = tb * 512
            for hh in range(4):
                P.add("pe", lambda e, hh=hh, tk=tk, pya=pya: e.matmul(pss[pya][:], lhsT=wab[:, hh, oc * 128:(oc + 1) * 128],
                                                                       rhs=oT[:, hh, tk:tk + 512], start=(hh == 0), stop=(hh == 3)),
                      reads=["wab", "oT"], writes=[("ps", pya)])
            for k in range(KC):
                P.add("pe", lambda e, k=k, tk=tk, pyc=pyc: e.matmul(pss[pyc][:], lhsT=wcb[:, k, oc * 128:(oc + 1) * 128],
                                                                     rhs=cvT[:, k, tk:tk + 512], start=(k == 0), stop=(k == KC - 1)),
                      reads=["wcb", "cvT"], writes=[("ps", pyc)])
            for k in range(KC):
                P.add("pe", lambda e, k=k, tk=tk, pga=pga: e.matmul(pss[pga][:], lhsT=wga[:, k, :], rhs=h2o[:, k, tk:tk + 512],
                                                                     start=(k == 0), stop=(k == KC - 1)),
                      reads=[("rg", s, 0), "h2o"], writes=[("ps", pga)])
            for k in range(KC):
                P.add("pe", lambda e, k=k, tk=tk, pgc=pgc: e.matmul(pss[pgc][:], lhsT=wgc[:, k, :], rhs=h2o[:, k, tk:tk + 512],
                                                                     start=(k == 0), stop=(k == KC - 1)),
                      reads=[("rg", s, 1), "h2o"], writes=[("ps", pgc)])
            ta = tab[2 * i2]; tc_ = tab[2 * i2 + 1]; m1 = m12[2 * i2]; m2 = m12[2 * i2 + 1]
            P.add("act", lambda e, ta=ta, pga=pga: e.activation(out=ta, in_=pss[pga][:], func=AF.Tanh, scale=0.5),
                  reads=[("ps", pga)], writes=[("ta", i2)])
            P.add("act", lambda e, tc_=tc_, pgc=pgc: e.activation(out=tc_, in_=pss[pgc][:], func=AF.Tanh, scale=0.5),
                  reads=[("ps", pgc)], writes=[("tc", i2)])
            P.add("dve", lambda e, ta=ta, m1=m1, pya=pya: e.scalar_tensor_tensor(out=m1, in0=ta, scalar=1.0, in1=pss[pya][:],
                                                                                 op0=ALU.add, op1=ALU.mult),
                  reads=[("ta", i2), ("ps", pya)], writes=[("m1", i2)])
            P.add("dve", lambda e, tc_=tc_, m2=m2, pyc=pyc: e.scalar_tensor_tensor(out=m2, in0=tc_, scalar=1.0, in1=pss[pyc][:],
                                                                                   op0=ALU.add, op1=ALU.mult),
                  reads=[("tc", i2), ("ps", pyc)], writes=[("m2", i2)])
            P.add("dve", lambda e, m1=m1, m2=m2, tk=tk: e.tensor_tensor(out=mT[:, oc, tk:tk + 512], in0=m1, in1=m2, op=ALU.add),
                  reads=[("m1", i2), ("m2", i2)], pwrites=["mT"])
        issue_ring()
    P.barrier()

    for tg in range(3):
        P.add("sp", lambda e, tg=tg: e.dma_start(out=xbuf[:, 4 * tg:4 * tg + 4, :], in_=x1s_d[:, 4 * tg:4 * tg + 4, :]),
              reads=[("x1s", tg)], writes=[("x", 4 * tg + i) for i in range(4)], chan=("xg", tg))
    for k in range(KC):
        P.add("dve", lambda e, k=k: e.tensor_tensor(out=wout[:, k, :], in0=wout_st[:, k, :], in1=gtrows[:, 1, :], op=ALU.mult),
              reads=["wout_st", ("gt", 1)], pwrites=["wout"])
    P.add("sp", lambda e: e.dma_start(out=xbuf[:, 12:16, :], in_=x1s_d[:, 12:16, :]),
          reads=[("x1s", 3)], writes=[("x", 12 + i) for i in range(4)] + ["wout_st"], chan=("xg", 3))
    for (j0, nj) in PIECES:
        q_piece(f2g_d, f2u_d, f2d_d, j0, nj)
    assert piece_ctr[0] % 3 == 0
    issue_load(); issue_load()
    for n in range(NT):
        for hf in range(2):
            pa = 2 * (n % 2) + hf
            for k in range(KC):
                P.add("pe", lambda e, k=k, n=n, hf=hf, pa=pa: e.matmul(pss[pa][:], lhsT=mT[:, k, n * 128:(n + 1) * 128],
                                                                        rhs=wout[:, k, hf * 512:(hf + 1) * 512],
                                                                        start=(k == 0), stop=(k == KC - 1)),
                      reads=["mT", "wout"], writes=[("ps", pa)])
            P.add("dve", lambda e, n=n, hf=hf, pa=pa: e.tensor_tensor(
                out=xbuf[:, n, hf * 512:(hf + 1) * 512], in0=pss[pa][:], in1=xbuf[:, n, hf * 512:(hf + 1) * 512], op=ALU.add),
                reads=[("ps", pa), ("x", n)], writes=[("x", n)])
    P.barrier()

    issue_load()

    def post_out(tg):
        P.add("sp", lambda e: e.dma_start(out=out_d[:, 4 * tg:4 * tg + 4, :], in_=xbuf[:, 4 * tg:4 * tg + 4, :]),
              reads=[("x", 4 * tg + i) for i in range(4)], writes=[("out", tg)], chan=("out", tg))

    ffn_pass(2, lambda tg: norm_front(tg, 64), lambda tg: norm_back(tg, 32, 40, 64), lambda tg: None, post_out, {})
    P.add("sp", None, reads=[("out", tg) for tg in range(4)])
    P.emit(nc)
    st.close()
    return nc


def _masks():
    m = np.zeros((128, 6, 128), np.float32)
    i = np.arange(128)
    mloc = [i, i, i]
    for g in range(3):
        ml = mloc[g]
        k = ml[:, None]; q = ml[None, :]
        m[:, 2 * g, :] = np.where(k >= q, 0.0, NEG)
        m[:, 2 * g + 1, :] = np.where(k <= q, 0.0, NEG)
    return m.astype(ml_dtypes.bfloat16)


def _host_inputs(inputs):
    x = np.ascontiguousarray(np.asarray(inputs["x"], dtype=np.float32))
    c = np.asarray(inputs["c"], dtype=np.float32)
    sq = lambda name: np.ascontiguousarray(np.asarray(inputs[name], dtype=np.float32)[0])
    col = lambda v: np.ascontiguousarray(v.reshape(KC, 128).T)
    shared = {
        "w_ada": sq("w_ada"), "b_ada": sq("b_ada"),
        "gcols": np.ascontiguousarray(np.concatenate([col(sq("norm_ffn1")), col(sq("norm_mix")), col(sq("norm_ffn2"))], axis=1)),
        "qkn": np.ascontiguousarray(np.stack([sq("q_norm"), sq("k_norm")], axis=1)),
        "cw": np.ascontiguousarray(sq("conv_w").reshape(3, KC, 128).transpose(2, 1, 0)),
        "bcol": np.ascontiguousarray(sq("b_ada").reshape(9 * KC, 128).T),
        "ffn1_w_gate": sq("ffn1_w_gate"), "ffn1_w_up": sq("ffn1_w_up"), "ffn1_w_down": sq("ffn1_w_down"),
        "ffn2_w_gate": sq("ffn2_w_gate"), "ffn2_w_up": sq("ffn2_w_up"), "ffn2_w_down": sq("ffn2_w_down"),
        "w_in": sq("w_in"), "w_attn_branch": sq("w_attn_branch"), "w_conv_branch": sq("w_conv_branch"),
        "w_out": sq("w_out"),
        "ident": np.eye(128).astype(ml_dtypes.bfloat16), "identf": np.eye(128, dtype=np.float32),
        "masks": _masks(),
    }
    in_maps = []
    for core in range(8):
        b, ch = core // 4, core % 4
        xo = x[b, ch * T:(ch + 1) * T].reshape(128, NT, D)
        xh = x[b, (ch - 1) * T:ch * T].reshape(128, NT, D) if ch > 0 else np.zeros((128, NT, D), np.float32)
        hm = np.zeros((128, 2), np.float32)
        hm[:, 0] = 0.0 if ch > 0 else NEG
        hm[:, 1] = 1.0 if ch > 0 else 0.0
        m = dict(shared)
        m.update({"xo": np.ascontiguousarray(xo), "xh": np.ascontiguousarray(xh), "cT": col(c[b]), "hm": hm})
        in_maps.append(m)
    return in_maps


_NC_CACHE = {}


def kernel(**inputs):
    in_maps = _host_inputs(inputs)
    if "nc" not in _NC_CACHE:
        _NC_CACHE["nc"] = build_nc()
    res = run_bass_kernel_spmd(_NC_CACHE["nc"], in_maps, core_ids=list(range(8)))
    out = np.empty((2, 4 * T, D), np.float32)
    for core in range(8):
        b, ch = core // 4, core % 4
        out[b, ch * T:(ch + 1) * T] = np.asarray(res.results[core]["out"]).reshape(T, D)
    return out
```

```python
from contextlib import ExitStack
import numpy as np
import ml_dtypes
import concourse.bass as bass
import concourse.mybir as mybir
from concourse.bass_utils import run_bass_kernel_spmd

F32 = mybir.dt.float32
BF16 = mybir.dt.bfloat16
AF = mybir.ActivationFunctionType
ALU = mybir.AluOpType
AX = mybir.AxisListType

NEG = -30000.0
EPS = 1e-6
D = 1024
KC = 8
DFF = 2816
NJ = 22
T = 2048
NT = 16
INW = 9728


class Op:
    __slots__ = ("eng", "fn", "deps", "chan", "chan_val", "needs_inc", "inc_val")

    def __init__(self, eng, fn, chan):
        self.eng = eng
        self.fn = fn
        self.chan = chan
        self.chan_val = 0
        self.deps = ()
        self.needs_inc = False
        self.inc_val = 0


class _Rec:
    def __init__(self):
        self.call = None

    def __getattr__(self, name):
        def f(*a, **kw):
            self.call = (name, a, kw)
        return f


class Prog:
    ENGS = ("pe", "act", "dve", "pool", "sp")
    BLK = {"pe": "tensor", "act": "scalar", "dve": "vector", "pool": "gpsimd", "sp": "sync"}

    def __init__(self):
        self.ops = []
        self.last_write = {}
        self.readers = {}
        self.chan_count = {}
        self.last_on_eng = {}
        self.last_on_chan = {}
        self.par_epoch = {}
        self.epoch_base = {}
        self.new_epoch = set()

    def add(self, eng, fn, reads=(), writes=(), chan=None, extra_deps=(), pwrites=()):
        if fn is not None:
            rec = _Rec()
            fn(rec)
            assert rec.call is not None
            fn = rec.call
        op = Op(eng, fn, chan)
        deps = set(extra_deps)
        for r in reads:
            deps.update(self.last_write.get(r, ()))
            if isinstance(r, tuple) and r[0] == "ps":
                deps.update(o for o in self.readers.get(r, ()) if o.eng != eng)
        for w in writes:
            deps.update(self.last_write.get(w, ()))
            deps.update(self.readers.get(w, ()))
        for w in pwrites:
            rd = self.readers.get(w, ())
            if rd or not self.par_epoch.get(w, False):
                base = set(rd) | set(self.last_write.get(w, ()))
                self.epoch_base[w] = base
                self.new_epoch.add(w)
            deps.update(self.epoch_base.get(w, ()))
        if eng == "pe":
            deps = {d for d in deps if not (d.eng == "pe" and d.chan is None)}
        deps.discard(op)
        for r in reads:
            self.readers.setdefault(r, []).append(op)
        for w in writes:
            self.last_write[w] = [op]
            self.readers[w] = []
            self.par_epoch[w] = False
        for w in pwrites:
            if w in self.new_epoch:
                self.new_epoch.discard(w)
                self.last_write[w] = [op]
                self.readers[w] = []
                self.par_epoch[w] = True
            else:
                self.last_write[w].append(op)
        if chan is not None:
            n = self.chan_count.get(chan, 0) + 1
            self.chan_count[chan] = n
            op.chan_val = 16 * n
            self.last_on_chan[chan] = op
        elif fn is not None:
            self.last_on_eng[eng] = op
        for d in deps:
            if d.chan is None:
                d.needs_inc = True
        op.deps = tuple(deps)
        self.ops.append(op)
        return op

    def barrier(self):
        deps = list(self.last_on_eng.values()) + list(self.last_on_chan.values())
        for e in self.ENGS:
            self.add(e, None, extra_deps=deps)

    def emit(self, nc):
        cnt = {e: 0 for e in self.ENGS}
        for op in self.ops:
            if op.chan is None and op.needs_inc:
                cnt[op.eng] += 1
                op.inc_val = cnt[op.eng]
        with ExitStack() as st:
            sem_eng = {e: st.enter_context(nc.semaphore("s_" + e)) for e in self.ENGS}
            sem_chan = {c: st.enter_context(nc.semaphore("c_%d" % i))
                        for i, c in enumerate(self.chan_count)}
            block = st.enter_context(nc.Block())
            for e in self.ENGS:
                ops_e = [op for op in self.ops if op.eng == e]
                if not ops_e:
                    continue

                def body(engine, ops_e=ops_e, e=e):
                    waited = {}
                    for op in ops_e:
                        need = {}
                        for d in op.deps:
                            if d.chan is not None:
                                key = ("c", d.chan)
                                s, v = sem_chan[d.chan], d.chan_val
                            else:
                                key = ("e", d.eng)
                                s, v = sem_eng[d.eng], d.inc_val
                            if need.get(key, (None, 0))[1] < v:
                                need[key] = (s, v)
                        for key, (s, v) in need.items():
                            if waited.get(key, 0) < v:
                                engine.wait_ge(s, v)
                                waited[key] = v
                        if op.fn is None:
                            continue
                        name, a, kw = op.fn
                        ins = getattr(engine, name)(*a, **kw)
                        if op.chan is not None:
                            ins.then_inc(sem_chan[op.chan], 16)
                        elif op.needs_inc:
                            ins.then_inc(sem_eng[e], 1)

                getattr(block, self.BLK[e])(body)


def build_nc(debug_stop=None):
    nc = bass.Bass("TRN2", target_bir_lowering=False)

    def din(name, shape, dt=F32):
        return nc.dram_tensor(name, list(shape), dt, kind="ExternalInput").ap()

    xo_d = din("xo", [128, NT, D])
    xh_d = din("xh", [128, NT, D])
    cT_d = din("cT", [128, KC])
    hm_d = din("hm", [128, 2])
    wada_d = din("w_ada", [D, 9 * D])
    bada_d = din("b_ada", [9 * D])
    gcol_d = din("gcols", [128, 3 * KC])
    qk_d = din("qkn", [128, 2])
    cw_d = din("cw", [128, KC, 3])
    bcol_d = din("bcol", [128, 9 * KC])
    f1g_d = din("ffn1_w_gate", [D, DFF]); f1u_d = din("ffn1_w_up", [D, DFF]); f1d_d = din("ffn1_w_down", [DFF, D])
    f2g_d = din("ffn2_w_gate", [D, DFF]); f2u_d = din("ffn2_w_up", [D, DFF]); f2d_d = din("ffn2_w_down", [DFF, D])
    win_d = din("w_in", [D, INW])
    wab_d = din("w_attn_branch", [512, D])
    wcb_d = din("w_conv_branch", [D, D])
    wout_d = din("w_out", [D, D])
    ident_d = din("ident", [128, 128], BF16)
    identf_d = din("identf", [128, 128], F32)
    masks_d = din("masks", [128, 6, 128], BF16)
    out_d = nc.dram_tensor("out", [128, NT, D], F32, kind="ExternalOutput").ap()
    x1s_d = nc.dram_tensor("x1s", [128, NT, D], F32).ap()
    h2s_d = nc.dram_tensor("h2s", [128, KC, T], BF16).ap()

    P = Prog()
    st = ExitStack()
    sb = lambda name, shape, dt: st.enter_context(nc.sbuf_tensor(name, list(shape), dt))
    A = sb("A", [128, 32768], BF16)
    B = sb("B", [128, 16384], BF16)
    C = sb("C", [128, 16384], BF16)
    Dw = sb("Dw", [128, 24576], BF16)
    E = sb("E", [128, 8192], BF16)
    ident = sb("ident_s", [128, 128], BF16)
    identf = sb("identf_s", [128, 128], F32)
    masks = sb("masks_s", [128, 8, 128], BF16)
    masksh = sb("masksh_s", [128, 3, 128], BF16)
    ones_bf = sb("ones_s", [128, 128], BF16)
    gtrows = sb("gtrows", [128, 3, D], BF16)
    cols = sb("cols", [128, 64], F32)
    gcols = sb("gcols_s", [128, 3 * KC], F32)
    qk = sb("qk_s", [128, 4], F32)
    cw = sb("cw_s", [128, KC, 3], F32)
    hm = sb("hm_s", [128, 2], F32)
    cact = sb("cact", [128, KC], F32)
    cactb = sb("cactb", [128, KC], BF16)
    bcol = sb("bcol_s", [128, 9 * KC], F32)
    crep = sb("crep", [128, KC, 128], BF16)
    ssb = sb("ssb", [128, 6 * NT], F32)
    rstd = sb("rstd", [128, 6 * NT], F32)
    mhalf = sb("mhalf", [128, NT], F32)
    pss = [st.enter_context(nc.psum_tensor("ps%d" % i, [128, 512], F32)) for i in range(8)]

    def psb(i):
        return pss[i][:].bitcast(BF16)

    xbuf = A[:].bitcast(F32).rearrange("p (n d) -> p n d", n=NT)
    hT = B[:].rearrange("p (k t) -> p k t", k=KC)

    cload = []
    def cl(dst, src, key):
        cload.append(key)
        P.add("sp", lambda e: e.dma_start(out=dst, in_=src), writes=[key], chan="const")
    cl(ident[:], ident_d, "ident"); cl(identf[:], identf_d, "identf")
    cl(masks[:, 0:6, :], masks_d, "masks"); cl(gcols[:], gcol_d, "gcols")
    cl(qk[:, 0:2], qk_d, "qk"); cl(cw[:], cw_d, "cw"); cl(hm[:], hm_d, "hm"); cl(cact[:], cT_d, "cact"); cl(bcol[:], bcol_d, "bcol")
    P.add("dve", lambda e: e.memset(ones_bf[:], 1.0), reads=cload, writes=cload + ["ones"])
    P.add("dve", lambda e: e.memset(mhalf[:], -0.5), writes=["mhalf"])
    for g in range(3):
        P.add("dve", lambda e, g=g: e.tensor_scalar(out=masksh[:, g, :], in0=masks[:, 2 * g, :],
                                                    scalar1=hm[:, 0:1], scalar2=None, op0=ALU.add),
              reads=["masks", "hm"], writes=["masksh"])
    P.add("dve", lambda e: e.tensor_scalar(out=qk[:, 2:3], in0=qk[:, 1:2], scalar1=float(128 ** -0.5),
                                           scalar2=None, op0=ALU.mult), reads=["qk"], writes=["qk"])
    P.add("act", lambda e: e.activation(out=cact[:], in_=cact[:], func=AF.Silu), reads=["cact"], writes=["cact"])
    P.add("dve", lambda e: e.tensor_copy(out=cactb[:], in_=cact[:]), reads=["cact"], writes=["cactb"])
    for k in range(KC):
        P.add("dve", lambda e, k=k: e.tensor_scalar(out=crep[:, k, :], in0=ones_bf[:], scalar1=cact[:, k:k + 1],
                                                    scalar2=None, op0=ALU.mult),
              reads=["cact", "ones"], writes=["crep"])

    def slot_views(s):
        base = Dw[:, s * 12288:(s + 1) * 12288] if s < 2 else C[:, 0:12288]
        wg = base[:, 0:4096].rearrange("p (k n) -> p k n", k=KC)
        wu = base[:, 4096:8192].rearrange("p (k n) -> p k n", k=KC)
        wd = base[:, 8192:12288].rearrange("p (j n) -> p j n", j=4)
        wa = base[:, 0:8192].rearrange("p (k n) -> p k n", k=KC)
        return wg, wu, wd, wa
    SLOTS = [slot_views(s) for s in range(3)]

    hid = [E[:, i * 2048:(i + 1) * 2048].rearrange("p (j t) -> p j t", j=4) for i in range(2)]
    sgb = [E[:, 4096 + i * 1024:4096 + (i + 1) * 1024].bitcast(F32) for i in range(2)]
    xnb = [E[:, 6144:7168], E[:, 7168:8192], C[:, 13312:14336], C[:, 14336:15360]]
    junk = C[:, 12288:13312]
    brow = C[0:1, 15360:16384]
    PIECES = [(0, 4), (4, 4), (8, 2), (10, 4), (14, 4), (18, 4)]

    pending_loads = []
    piece_ctr = [0]

    def q_piece(wg_d, wu_d, wd_d, j0, nj):
        def mk():
            idx = piece_ctr[0]; piece_ctr[0] += 1
            s = idx % 3
            wg, wu, wd, _ = SLOTS[s]
            nc_ = nj * 128
            P.add("pool", lambda e: e.dma_start(out=wg[:, :, 0:nc_],
                  in_=wg_d[:, j0 * 128:j0 * 128 + nc_].rearrange("(k p) n -> p k n", p=128)),
                  writes=[("wg", s)], chan=("wg", s))
            P.add("pool", lambda e: e.dma_start(out=wu[:, :, 0:nc_],
                  in_=wu_d[:, j0 * 128:j0 * 128 + nc_].rearrange("(k p) n -> p k n", p=128)),
                  writes=[("wu", s)], chan=("wu", s))
            P.add("pool", lambda e: e.dma_start(out=wd[:, 0:nj, :],
                  in_=wd_d[j0 * 128:j0 * 128 + nc_, :].rearrange("(j p) n -> p j n", p=128)),
                  writes=[("wd", s)], chan=("wd", s))
            return s
        pending_loads.append(mk)

    def q_mod(j):
        def mk():
            idx = piece_ctr[0]; piece_ctr[0] += 1
            s = idx % 3
            wa = SLOTS[s][3]
            P.add("pool", lambda e: e.dma_start(
                out=wa, in_=wada_d[:, j * D:(j + 1) * D].rearrange("(k p) n -> p k n", p=128)),
                writes=[("wg", s), ("wu", s)], chan=("wg", s))
            if j % 3 == 2:
                P.add("pool", lambda e: e.dma_start(out=brow, in_=bada_d[j * D:(j + 1) * D].rearrange("(o n) -> o n", o=1)),
                      writes=["brow"], chan="brow")
            return s
        pending_loads.append(mk)

    loaded = []

    def issue_load():
        if pending_loads:
            loaded.append(pending_loads.pop(0)())

    def mod_block(j, issue=True):
        s = loaded.pop(0)
        wa = SLOTS[s][3]
        role, li = j % 3, j // 3
        if role == 2:
            for hf in range(2):
                pb = 6 + hf
                for k in range(KC):
                    P.add("pe", lambda e, k=k: e.matmul(pss[pb][:], lhsT=crep[:, k, :], rhs=wa[:, k, hf * 512:(hf + 1) * 512],
                                                        start=(k == 0), stop=False),
                          reads=["crep", ("wg", s), ("wu", s)], writes=[("ps", pb)])
                P.add("pe", lambda e: e.matmul(pss[pb][:], lhsT=ones_bf[0:1, :], rhs=brow[:, hf * 512:(hf + 1) * 512],
                                               start=False, stop=True),
                      reads=["ones", "brow"], writes=[("ps", pb)])
                P.add("dve", lambda e: e.tensor_scalar(out=gtrows[:, li, hf * 512:(hf + 1) * 512], in0=pss[pb][:], scalar1=0.5,
                                                       scalar2=None, op0=ALU.mult),
                      reads=[("ps", pb)], writes=[("gt", li)])
        else:
            pb = 7
            for kc in range(KC):
                for k in range(KC):
                    P.add("pe", lambda e, k=k, kc=kc: e.matmul(pss[pb][:, kc:kc + 1], lhsT=wa[:, k, kc * 128:(kc + 1) * 128],
                                                               rhs=cactb[:, k:k + 1], start=(k == 0), stop=(k == KC - 1)),
                          reads=["cactb", ("wg", s), ("wu", s)], writes=[("ps", pb)])
            if role == 0:
                P.add("dve", lambda e: e.tensor_tensor(out=cols[:, 16 * li + 8:16 * li + 16], in0=pss[pb][:, 0:KC],
                                                       in1=bcol[:, j * KC:(j + 1) * KC], op=ALU.add),
                      reads=[("ps", pb), "bcol"], writes=["cols"])
            else:
                P.add("dve", lambda e: e.tensor_tensor(out=cols[:, 48:56], in0=pss[pb][:, 0:KC],
                                                       in1=bcol[:, j * KC:(j + 1) * KC], op=ALU.add),
                      reads=[("ps", pb), "bcol"], writes=["cols"])
                P.add("dve", lambda e: e.scalar_tensor_tensor(
                    out=cols[:, 16 * li:16 * li + 8], in0=cols[:, 48:56], scalar=1.0, in1=gcols[:, li * KC:(li + 1) * KC],
                    op0=ALU.add, op1=ALU.mult), reads=["cols", "gcols"], writes=["cols"])
        if issue:
            issue_load()

    def load_x(src_d, tg, after=()):
        P.add("sp", lambda e: e.dma_start(out=xbuf[:, 4 * tg:4 * tg + 4, :], in_=src_d[:, 4 * tg:4 * tg + 4, :]),
              reads=list(after), writes=[("x", 4 * tg + i) for i in range(4)], chan=("xg", tg))

    tctr = [0]

    ng_state = {}

    def norm_front(tg, off=0):
        c0 = off + 4 * tg
        for i in range(4):
            n = 4 * tg + i
            P.add("act", lambda e, n=n, i=i: e.activation(out=junk, in_=xbuf[:, n, :], func=AF.Square,
                                                          accum_out=ssb[:, c0 + i:c0 + i + 1]),
                  reads=[("x", n), "ssb"], writes=[("ss", off + n), "junk"])
        P.add("dve", lambda e: e.tensor_scalar(out=rstd[:, c0:c0 + 4], in0=ssb[:, c0:c0 + 4],
                                               scalar1=1.0 / D, scalar2=EPS, op0=ALU.mult, op1=ALU.add),
              reads=["ssb"] + [("ss", off + 4 * tg + i) for i in range(4)], writes=[("rs", off + tg)])
        P.add("pool", lambda e: e.tensor_tensor(out=rstd[:, c0:c0 + 4], in0=rstd[:, c0:c0 + 4],
                                                in1=mhalf[:, 0:4], op=ALU.pow),
              reads=[("rs", off + tg), "mhalf"], writes=[("rs", off + tg)])
        tiles = []
        for i in range(4):
            n = 4 * tg + i
            xi = tctr[0] % 4
            pb = 6 + (tctr[0] % 2); tctr[0] += 1
            xb = xnb[xi]
            tiles.append((n, i, xi, pb, xb))
            P.add("pool", lambda e, n=n, i=i, xb=xb: e.tensor_scalar(out=xb, in0=xbuf[:, n, :], scalar1=rstd[:, c0 + i:c0 + i + 1],
                                                                     scalar2=0.0, op0=ALU.mult, op1=ALU.add),
                  reads=[("x", n), ("rs", off + tg)], writes=[("xn", xi)])
        ng_state[(off, tg)] = tiles

    def norm_back(tg, acol, scol, off=0):
        tiles = ng_state.pop((off, tg))

        def transp(t):
            n, i, xi, pb, xb = t
            for c in range(KC):
                P.add("pe", lambda e, c=c: e.transpose(
                    out=psb(pb)[:, c * 128:(c + 1) * 128], in_=xb[:, c * 128:(c + 1) * 128], identity=ident[:]),
                    reads=[("xn", xi), "ident"], writes=[("ps", pb)])

        def evac(t):
            n, i, xi, pb, xb = t
            for c in range(KC):
                if pb == 6:
                    P.add("dve", lambda e, c=c: e.tensor_scalar(
                        out=hT[:, c, n * 128:(n + 1) * 128], in0=psb(pb)[:, c * 128:(c + 1) * 128],
                        scalar1=cols[:, acol + c:acol + c + 1], scalar2=cols[:, scol + c:scol + c + 1],
                        op0=ALU.mult, op1=ALU.add), reads=[("ps", pb), "cols"], pwrites=[("hT", n)])
                else:
                    P.add("act", lambda e, c=c: e.activation(
                        out=hT[:, c, n * 128:(n + 1) * 128], in_=psb(pb)[:, c * 128:(c + 1) * 128],
                        func=AF.Identity, scale=cols[:, acol + c:acol + c + 1], bias=cols[:, scol + c:scol + c + 1]),
                        reads=[("ps", pb), "cols"], pwrites=[("hT", n)])

        transp(tiles[0]); transp(tiles[1])
        evac(tiles[0]); transp(tiles[2])
        evac(tiles[1]); transp(tiles[3])
        evac(tiles[2]); evac(tiles[3])

    mctr = [0]
    actr = [0]

    def unit_gu(s, nj, tg, hb, js):
        wg, wu, wd, _ = SLOTS[s]
        hd = hid[hb]
        for j in js:
            pg = j % 2
            for k in range(KC):
                P.add("pe", lambda e, j=j, k=k: e.matmul(
                    pss[pg][:], lhsT=wg[:, k, j * 128:(j + 1) * 128], rhs=hT[:, k, tg * 512:(tg + 1) * 512],
                    start=(k == 0), stop=(k == KC - 1)),
                    reads=[("wg", s)] + [("hT", 4 * tg + i) for i in range(4)], writes=[("ps", pg)])
            for k in range(KC):
                P.add("pe", lambda e, j=j, k=k: e.matmul(
                    pss[2 + pg][:], lhsT=wu[:, k, j * 128:(j + 1) * 128], rhs=hT[:, k, tg * 512:(tg + 1) * 512],
                    start=(k == 0), stop=(k == KC - 1)),
                    reads=[("wu", s)] + [("hT", 4 * tg + i) for i in range(4)], writes=[("ps", 2 + pg)])
            P.add("act", lambda e: e.activation(out=sgb[pg], in_=pss[pg][:], func=AF.Silu),
                  reads=[("ps", pg)], writes=[("sg", pg)])
            P.add("dve", lambda e, j=j: e.tensor_tensor(out=hd[:, j, :], in0=pss[2 + pg][:], in1=sgb[pg], op=ALU.mult),
                  reads=[("ps", 2 + pg), ("sg", pg)], pwrites=[("hid", hb)])

    def unit_down(s, nj, tg, hb):
        wg, wu, wd, _ = SLOTS[s]
        hd = hid[hb]
        for t in range(4):
            n = 4 * tg + t
            for hf in range(2):
                pa = 4 + (actr[0] % 4); actr[0] += 1
                for j in range(nj):
                    P.add("pe", lambda e, j=j: e.matmul(
                        pss[pa][:], lhsT=hd[:, j, t * 128:(t + 1) * 128], rhs=wd[:, j, hf * 512:(hf + 1) * 512],
                        start=(j == 0), stop=(j == nj - 1)),
                        reads=[("hid", hb), ("wd", s)], writes=[("ps", pa)])
                P.add("dve", lambda e: e.tensor_tensor(
                    out=xbuf[:, n, hf * 512:(hf + 1) * 512], in0=pss[pa][:],
                    in1=xbuf[:, n, hf * 512:(hf + 1) * 512], op=ALU.add),
                    reads=[("ps", pa), ("x", n)], writes=[("x", n)])

    def new_hb():
        hb = mctr[0] % 2; mctr[0] += 1
        return hb

    def ffn_unit(s, nj, tg, mid=None):
        hb = new_hb()
        unit_gu(s, nj, tg, hb, range(nj))
        if mid is not None:
            mid()
        unit_down(s, nj, tg, hb)

    def ffn_pass(gi, pre_f, pre_b, post_f, post_b, extras, first_mid=None, nxt=None, head=None):
        if head is None:
            pre_f(0); pre_b(0); pre_f(1)
        deferred = None
        for q, (j0, nj) in enumerate(PIECES):
            s = loaded.pop(0)
            wd = SLOTS[s][2]

            def scale_wd(s=s, wd=wd, nj=nj):
                for j in range(nj):
                    P.add("dve", lambda e, j=j: e.tensor_tensor(out=wd[:, j, :], in0=wd[:, j, :], in1=gtrows[:, gi, :], op=ALU.mult),
                          reads=[("wd", s), ("gt", gi)], writes=[("wd", s)])
            mid0 = None
            if q == 0 and first_mid is not None:
                def mid0(scale_wd=scale_wd):
                    first_mid()
                    scale_wd()
            else:
                scale_wd()
            last = (q == len(PIECES) - 1)
            hbs = [None] * 4
            for tg in range(4):
                if q == 0:
                    ffn_unit(s, nj, tg, mid=(mid0 if tg == 0 else None))
                else:
                    if tg == 0:
                        hbs[0] = new_hb()
                        unit_gu(s, nj, 0, hbs[0], range(nj))
                    if tg < 3:
                        hbs[tg + 1] = new_hb()
                        unit_gu(s, nj, tg + 1, hbs[tg + 1], [0])
                    unit_down(s, nj, tg, hbs[tg])
                    if tg < 3:
                        unit_gu(s, nj, tg + 1, hbs[tg + 1], range(1, nj))
                if q == 0:
                    if tg == 0 and head is not None:
                        head()
                    if tg + 1 < 4:
                        pre_b(tg + 1)
                    if tg + 2 < 4:
                        pre_f(tg + 2)
                if last:
                    if tg >= 1:
                        post_b(tg - 1)
                    post_f(tg)
                    if nxt is not None and tg == 2:
                        nxt[0](0)
                    if nxt is not None and tg == 3:
                        nxt[1](0)
                        nxt[0](1)
            issue_load()
            if q == 0 and first_mid is not None:
                issue_load()
            for fn in extras.get(q, ()):
                fn()
        if nxt is not None:
            return lambda: post_b(3)
        post_b(3)
        return None

    def ring(s):
        base = Dw[:, s * 3072:(s + 1) * 3072]
        return [base[:, i * 1024:(i + 1) * 1024].rearrange("p (k n) -> p k n", k=KC) for i in range(3)]
    RING = [ring(s) for s in range(3)]
    wab = Dw[:, 9216:13312].rearrange("p (h n) -> p h n", h=4)
    wcb = Dw[:, 13312:21504].rearrange("p (k n) -> p k n", k=KC)
    wout = E[:, 0:8192].rearrange("p (k n) -> p k n", k=KC)

    win_v = win_d.rearrange("(k p) n -> p k n", p=128)
    rctr = [0]
    ring_extra = []
    rloads = []

    def queue_ring(colsets):
        def mk():
            s = rctr[0] % 3; rctr[0] += 1
            for i, c0 in enumerate(colsets):
                P.add("pool", lambda e, i=i, c0=c0, s=s: e.dma_start(out=RING[s][i], in_=win_v[:, :, c0:c0 + 128]),
                      writes=[("rg", s, i)] + ring_extra, chan=("rg", s, i))
            return s
        rloads.append(mk)
    rloaded = []

    def issue_ring():
        if rloads:
            rloaded.append(rloads.pop(0)())

    order = [(g, h) for h in range(4) for g in range(3)]
    for (g, h) in order:
        idx = g * 4 + h
        queue_ring([idx * 128, 1536 + idx * 128, 3072 + idx * 128])
    for fc in range(KC):
        queue_ring([4608 + fc * 128, 6656 + fc * 128, 5632 + fc * 128])
    for oc in range(KC):
        queue_ring([7680 + oc * 128, 8704 + oc * 128])

    F1 = (f1g_d, f1u_d, f1d_d)
    for j in (0, 1):
        q_mod(j)
    for q, (j0, nj) in enumerate(PIECES):
        q_piece(*F1, j0, nj)
        if q <= 2:
            q_mod(2 + q)
    for q, (j0, nj) in enumerate(PIECES):
        q_piece(*F1, j0, nj)
        if q < 4:
            q_mod(5 + q)

    load_x(xh_d, 0)
    for _ in range(3):
        issue_load()
    for tg in range(1, 4):
        load_x(xh_d, tg, after=[("wg", 0)])
    P.add("dve", lambda e: e.memset(ssb[:], 0.0), writes=["ssb"])
    for j in (0, 1):
        mod_block(j)

    def post_halo_b(tg):
        norm_back(tg, 16, 24, 16)
        P.add("sp", lambda e: e.dma_start(out=h2s_d[:, :, tg * 512:(tg + 1) * 512], in_=hT[:, :, tg * 512:(tg + 1) * 512]),
              reads=[("hT", 4 * tg + i) for i in range(4)], writes=[("h2s", tg)], chan=("h2s", tg))
        load_x(xo_d, tg)

    own_pre = (lambda tg: norm_front(tg, 32), lambda tg: norm_back(tg, 0, 8, 32))
    tail = ffn_pass(0, lambda tg: norm_front(tg, 0), lambda tg: norm_back(tg, 0, 8, 0),
                    lambda tg: norm_front(tg, 16), post_halo_b,
                    {1: [lambda: mod_block(3)], 2: [lambda: mod_block(4)]},
                    first_mid=lambda: mod_block(2, issue=False))

    def early_ring():
        ring_extra.extend([("wg", 0), ("wu", 0), ("wd", 0)])
        for _ in range(3):
            issue_ring()
        del ring_extra[:]

    def post_own_f(tg):
        P.add("sp", lambda e: e.dma_start(out=x1s_d[:, 4 * tg:4 * tg + 4, :], in_=xbuf[:, 4 * tg:4 * tg + 4, :]),
              reads=[("x", 4 * tg + i) for i in range(4)], writes=[("x1s", tg)], chan=("x1s", tg))
        norm_front(tg, 48)

    ffn_pass(0, own_pre[0], own_pre[1],
             post_own_f, lambda tg: norm_back(tg, 16, 24, 48),
             {0: [lambda: mod_block(5)], 1: [lambda: mod_block(6)], 2: [lambda: mod_block(7)], 3: [lambda: mod_block(8), early_ring]},
             head=None)
    P.barrier()

    h2o = hT
    h2h = C[:].rearrange("p (k t) -> p k t", k=KC)
    for tg in range(4):
        P.add("sp", lambda e, tg=tg: e.dma_start(out=h2h[:, :, tg * 512:(tg + 1) * 512], in_=h2s_d[:, :, tg * 512:(tg + 1) * 512]),
              reads=[("h2s", tg)], pwrites=["h2h"], chan=("h2h", tg))
    oT = A[:, 0:8192].rearrange("p (h t) -> p h t", h=4)
    QTg = A[:, 8192:10240].rearrange("p (b i) -> p b i", b=16)
    KTg = A[:, 10240:14336].rearrange("p (b i) -> p b i", b=32)
    VTg = A[:, 14336:18432].rearrange("p (b i) -> p b i", b=32)
    Vb = A[:, 18432:22528].rearrange("p (b i) -> p b i", b=32)
    num = A[:, 22528:26624].bitcast(F32)
    den = A[:, 26624:30720].bitcast(F32)
    PT = [A[:, 30720 + i * 512:30720 + (i + 1) * 512] for i in range(2)]
    PT4 = PT + [E[:, 5120:5632], E[:, 5632:6144]]
    sqb = [A[:, 31744 + i * 512:31744 + (i + 1) * 512] for i in range(2)]
    rrb = [E[:, i * 1024:(i + 1) * 1024].bitcast(F32) for i in range(2)]

    P.add("pool", lambda e: e.dma_start(out=wab, in_=wab_d.rearrange("(h p) n -> p h n", p=128)),
          writes=["wab"], chan="wab")

    def dst_src(buf, base_blk, g, tb):
        flat = buf.rearrange("p b i -> p (b i)")
        o = base_blk * 128
        if g == 2:
            return flat[:, o + tb * 512:o + (tb + 1) * 512], (lambda ap: ap)
        if g == 1:
            dst = flat[:, o:o + 2048].rearrange("p (r q a) -> p r q a", r=4, a=4)[:, :, :, tb]
            return dst, (lambda ap: ap.rearrange("p (r q) -> p r q", r=4))
        dst = flat[:, o:o + 2048].rearrange("p (q a) -> p a q", a=16)[:, 4 * tb:4 * tb + 4, :]
        return dst, (lambda ap: ap.rearrange("p (a q) -> p a q", a=4))

    pctr = [0]
    qctr = [0]
    tctr2 = [0]
    ptmp = [E[:, 2048 + i * 512:2048 + (i + 1) * 512] for i in range(6)]
    pending_fin = []

    def proj_block(w, rhs_fn, ncols, kind, gcol, dst, shape_fn, contig, src="h2o"):
        pb = pctr[0] % 3; pctr[0] += 1
        for k in range(KC):
            P.add("pe", lambda e, k=k: e.matmul(pss[pb][:, 0:ncols], lhsT=w[0][:, k, :], rhs=rhs_fn(k),
                                                start=(k == 0), stop=(k == KC - 1)),
                  reads=[w[1], src], writes=[("ps", pb)])
        key = {"q": "QTg", "k": "KTg", "v": "VTg"}[kind]

        def store(src_fn, reads, eng_kind):
            if contig:
                out_ap, wkey, pw = dst, None, [key]
            else:
                ti = tctr2[0] % 6; tctr2[0] += 1
                out_ap, wkey, pw = ptmp[ti][:, 0:ncols], ("ptmp", ti), []
            src_fn(out_ap if contig else out_ap, reads, [wkey] if wkey else [], pw)
            if not contig:
                P.add("pool", lambda e: e.tensor_copy(out=dst, in_=shape_fn(ptmp[ti][:, 0:ncols])),
                      reads=[("ptmp", ti)], pwrites=[key])

        if kind == "v":
            def src(out_ap, reads, wr, pw):
                P.add("dve", lambda e: e.tensor_copy(out=out_ap, in_=pss[pb][:, 0:ncols]),
                      reads=[("ps", pb)], writes=wr, pwrites=pw)
            store(src, None, None)
            while pending_fin:
                pending_fin.pop(0)()
            return
        i2 = qctr[0] % 2; qctr[0] += 1
        sq = sqb[i2]; rr = rrb[i2]
        P.add("act", lambda e: e.activation(out=sq[:, 0:ncols], in_=pss[pb][:, 0:ncols], func=AF.Square),
              reads=[("ps", pb)], writes=[("sq", i2)])

        def fin():
            P.add("pe", lambda e: e.matmul(pss[3 + i2][:, 0:ncols], lhsT=ones_bf[:], rhs=sq[:, 0:ncols], start=True, stop=True),
                  reads=[("sq", i2), "ones"], writes=[("ps", 3 + i2)])
            P.add("act", lambda e: e.activation(out=rr[:, 0:ncols], in_=pss[3 + i2][:, 0:ncols], func=AF.Ln,
                                                scale=(1.0 if kind == "q" else 1.0 / 128),
                                                bias=(cols[:, 57:58] if kind == "q" else cols[:, 56:57])),
                  reads=[("ps", 3 + i2), "cols"], writes=[("rr", i2)])
            P.add("act", lambda e: e.activation(out=rr[:, 0:ncols], in_=rr[:, 0:ncols], func=AF.Exp, scale=-0.5),
                  reads=[("rr", i2)], writes=[("rr", i2)])

            def src(out_ap, reads, wr, pw):
                P.add("dve", lambda e: e.scalar_tensor_tensor(out=out_ap, in0=pss[pb][:, 0:ncols], scalar=gcol,
                                                              in1=rr[:, 0:ncols], op0=ALU.mult, op1=ALU.mult),
                      reads=[("ps", pb), ("rr", i2), "qk"], writes=wr, pwrites=pw)
            store(src, None, None)
        prev = list(pending_fin)
        del pending_fin[:]
        pending_fin.append(fin)
        for f in prev:
            f()

    def flush_fin():
        while pending_fin:
            pending_fin.pop(0)()

    P.add("dve", lambda e: e.memset(cols[:, 56:57], EPS), writes=["cols"])
    P.add("dve", lambda e: e.memset(cols[:, 57:58], 128 * EPS), writes=["cols"])

    h2h4 = h2h.rearrange("p k (n q) -> p k n q", n=16)
    sctr = [0]
    for (g, h) in order:
        s = rloaded.pop(0)
        wq = (RING[s][0], ("rg", s, 0)); wk = (RING[s][1], ("rg", s, 1)); wv = (RING[s][2], ("rg", s, 2))
        for tb in range(4):
            rf = lambda k, tb=tb: h2o[:, k, tb * 512:(tb + 1) * 512]
            d_, sf = dst_src(VTg, 16, g, tb)
            proj_block(wv, rf, 512, "v", None, d_, sf, g == 2)
            d_, sf = dst_src(QTg, 0, g, tb)
            proj_block(wq, rf, 512, "q", qk[:, 0:1], d_, sf, g == 2)
            d_, sf = dst_src(KTg, 16, g, tb)
            proj_block(wk, rf, 512, "k", qk[:, 1:2], d_, sf, g == 2)
        if g == 2:
            for tb in range(4):
                rf = lambda k, tb=tb: h2h[:, k, tb * 512:(tb + 1) * 512]
                d_, sf = dst_src(VTg, 0, 2, tb)
                proj_block(wv, rf, 512, "v", None, d_, sf, True, src="h2h")
            for tb in range(4):
                rf = lambda k, tb=tb: h2h[:, k, tb * 512:(tb + 1) * 512]
                d_, sf = dst_src(KTg, 0, 2, tb)
                proj_block(wk, rf, 512, "k", qk[:, 1:2], d_, sf, True, src="h2h")
            nhalo = 16
        elif g == 1:
            rf = lambda k: h2h4[:, k, :, 96:128]
            sf = lambda ap: ap.rearrange("p (a rb) -> p a rb", a=4)
            dv = lambda buf: buf.rearrange("p b i -> p (b i)")[:, 0:512].rearrange("p (rb a) -> p a rb", a=4)
            proj_block(wv, rf, 512, "v", None, dv(VTg), sf, False, src="h2h")
            proj_block(wk, rf, 512, "k", qk[:, 1:2], dv(KTg), sf, False, src="h2h")
            nhalo = 4
        else:
            rf = lambda k: h2h4[:, k, :, 120:128]
            sf = lambda ap: ap.rearrange("p (a b) -> p a b", a=16)
            dv = lambda buf: buf[:, 0, :].rearrange("p (b a) -> p a b", a=16)
            proj_block(wv, rf, 128, "v", None, dv(VTg), sf, False, src="h2h")
            proj_block(wk, rf, 128, "k", qk[:, 1:2], dv(KTg), sf, False, src="h2h")
            nhalo = 1
        flush_fin()
        issue_ring()
        grps = [list(range(i0, min(i0 + 8, nhalo))) for i0 in range(0, nhalo, 8)] + [list(range(16, 24)), list(range(24, 32))]
        for grp in grps:
            pb = 6 + (sctr[0] % 2); sctr[0] += 1
            for ii, blk in enumerate(grp):
                P.add("pe", lambda e, ii=ii, blk=blk, pb=pb: e.transpose(
                    out=psb(pb)[:, ii * 128:(ii + 1) * 128], in_=VTg[:, blk, :], identity=ident[:]),
                    reads=["VTg", "ident"], writes=[("ps", pb)])
            b0 = grp[0]; nb = len(grp)
            P.add("dve", lambda e, b0=b0, nb=nb, pb=pb: e.tensor_copy(
                out=Vb[:, b0:b0 + nb, :], in_=psb(pb)[:, 0:nb * 128].rearrange("p (b i) -> p b i", b=nb)),
                reads=[("ps", pb)], pwrites=["Vb"])
        def batch_info(qb4):
            qblks = [4 * qb4 + i for i in range(4)]
            info = []
            for qb in qblks:
                cur = 16 + qb
                if g == 2:
                    prev, halo = qb, True
                elif g == 1:
                    r4, n1 = qb // 4, qb % 4
                    prev, halo = (r4, True) if n1 == 0 else (16 + qb - 1, False)
                else:
                    prev, halo = (0, True) if qb == 0 else (16 + qb - 1, False)
                info.append((prev, halo, cur))
            return qblks, info

        def scores(qb4):
            qblks, info = batch_info(qb4)
            par = qb4 % 2
            for kbi in range(2):
                ps_s = 4 + 2 * par + kbi
                pt = PT4[2 * par + kbi]
                for bi, qb in enumerate(qblks):
                    prev, halo, cur = info[bi]
                    kb = prev if kbi == 0 else cur
                    if kbi == 0:
                        mk_ap = masksh[:, g, :] if halo else masks[:, 2 * g, :]
                    else:
                        mk_ap = masks[:, 2 * g + 1, :]
                    P.add("pe", lambda e, bi=bi, qb=qb, kb=kb: e.matmul(
                        pss[ps_s][:, bi * 128:(bi + 1) * 128], lhsT=KTg[:, kb, :], rhs=QTg[:, qb, :], start=True, stop=False),
                        reads=["KTg", "QTg"], writes=[("ps", ps_s)])
                    P.add("pe", lambda e, bi=bi, mk_ap=mk_ap: e.matmul(
                        pss[ps_s][:, bi * 128:(bi + 1) * 128], lhsT=ident[:], rhs=mk_ap, start=False, stop=True),
                        reads=["ident", "masks", "masksh"], writes=[("ps", ps_s)])
                P.add("act", lambda e: e.activation(out=pt, in_=pss[ps_s][:], func=AF.Exp),
                      reads=[("ps", ps_s)], writes=[("PT", 2 * par + kbi)])

        def pv(qb4):
            qblks, info = batch_info(qb4)
            par = qb4 % 2
            pn = 0 + par; pd = 2 + par
            for bi, qb in enumerate(qblks):
                prev, halo, cur = info[bi]
                for kbi in range(2):
                    kb = prev if kbi == 0 else cur
                    pt = PT4[2 * par + kbi]
                    P.add("pe", lambda e, bi=bi, kb=kb, kbi=kbi, pt=pt: e.matmul(
                        pss[pn][:, bi * 128:(bi + 1) * 128], lhsT=Vb[:, kb, :], rhs=pt[:, bi * 128:(bi + 1) * 128],
                        start=(kbi == 0), stop=(kbi == 1)),
                        reads=["Vb", ("PT", 2 * par + kbi)], writes=[("ps", pn)])
                for kbi in range(2):
                    pt = PT4[2 * par + kbi]
                    P.add("pe", lambda e, bi=bi, kbi=kbi, pt=pt: e.matmul(
                        pss[pd][:, bi * 128:(bi + 1) * 128], lhsT=ones_bf[:], rhs=pt[:, bi * 128:(bi + 1) * 128],
                        start=(kbi == 0), stop=(kbi == 1)),
                        reads=["ones", ("PT", 2 * par + kbi)], writes=[("ps", pd)])
            def canon(buf):
                if g == 2:
                    return buf[:, qb4 * 512:(qb4 + 1) * 512], (lambda ap: ap)
                if g == 1:
                    v = buf.rearrange("p (a r q) -> p r a q", a=4, r=4)[:, qb4]
                    return v, (lambda ap: ap.rearrange("p (q a) -> p a q", a=4))
                v = buf.rearrange("p (a q) -> p a q", a=16)[:, :, 32 * qb4:32 * qb4 + 32]
                return v, (lambda ap: ap.rearrange("p (q a) -> p a q", a=16))
            dn, shp = canon(num)
            dd, shp2 = canon(den)
            if g == 0:
                P.add("dve", lambda e: e.tensor_copy(out=dn, in_=shp(pss[pn][:])), reads=[("ps", pn)], writes=["num"])
                P.add("dve", lambda e: e.tensor_copy(out=dd, in_=shp2(pss[pd][:])), reads=[("ps", pd)], writes=["den"])
            else:
                P.add("dve", lambda e: e.tensor_tensor(out=dn, in0=shp(pss[pn][:]), in1=dn, op=ALU.add),
                      reads=[("ps", pn), "num"], writes=["num"])
                P.add("dve", lambda e: e.tensor_tensor(out=dd, in0=shp2(pss[pd][:]), in1=dd, op=ALU.add),
                      reads=[("ps", pd), "den"], writes=["den"])

        for qb4 in range(4):
            scores(qb4)
            if qb4 >= 1:
                pv(qb4 - 1)
        pv(3)
        if g == 2:
            P.add("act", lambda e: e.activation(out=den, in_=den, func=AF.Ln), reads=["den"], writes=["den"])
            P.add("act", lambda e: e.activation(out=den, in_=den, func=AF.Exp, scale=-1.0), reads=["den"], writes=["den"])
            P.add("dve", lambda e, h=h: e.tensor_tensor(out=oT[:, h, :], in0=num, in1=den, op=ALU.mult),
                  reads=["num", "den"], writes=["oT"])
    P.barrier()

    cvT = A[:, 8192:24576].rearrange("p (k t) -> p k t", k=KC)
    xc = A[:, 24576:28704].bitcast(F32).rearrange("p (n q) -> p n q", n=16)
    caccs = [E[:, i * 4096:(i + 1) * 4096].bitcast(F32).rearrange("p (n q) -> p n q", n=16) for i in range(2)]
    ctmp = [A[:, 28704 + i * 1024:28704 + (i + 1) * 1024].bitcast(F32) for i in range(2)]
    P.add("pool", lambda e: e.dma_start(out=wcb, in_=wcb_d.rearrange("(k p) n -> p k n", p=128)),
          writes=["wcb"], chan="wcb")
    h2h_t = h2h.rearrange("p k (n q) -> p k n q", n=16)[:, :, 14:16, 127]
    cctr = [0]

    def conv_uc(fc, s):
        wu_ = RING[s][0]; wc_ = RING[s][1]
        for tb in range(4):
            i2 = cctr[0] % 2; cctr[0] += 1
            pu = 0 + i2; pc = 2 + i2
            for k in range(KC):
                P.add("pe", lambda e, k=k: e.matmul(pss[pu][:], lhsT=wu_[:, k, :], rhs=h2o[:, k, tb * 512:(tb + 1) * 512],
                                                    start=(k == 0), stop=(k == KC - 1)),
                      reads=[("rg", s, 0), "h2o"], writes=[("ps", pu)])
            for k in range(KC):
                P.add("pe", lambda e, k=k: e.matmul(pss[pc][:], lhsT=wc_[:, k, :], rhs=h2o[:, k, tb * 512:(tb + 1) * 512],
                                                    start=(k == 0), stop=(k == KC - 1)),
                      reads=[("rg", s, 1), "h2o"], writes=[("ps", pc)])
            P.add("act", lambda e: e.activation(out=ctmp[i2], in_=pss[pc][:], func=AF.Copy),
                  reads=[("ps", pc)], writes=[("ctmp", i2)])
            P.add("dve", lambda e: e.tensor_tensor(
                out=xc[:, 4 * tb:4 * tb + 4, 1:129], in0=pss[pu][:].rearrange("p (n q) -> p n q", n=4),
                in1=ctmp[i2].rearrange("p (n q) -> p n q", n=4), op=ALU.mult),
                reads=[("ps", pu), ("ctmp", i2)], pwrites=["xc"])
        for k in range(KC):
            P.add("pe", lambda e, k=k: e.matmul(pss[6][:, 0:2], lhsT=wu_[:, k, :], rhs=h2h_t[:, k, :],
                                                start=(k == 0), stop=(k == KC - 1)),
                  reads=[("rg", s, 0), "h2h"], writes=[("ps", 6)])
        for k in range(KC):
            P.add("pe", lambda e, k=k: e.matmul(pss[7][:, 0:2], lhsT=wc_[:, k, :], rhs=h2h_t[:, k, :],
                                                start=(k == 0), stop=(k == KC - 1)),
                  reads=[("rg", s, 1), "h2h"], writes=[("ps", 7)])
        P.add("act", lambda e: e.activation(out=cols[:, 58:60], in_=pss[7][:, 0:2], func=AF.Copy),
              reads=[("ps", 7)], writes=["cols"])
        P.add("dve", lambda e: e.scalar_tensor_tensor(out=xc[:, 14:16, 0], in0=pss[6][:, 0:2], scalar=hm[:, 1:2],
                                                      in1=cols[:, 58:60], op0=ALU.mult, op1=ALU.mult),
              reads=[("ps", 6), "cols", "hm"], writes=["xc"])

    def conv_taps(fc):
        cacc = caccs[fc % 2]; ck = ("cacc", fc % 2)
        w0 = cw[:, fc, 0:1]; w1 = cw[:, fc, 1:2]; w2 = cw[:, fc, 2:3]
        P.add("dve", lambda e: e.tensor_scalar(out=cacc[:, :, :], in0=xc[:, :, 1:129], scalar1=w2, scalar2=None, op0=ALU.mult),
              reads=["xc", "cw"], writes=[ck])
        P.add("dve", lambda e: e.scalar_tensor_tensor(out=cacc[:, 1:16, :], in0=xc[:, 0:15, 1:129], scalar=w1,
                                                      in1=cacc[:, 1:16, :], op0=ALU.mult, op1=ALU.add),
              reads=["xc", "cw", ck], writes=[ck])
        P.add("dve", lambda e: e.scalar_tensor_tensor(out=cacc[:, 0, :], in0=xc[:, 15, 0:128], scalar=w1,
                                                      in1=cacc[:, 0, :], op0=ALU.mult, op1=ALU.add),
              reads=["xc", "cw", ck], writes=[ck])
        P.add("dve", lambda e: e.scalar_tensor_tensor(out=cacc[:, 2:16, :], in0=xc[:, 0:14, 1:129], scalar=w0,
                                                      in1=cacc[:, 2:16, :], op0=ALU.mult, op1=ALU.add),
              reads=["xc", "cw", ck], writes=[ck])
        P.add("dve", lambda e: e.scalar_tensor_tensor(out=cacc[:, 0:2, :], in0=xc[:, 14:16, 0:128], scalar=w0,
                                                      in1=cacc[:, 0:2, :], op0=ALU.mult, op1=ALU.add),
              reads=["xc", "cw", ck], writes=[ck])

    def conv_b(fc, s):
        wb_ = RING[s][2]
        cacc = caccs[fc % 2]; ck = ("cacc", fc % 2)
        for tb in range(4):
            pbk = 4 + tb
            for k in range(KC):
                P.add("pe", lambda e, k=k: e.matmul(pss[pbk][:], lhsT=wb_[:, k, :], rhs=h2o[:, k, tb * 512:(tb + 1) * 512],
                                                    start=(k == 0), stop=(k == KC - 1)),
                      reads=[("rg", s, 2), "h2o"], writes=[("ps", pbk)])
            P.add("dve", lambda e: e.tensor_tensor(
                out=cvT[:, fc, tb * 512:(tb + 1) * 512], in0=pss[pbk][:],
                in1=cacc[:, 4 * tb:4 * tb + 4, :].rearrange("p n q -> p (n q)"), op=ALU.mult),
                reads=[("ps", pbk), ck], pwrites=["cvT"])

    cslots = {}
    for fc in range(KC):
        cslots[fc] = rloaded.pop(0)
        conv_uc(fc, cslots[fc])
        conv_taps(fc)
        if fc >= 1:
            conv_b(fc - 1, cslots[fc - 1])
            issue_ring()
    conv_b(KC - 1, cslots[KC - 1])
    issue_ring()
    P.barrier()

    wout_st = A[:, 24576:32768].rearrange("p (k n) -> p k n", k=KC)
    P.add("pool", lambda e: e.dma_start(out=wout_st, in_=wout_d.rearrange("(k p) n -> p k n", p=128)),
          writes=["wout_st"], chan="wout")
    mT = C[:].rearrange("p (k t) -> p k t", k=KC)
    tab = [E[:, i * 1024:(i + 1) * 1024].bitcast(F32) for i in range(4)]
    m12 = [E[:, 4096 + i * 1024:4096 + (i + 1) * 1024].bitcast(F32) for i in range(4)]
    gctr = [0]
    for oc in range(KC):
        s = rloaded.pop(0)
        wga = RING[s][0]; wgc = RING[s][1]
        for tb in range(4):
            i2 = gctr[0] % 2; gctr[0] += 1
            pya, pyc, pga, pgc = 4 * i2, 4 * i2 + 1, 4 * i2 + 2, 4 * i2 + 3
            tk = tb * 512
            for hh in range(4):
                P.add("pe", lambda e, hh=hh, tk=tk, pya=pya: e.matmul(pss[pya][:], lhsT=wab[:, hh, oc * 128:(oc + 1) * 128],
                                                                       rhs=oT[:, hh, tk:tk + 512], start=(hh == 0), stop=(hh == 3)),
                      reads=["wab", "oT"], writes=[("ps", pya)])
            for k in range(KC):
                P.add("pe", lambda e, k=k, tk=tk, pyc=pyc: e.matmul(pss[pyc][:], lhsT=wcb[:, k, oc * 128:(oc + 1) * 128],
                                                                     rhs=cvT[:, k, tk:tk + 512], start=(k == 0), stop=(k == KC - 1)),
                      reads=["wcb", "cvT"], writes=[("ps", pyc)])
            for k in range(KC):
                P.add("pe", lambda e, k=k, tk=tk, pga=pga: e.matmul(pss[pga][:], lhsT=wga[:, k, :], rhs=h2o[:, k, tk:tk + 512],
                                                                     start=(k == 0), stop=(k == KC - 1)),
                      reads=[("rg", s, 0), "h2o"], writes=[("ps", pga)])
            for k in range(KC):
                P.add("pe", lambda e, k=k, tk=tk, pgc=pgc: e.matmul(pss[pgc][:], lhsT=wgc[:, k, :], rhs=h2o[:, k, tk:tk + 512],
                                                                     start=(k == 0), stop=(k == KC - 1)),
                      reads=[("rg", s, 1), "h2o"], writes=[("ps", pgc)])
            ta = tab[2 * i2]; tc_ = tab[2 * i2 + 1]; m1 = m12[2 * i2]; m2 = m12[2 * i2 + 1]
            P.add("act", lambda e, ta=ta, pga=pga: e.activation(out=ta, in_=pss[pga][:], func=AF.Tanh, scale=0.5),
                  reads=[("ps", pga)], writes=[("ta", i2)])
            P.add("act", lambda e, tc_=tc_, pgc=pgc: e.activation(out=tc_, in_=pss[pgc][:], func=AF.Tanh, scale=0.5),
                  reads=[("ps", pgc)], writes=[("tc", i2)])
            P.add("dve", lambda e, ta=ta, m1=m1, pya=pya: e.scalar_tensor_tensor(out=m1, in0=ta, scalar=1.0, in1=pss[pya][:],
                                                                                 op0=ALU.add, op1=ALU.mult),
                  reads=[("ta", i2), ("ps", pya)], writes=[("m1", i2)])
            P.add("dve", lambda e, tc_=tc_, m2=m2, pyc=pyc: e.scalar_tensor_tensor(out=m2, in0=tc_, scalar=1.0, in1=pss[pyc][:],
                                                                                   op0=ALU.add, op1=ALU.mult),
                  reads=[("tc", i2), ("ps", pyc)], writes=[("m2", i2)])
            P.add("dve", lambda e, m1=m1, m2=m2, tk=tk: e.tensor_tensor(out=mT[:, oc, tk:tk + 512], in0=m1, in1=m2, op=ALU.add),
                  reads=[("m1", i2), ("m2", i2)], pwrites=["mT"])
        issue_ring()
    P.barrier()

    for tg in range(3):
        P.add("sp", lambda e, tg=tg: e.dma_start(out=xbuf[:, 4 * tg:4 * tg + 4, :], in_=x1s_d[:, 4 * tg:4 * tg + 4, :]),
              reads=[("x1s", tg)], writes=[("x", 4 * tg + i) for i in range(4)], chan=("xg", tg))
    for k in range(KC):
        P.add("dve", lambda e, k=k: e.tensor_tensor(out=wout[:, k, :], in0=wout_st[:, k, :], in1=gtrows[:, 1, :], op=ALU.mult),
              reads=["wout_st", ("gt", 1)], pwrites=["wout"])
    P.add("sp", lambda e: e.dma_start(out=xbuf[:, 12:16, :], in_=x1s_d[:, 12:16, :]),
          reads=[("x1s", 3)], writes=[("x", 12 + i) for i in range(4)] + ["wout_st"], chan=("xg", 3))
    for (j0, nj) in PIECES:
        q_piece(f2g_d, f2u_d, f2d_d, j0, nj)
    assert piece_ctr[0] % 3 == 0
    issue_load(); issue_load()
    for n in range(NT):
        for hf in range(2):
            pa = 2 * (n % 2) + hf
            for k in range(KC):
                P.add("pe", lambda e, k=k, n=n, hf=hf, pa=pa: e.matmul(pss[pa][:], lhsT=mT[:, k, n * 128:(n + 1) * 128],
                                                                        rhs=wout[:, k, hf * 512:(hf + 1) * 512],
                                                                        start=(k == 0), stop=(k == KC - 1)),
                      reads=["mT", "wout"], writes=[("ps", pa)])
            P.add("dve", lambda e, n=n, hf=hf, pa=pa: e.tensor_tensor(
                out=xbuf[:, n, hf * 512:(hf + 1) * 512], in0=pss[pa][:], in1=xbuf[:, n, hf * 512:(hf + 1) * 512], op=ALU.add),
                reads=[("ps", pa), ("x", n)], writes=[("x", n)])
    P.barrier()

    issue_load()

    def post_out(tg):
        P.add("sp", lambda e: e.dma_start(out=out_d[:, 4 * tg:4 * tg + 4, :], in_=xbuf[:, 4 * tg:4 * tg + 4, :]),
              reads=[("x", 4 * tg + i) for i in range(4)], writes=[("out", tg)], chan=("out", tg))

    ffn_pass(2, lambda tg: norm_front(tg, 64), lambda tg: norm_back(tg, 32, 40, 64), lambda tg: None, post_out, {})
    P.add("sp", None, reads=[("out", tg) for tg in range(4)])
    P.emit(nc)
    st.close()
    return nc


def _masks():
    m = np.zeros((128, 6, 128), np.float32)
    i = np.arange(128)
    mloc = [i, i, i]
    for g in range(3):
        ml = mloc[g]
        k = ml[:, None]; q = ml[None, :]
        m[:, 2 * g, :] = np.where(k >= q, 0.0, NEG)
        m[:, 2 * g + 1, :] = np.where(k <= q, 0.0, NEG)
    return m.astype(ml_dtypes.bfloat16)


def _host_inputs(inputs):
    x = np.ascontiguousarray(np.asarray(inputs["x"], dtype=np.float32))
    c = np.asarray(inputs["c"], dtype=np.float32)
    sq = lambda name: np.ascontiguousarray(np.asarray(inputs[name], dtype=np.float32)[0])
    col = lambda v: np.ascontiguousarray(v.reshape(KC, 128).T)
    shared = {
        "w_ada": sq("w_ada"), "b_ada": sq("b_ada"),
        "gcols": np.ascontiguousarray(np.concatenate([col(sq("norm_ffn1")), col(sq("norm_mix")), col(sq("norm_ffn2"))], axis=1)),
        "qkn": np.ascontiguousarray(np.stack([sq("q_norm"), sq("k_norm")], axis=1)),
        "cw": np.ascontiguousarray(sq("conv_w").reshape(3, KC, 128).transpose(2, 1, 0)),
        "bcol": np.ascontiguousarray(sq("b_ada").reshape(9 * KC, 128).T),
        "ffn1_w_gate": sq("ffn1_w_gate"), "ffn1_w_up": sq("ffn1_w_up"), "ffn1_w_down": sq("ffn1_w_down"),
        "ffn2_w_gate": sq("ffn2_w_gate"), "ffn2_w_up": sq("ffn2_w_up"), "ffn2_w_down": sq("ffn2_w_down"),
        "w_in": sq("w_in"), "w_attn_branch": sq("w_attn_branch"), "w_conv_branch": sq("w_conv_branch"),
        "w_out": sq("w_out"),
        "ident": np.eye(128).astype(ml_dtypes.bfloat16), "identf": np.eye(128, dtype=np.float32),
        "masks": _masks(),
    }
    in_maps = []
    for core in range(8):
        b, ch = core // 4, core % 4
        xo = x[b, ch * T:(ch + 1) * T].reshape(128, NT, D)
        xh = x[b, (ch - 1) * T:ch * T].reshape(128, NT, D) if ch > 0 else np.zeros((128, NT, D), np.float32)
        hm = np.zeros((128, 2), np.float32)
        hm[:, 0] = 0.0 if ch > 0 else NEG
        hm[:, 1] = 1.0 if ch > 0 else 0.0
        m = dict(shared)
        m.update({"xo": np.ascontiguousarray(xo), "xh": np.ascontiguousarray(xh), "cT": col(c[b]), "hm": hm})
        in_maps.append(m)
    return in_maps


_NC_CACHE = {}


def kernel(**inputs):
    in_maps = _host_inputs(inputs)
    if "nc" not in _NC_CACHE:
        _NC_CACHE["nc"] = build_nc()
    res = run_bass_kernel_spmd(_NC_CACHE["nc"], in_maps, core_ids=list(range(8)))
    out = np.empty((2, 4 * T, D), np.float32)
    for core in range(8):
        b, ch = core // 4, core % 4
        out[b, ch * T:(ch + 1) * T] = np.asarray(res.results[core]["out"]).reshape(T, D)
    return out
```

```python
from contextlib import ExitStack
import numpy as np
import ml_dtypes
import concourse.bass as bass
import concourse.mybir as mybir
from concourse.bass_utils import run_bass_kernel_spmd

F32 = mybir.dt.float32
BF16 = mybir.dt.bfloat16
AF = mybir.ActivationFunctionType
ALU = mybir.AluOpType
AX = mybir.AxisListType

NEG = -30000.0
EPS = 1e-6
D = 1024
KC = 8
DFF = 2816
NJ = 22
T = 2048
NT = 16
INW = 9728


class Op:
    __slots__ = ("eng", "fn", "deps", "chan", "chan_val", "needs_inc", "inc_val")

    def __init__(self, eng, fn, chan):
        self.eng = eng
        self.fn = fn
        self.chan = chan
        self.chan_val = 0
        self.deps = ()
        self.needs_inc = False
        self.inc_val = 0


class _Rec:
    def __init__(self):
        self.call = None

    def __getattr__(self, name):
        def f(*a, **kw):
            self.call = (name, a, kw)
        return f


class Prog:
    ENGS = ("pe", "act", "dve", "pool", "sp")
    BLK = {"pe": "tensor", "act": "scalar", "dve": "vector", "pool": "gpsimd", "sp": "sync"}

    def __init__(self):
        self.ops = []
        self.last_write = {}
        self.readers = {}
        self.chan_count = {}
        self.last_on_eng = {}
        self.last_on_chan = {}
        self.par_epoch = {}
        self.epoch_base = {}
        self.new_epoch = set()

    def add(self, eng, fn, reads=(), writes=(), chan=None, extra_deps=(), pwrites=()):
        if fn is not None:
            rec = _Rec()
            fn(rec)
            assert rec.call is not None
            fn = rec.call
        op = Op(eng, fn, chan)
        deps = set(extra_deps)
        for r in reads:
            deps.update(self.last_write.get(r, ()))
            if isinstance(r, tuple) and r[0] == "ps":
                deps.update(o for o in self.readers.get(r, ()) if o.eng != eng)
        for w in writes:
            deps.update(self.last_write.get(w, ()))
            deps.update(self.readers.get(w, ()))
        for w in pwrites:
            rd = self.readers.get(w, ())
            if rd or not self.par_epoch.get(w, False):
                base = set(rd) | set(self.last_write.get(w, ()))
                self.epoch_base[w] = base
                self.new_epoch.add(w)
            deps.update(self.epoch_base.get(w, ()))
        if eng == "pe":
            deps = {d for d in deps if not (d.eng == "pe" and d.chan is None)}
        deps.discard(op)
        for r in reads:
            self.readers.setdefault(r, []).append(op)
        for w in writes:
            self.last_write[w] = [op]
            self.readers[w] = []
            self.par_epoch[w] = False
        for w in pwrites:
            if w in self.new_epoch:
                self.new_epoch.discard(w)
                self.last_write[w] = [op]
                self.readers[w] = []
                self.par_epoch[w] = True
            else:
                self.last_write[w].append(op)
        if chan is not None:
            n = self.chan_count.get(chan, 0) + 1
            self.chan_count[chan] = n
            op.chan_val = 16 * n
            self.last_on_chan[chan] = op
        elif fn is not None:
            self.last_on_eng[eng] = op
        for d in deps:
            if d.chan is None:
                d.needs_inc = True
        op.deps = tuple(deps)
        self.ops.append(op)
        return op

    def barrier(self):
        deps = list(self.last_on_eng.values()) + list(self.last_on_chan.values())
        for e in self.ENGS:
            self.add(e, None, extra_deps=deps)

    def emit(self, nc):
        cnt = {e: 0 for e in self.ENGS}
        for op in self.ops:
            if op.chan is None and op.needs_inc:
                cnt[op.eng] += 1
                op.inc_val = cnt[op.eng]
        with ExitStack() as st:
            sem_eng = {e: st.enter_context(nc.semaphore("s_" + e)) for e in self.ENGS}
            sem_chan = {c: st.enter_context(nc.semaphore("c_%d" % i))
                        for i, c in enumerate(self.chan_count)}
            block = st.enter_context(nc.Block())
            for e in self.ENGS:
                ops_e = [op for op in self.ops if op.eng == e]
                if not ops_e:
                    continue

                def body(engine, ops_e=ops_e, e=e):
                    waited = {}
                    for op in ops_e:
                        need = {}
                        for d in op.deps:
                            if d.chan is not None:
                                key = ("c", d.chan)
                                s, v = sem_chan[d.chan], d.chan_val
                            else:
                                key = ("e", d.eng)
                                s, v = sem_eng[d.eng], d.inc_val
                            if need.get(key, (None, 0))[1] < v:
                                need[key] = (s, v)
                        for key, (s, v) in need.items():
                            if waited.get(key, 0) < v:
                                engine.wait_ge(s, v)
                                waited[key] = v
                        if op.fn is None:
                            continue
                        name, a, kw = op.fn
                        ins = getattr(engine, name)(*a, **kw)
                        if op.chan is not None:
                            ins.then_inc(sem_chan[op.chan], 16)
                        elif op.needs_inc:
                            ins.then_inc(sem_eng[e], 1)

                getattr(block, self.BLK[e])(body)


def build_nc(debug_stop=None):
    nc = bass.Bass("TRN2", target_bir_lowering=False)

    def din(name, shape, dt=F32):
        return nc.dram_tensor(name, list(shape), dt, kind="ExternalInput").ap()

    xo_d = din("xo", [128, NT, D])
    xh_d = din("xh", [128, NT, D])
    cT_d = din("cT", [128, KC])
    hm_d = din("hm", [128, 2])
    wada_d = din("w_ada", [D, 9 * D])
    bada_d = din("b_ada", [9 * D])
    gcol_d = din("gcols", [128, 3 * KC])
    qk_d = din("qkn", [128, 2])
    cw_d = din("cw", [128, KC, 3])
    bcol_d = din("bcol", [128, 9 * KC])
    f1g_d = din("ffn1_w_gate", [D, DFF]); f1u_d = din("ffn1_w_up", [D, DFF]); f1d_d = din("ffn1_w_down", [DFF, D])
    f2g_d = din("ffn2_w_gate", [D, DFF]); f2u_d = din("ffn2_w_up", [D, DFF]); f2d_d = din("ffn2_w_down", [DFF, D])
    win_d = din("w_in", [D, INW])
    wab_d = din("w_attn_branch", [512, D])
    wcb_d = din("w_conv_branch", [D, D])
    wout_d = din("w_out", [D, D])
    ident_d = din("ident", [128, 128], BF16)
    identf_d = din("identf", [128, 128], F32)
    masks_d = din("masks", [128, 6, 128], BF16)
    out_d = nc.dram_tensor("out", [128, NT, D], F32, kind="ExternalOutput").ap()
    x1s_d = nc.dram_tensor("x1s", [128, NT, D], F32).ap()
    h2s_d = nc.dram_tensor("h2s", [128, KC, T], BF16).ap()

    P = Prog()
    st = ExitStack()
    sb = lambda name, shape, dt: st.enter_context(nc.sbuf_tensor(name, list(shape), dt))
    A = sb("A", [128, 32768], BF16)
    B = sb("B", [128, 16384], BF16)
    C = sb("C", [128, 16384], BF16)
    Dw = sb("Dw", [128, 24576], BF16)
    E = sb("E", [128, 8192], BF16)
    ident = sb("ident_s", [128, 128], BF16)
    identf = sb("identf_s", [128, 128], F32)
    masks = sb("masks_s", [128, 8, 128], BF16)
    masksh = sb("masksh_s", [128, 3, 128], BF16)
    ones_bf = sb("ones_s", [128, 128], BF16)
    gtrows = sb("gtrows", [128, 3, D], BF16)
    cols = sb("cols", [128, 64], F32)
    gcols = sb("gcols_s", [128, 3 * KC], F32)
    qk = sb("qk_s", [128, 4], F32)
    cw = sb("cw_s", [128, KC, 3], F32)
    hm = sb("hm_s", [128, 2], F32)
    cact = sb("cact", [128, KC], F32)
    cactb = sb("cactb", [128, KC], BF16)
    bcol = sb("bcol_s", [128, 9 * KC], F32)
    crep = sb("crep", [128, KC, 128], BF16)
    ssb = sb("ssb", [128, 6 * NT], F32)
    rstd = sb("rstd", [128, 6 * NT], F32)
    mhalf = sb("mhalf", [128, NT], F32)
    pss = [st.enter_context(nc.psum_tensor("ps%d" % i, [128, 512], F32)) for i in range(8)]

    def psb(i):
        return pss[i][:].bitcast(BF16)

    xbuf = A[:].bitcast(F32).rearrange("p (n d) -> p n d", n=NT)
    hT = B[:].rearrange("p (k t) -> p k t", k=KC)

    cload = []
    def cl(dst, src, key):
        cload.append(key)
        P.add("sp", lambda e: e.dma_start(out=dst, in_=src), writes=[key], chan="const")
    cl(ident[:], ident_d, "ident"); cl(identf[:], identf_d, "identf")
    cl(masks[:, 0:6, :], masks_d, "masks"); cl(gcols[:], gcol_d, "gcols")
    cl(qk[:, 0:2], qk_d, "qk"); cl(cw[:], cw_d, "cw"); cl(hm[:], hm_d, "hm"); cl(cact[:], cT_d, "cact"); cl(bcol[:], bcol_d, "bcol")
    P.add("dve", lambda e: e.memset(ones_bf[:], 1.0), reads=cload, writes=cload + ["ones"])
    P.add("dve", lambda e: e.memset(mhalf[:], -0.5), writes=["mhalf"])
    for g in range(3):
        P.add("dve", lambda e, g=g: e.tensor_scalar(out=masksh[:, g, :], in0=masks[:, 2 * g, :],
                                                    scalar1=hm[:, 0:1], scalar2=None, op0=ALU.add),
              reads=["masks", "hm"], writes=["masksh"])
    P.add("dve", lambda e: e.tensor_scalar(out=qk[:, 2:3], in0=qk[:, 1:2], scalar1=float(128 ** -0.5),
                                           scalar2=None, op0=ALU.mult), reads=["qk"], writes=["qk"])
    P.add("act", lambda e: e.activation(out=cact[:], in_=cact[:], func=AF.Silu), reads=["cact"], writes=["cact"])
    P.add("dve", lambda e: e.tensor_copy(out=cactb[:], in_=cact[:]), reads=["cact"], writes=["cactb"])
    for k in range(KC):
        P.add("dve", lambda e, k=k: e.tensor_scalar(out=crep[:, k, :], in0=ones_bf[:], scalar1=cact[:, k:k + 1],
                                                    scalar2=None, op0=ALU.mult),
              reads=["cact", "ones"], writes=["crep"])

    def slot_views(s):
        base = Dw[:, s * 12288:(s + 1) * 12288] if s < 2 else C[:, 0:12288]
        wg = base[:, 0:4096].rearrange("p (k n) -> p k n", k=KC)
        wu = base[:, 4096:8192].rearrange("p (k n) -> p k n", k=KC)
        wd = base[:, 8192:12288].rearrange("p (j n) -> p j n", j=4)
        wa = base[:, 0:8192].rearrange("p (k n) -> p k n", k=KC)
        return wg, wu, wd, wa
    SLOTS = [slot_views(s) for s in range(3)]

    hid = [E[:, i * 2048:(i + 1) * 2048].rearrange("p (j t) -> p j t", j=4) for i in range(2)]
    sgb = [E[:, 4096 + i * 1024:4096 + (i + 1) * 1024].bitcast(F32) for i in range(2)]
    xnb = [E[:, 6144:7168], E[:, 7168:8192], C[:, 13312:14336], C[:, 14336:15360]]
    junk = C[:, 12288:13312]
    brow = C[0:1, 15360:16384]
    PIECES = [(0, 4), (4, 4), (8, 2), (10, 4), (14, 4), (18, 4)]

    pending_loads = []
    piece_ctr = [0]

    def q_piece(wg_d, wu_d, wd_d, j0, nj):
        def mk():
            idx = piece_ctr[0]; piece_ctr[0] += 1
            s = idx % 3
            wg, wu, wd, _ = SLOTS[s]
            nc_ = nj * 128
            P.add("pool", lambda e: e.dma_start(out=wg[:, :, 0:nc_],
                  in_=wg_d[:, j0 * 128:j0 * 128 + nc_].rearrange("(k p) n -> p k n", p=128)),
                  writes=[("wg", s)], chan=("wg", s))
            P.add("pool", lambda e: e.dma_start(out=wu[:, :, 0:nc_],
                  in_=wu_d[:, j0 * 128:j0 * 128 + nc_].rearrange("(k p) n -> p k n", p=128)),
                  writes=[("wu", s)], chan=("wu", s))
            P.add("pool", lambda e: e.dma_start(out=wd[:, 0:nj, :],
                  in_=wd_d[j0 * 128:j0 * 128 + nc_, :].rearrange("(j p) n -> p j n", p=128)),
                  writes=[("wd", s)], chan=("wd", s))
            return s
        pending_loads.append(mk)

    def q_mod(j):
        def mk():
            idx = piece_ctr[0]; piece_ctr[0] += 1
            s = idx % 3
            wa = SLOTS[s][3]
            P.add("pool", lambda e: e.dma_start(
                out=wa, in_=wada_d[:, j * D:(j + 1) * D].rearrange("(k p) n -> p k n", p=128)),
                writes=[("wg", s), ("wu", s)], chan=("wg", s))
            if j % 3 == 2:
                P.add("pool", lambda e: e.dma_start(out=brow, in_=bada_d[j * D:(j + 1) * D].rearrange("(o n) -> o n", o=1)),
                      writes=["brow"], chan="brow")
            return s
        pending_loads.append(mk)

    loaded = []

    def issue_load():
        if pending_loads:
            loaded.append(pending_loads.pop(0)())

    def mod_block(j, issue=True):
        s = loaded.pop(0)
        wa = SLOTS[s][3]
        role, li = j % 3, j // 3
        if role == 2:
            for hf in range(2):
                pb = 6 + hf
                for k in range(KC):
                    P.add("pe", lambda e, k=k: e.matmul(pss[pb][:], lhsT=crep[:, k, :], rhs=wa[:, k, hf * 512:(hf + 1) * 512],
                                                        start=(k == 0), stop=False),
                          reads=["crep", ("wg", s), ("wu", s)], writes=[("ps", pb)])
                P.add("pe", lambda e: e.matmul(pss[pb][:], lhsT=ones_bf[0:1, :], rhs=brow[:, hf * 512:(hf + 1) * 512],
                                               start=False, stop=True),
                      reads=["ones", "brow"], writes=[("ps", pb)])
                P.add("dve", lambda e: e.tensor_scalar(out=gtrows[:, li, hf * 512:(hf + 1) * 512], in0=pss[pb][:], scalar1=0.5,
                                                       scalar2=None, op0=ALU.mult),
                      reads=[("ps", pb)], writes=[("gt", li)])
        else:
            pb = 7
            for kc in range(KC):
                for k in range(KC):
                    P.add("pe", lambda e, k=k, kc=kc: e.matmul(pss[pb][:, kc:kc + 1], lhsT=wa[:, k, kc * 128:(kc + 1) * 128],
                                                               rhs=cactb[:, k:k + 1], start=(k == 0), stop=(k == KC - 1)),
                          reads=["cactb", ("wg", s), ("wu", s)], writes=[("ps", pb)])
            if role == 0:
                P.add("dve", lambda e: e.tensor_tensor(out=cols[:, 16 * li + 8:16 * li + 16], in0=pss[pb][:, 0:KC],
                                                       in1=bcol[:, j * KC:(j + 1) * KC], op=ALU.add),
                      reads=[("ps", pb), "bcol"], writes=["cols"])
            else:
                P.add("dve", lambda e: e.tensor_tensor(out=cols[:, 48:56], in0=pss[pb][:, 0:KC],
                                                       in1=bcol[:, j * KC:(j + 1) * KC], op=ALU.add),
                      reads=[("ps", pb), "bcol"], writes=["cols"])
                P.add("dve", lambda e: e.scalar_tensor_tensor(
                    out=cols[:, 16 * li:16 * li + 8], in0=cols[:, 48:56], scalar=1.0, in1=gcols[:, li * KC:(li + 1) * KC],
                    op0=ALU.add, op1=ALU.mult), reads=["cols", "gcols"], writes=["cols"])
        if issue:
            issue_load()

    def load_x(src_d, tg, after=()):
        P.add("sp", lambda e: e.dma_start(out=xbuf[:, 4 * tg:4 * tg + 4, :], in_=src_d[:, 4 * tg:4 * tg + 4, :]),
              reads=list(after), writes=[("x", 4 * tg + i) for i in range(4)], chan=("xg", tg))

    tctr = [0]

    ng_state = {}

    def norm_front(tg, off=0):
        c0 = off + 4 * tg
        for i in range(4):
            n = 4 * tg + i
            P.add("act", lambda e, n=n, i=i: e.activation(out=junk, in_=xbuf[:, n, :], func=AF.Square,
                                                          accum_out=ssb[:, c0 + i:c0 + i + 1]),
                  reads=[("x", n), "ssb"], writes=[("ss", off + n), "junk"])
        P.add("dve", lambda e: e.tensor_scalar(out=rstd[:, c0:c0 + 4], in0=ssb[:, c0:c0 + 4],
                                               scalar1=1.0 / D, scalar2=EPS, op0=ALU.mult, op1=ALU.add),
              reads=["ssb"] + [("ss", off + 4 * tg + i) for i in range(4)], writes=[("rs", off + tg)])
        P.add("pool", lambda e: e.tensor_tensor(out=rstd[:, c0:c0 + 4], in0=rstd[:, c0:c0 + 4],
                                                in1=mhalf[:, 0:4], op=ALU.pow),
              reads=[("rs", off + tg), "mhalf"], writes=[("rs", off + tg)])
        tiles = []
        for i in range(4):
            n = 4 * tg + i
            xi = tctr[0] % 4
            pb = 6 + (tctr[0] % 2); tctr[0] += 1
            xb = xnb[xi]
            tiles.append((n, i, xi, pb, xb))
            P.add("pool", lambda e, n=n, i=i, xb=xb: e.tensor_scalar(out=xb, in0=xbuf[:, n, :], scalar1=rstd[:, c0 + i:c0 + i + 1],
                                                                     scalar2=0.0, op0=ALU.mult, op1=ALU.add),
                  reads=[("x", n), ("rs", off + tg)], writes=[("xn", xi)])
        ng_state[(off, tg)] = tiles

    def norm_back(tg, acol, scol, off=0):
        tiles = ng_state.pop((off, tg))

        def transp(t):
            n, i, xi, pb, xb = t
            for c in range(KC):
                P.add("pe", lambda e, c=c: e.transpose(
                    out=psb(pb)[:, c * 128:(c + 1) * 128], in_=xb[:, c * 128:(c + 1) * 128], identity=ident[:]),
                    reads=[("xn", xi), "ident"], writes=[("ps", pb)])

        def evac(t):
            n, i, xi, pb, xb = t
            for c in range(KC):
                if pb == 6:
                    P.add("dve", lambda e, c=c: e.tensor_scalar(
                        out=hT[:, c, n * 128:(n + 1) * 128], in0=psb(pb)[:, c * 128:(c + 1) * 128],
                        scalar1=cols[:, acol + c:acol + c + 1], scalar2=cols[:, scol + c:scol + c + 1],
                        op0=ALU.mult, op1=ALU.add), reads=[("ps", pb), "cols"], pwrites=[("hT", n)])
                else:
                    P.add("act", lambda e, c=c: e.activation(
                        out=hT[:, c, n * 128:(n + 1) * 128], in_=psb(pb)[:, c * 128:(c + 1) * 128],
                        func=AF.Identity, scale=cols[:, acol + c:acol + c + 1], bias=cols[:, scol + c:scol + c + 1]),
                        reads=[("ps", pb), "cols"], pwrites=[("hT", n)])

        transp(tiles[0]); transp(tiles[1])
        evac(tiles[0]); transp(tiles[2])
        evac(tiles[1]); transp(tiles[3])
        evac(tiles[2]); evac(tiles[3])

    mctr = [0]
    actr = [0]

    def unit_gu(s, nj, tg, hb, js):
        wg, wu, wd, _ = SLOTS[s]
        hd = hid[hb]
        for j in js:
            pg = j % 2
            for k in range(KC):
                P.add("pe", lambda e, j=j, k=k: e.matmul(
                    pss[pg][:], lhsT=wg[:, k, j * 128:(j + 1) * 128], rhs=hT[:, k, tg * 512:(tg + 1) * 512],
                    start=(k == 0), stop=(k == KC - 1)),
                    reads=[("wg", s)] + [("hT", 4 * tg + i) for i in range(4)], writes=[("ps", pg)])
            for k in range(KC):
                P.add("pe", lambda e, j=j, k=k: e.matmul(
                    pss[2 + pg][:], lhsT=wu[:, k, j * 128:(j + 1) * 128], rhs=hT[:, k, tg * 512:(tg + 1) * 512],
                    start=(k == 0), stop=(k == KC - 1)),
                    reads=[("wu", s)] + [("hT", 4 * tg + i) for i in range(4)], writes=[("ps", 2 + pg)])
            P.add("act", lambda e: e.activation(out=sgb[pg], in_=pss[pg][:], func=AF.Silu),
                  reads=[("ps", pg)], writes=[("sg", pg)])
            P.add("dve", lambda e, j=j: e.tensor_tensor(out=hd[:, j, :], in0=pss[2 + pg][:], in1=sgb[pg], op=ALU.mult),
                  reads=[("ps", 2 + pg), ("sg", pg)], pwrites=[("hid", hb)])

    def unit_down(s, nj, tg, hb):
        wg, wu, wd, _ = SLOTS[s]
        hd = hid[hb]
        for t in range(4):
            n = 4 * tg + t
            for hf in range(2):
                pa = 4 + (actr[0] % 4); actr[0] += 1
                for j in range(nj):
                    P.add("pe", lambda e, j=j: e.matmul(
                        pss[pa][:], lhsT=hd[:, j, t * 128:(t + 1) * 128], rhs=wd[:, j, hf * 512:(hf + 1) * 512],
                        start=(j == 0), stop=(j == nj - 1)),
                        reads=[("hid", hb), ("wd", s)], writes=[("ps", pa)])
                P.add("dve", lambda e: e.tensor_tensor(
                    out=xbuf[:, n, hf * 512:(hf + 1) * 512], in0=pss[pa][:],
                    in1=xbuf[:, n, hf * 512:(hf + 1) * 512], op=ALU.add),
                    reads=[("ps", pa), ("x", n)], writes=[("x", n)])

    def new_hb():
        hb = mctr[0] % 2; mctr[0] += 1
        return hb

    def ffn_unit(s, nj, tg, mid=None):
        hb = new_hb()
        unit_gu(s, nj, tg, hb, range(nj))
        if mid is not None:
            mid()
        unit_down(s, nj, tg, hb)

    def ffn_pass(gi, pre_f, pre_b, post_f, post_b, extras, first_mid=None, nxt=None, head=None):
        if head is None:
            pre_f(0); pre_b(0); pre_f(1)
        deferred = None
        for q, (j0, nj) in enumerate(PIECES):
            s = loaded.pop(0)
            wd = SLOTS[s][2]

            def scale_wd(s=s, wd=wd, nj=nj):
                for j in range(nj):
                    P.add("dve", lambda e, j=j: e.tensor_tensor(out=wd[:, j, :], in0=wd[:, j, :], in1=gtrows[:, gi, :], op=ALU.mult),
                          reads=[("wd", s), ("gt", gi)], writes=[("wd", s)])
            mid0 = None
            if q == 0 and first_mid is not None:
                def mid0(scale_wd=scale_wd):
                    first_mid()
                    scale_wd()
            else:
                scale_wd()
            last = (q == len(PIECES) - 1)
            hbs = [None] * 4
            for tg in range(4):
                if q == 0:
                    ffn_unit(s, nj, tg, mid=(mid0 if tg == 0 else None))
                else:
                    if tg == 0:
                        hbs[0] = new_hb()
                        unit_gu(s, nj, 0, hbs[0], range(nj))
                    if tg < 3:
                        hbs[tg + 1] = new_hb()
                        unit_gu(s, nj, tg + 1, hbs[tg + 1], [0])
                    unit_down(s, nj, tg, hbs[tg])
                    if tg < 3:
                        unit_gu(s, nj, tg + 1, hbs[tg + 1], range(1, nj))
                if q == 0:
                    if tg == 0 and head is not None:
                        head()
                    if tg + 1 < 4:
                        pre_b(tg + 1)
                    if tg + 2 < 4:
                        pre_f(tg + 2)
                if last:
                    if tg >= 1:
                        post_b(tg - 1)
                    post_f(tg)
                    if nxt is not None and tg == 2:
                        nxt[0](0)
                    if nxt is not None and tg == 3:
                        nxt[1](0)
                        nxt[0](1)
            issue_load()
            if q == 0 and first_mid is not None:
                issue_load()
            for fn in extras.get(q, ()):
                fn()
        if nxt is not None:
            return lambda: post_b(3)
        post_b(3)
        return None

    def ring(s):
        base = Dw[:, s * 3072:(s + 1) * 3072]
        return [base[:, i * 1024:(i + 1) * 1024].rearrange("p (k n) -> p k n", k=KC) for i in range(3)]
    RING = [ring(s) for s in range(3)]
    wab = Dw[:, 9216:13312].rearrange("p (h n) -> p h n", h=4)
    wcb = Dw[:, 13312:21504].rearrange("p (k n) -> p k n", k=KC)
    wout = E[:, 0:8192].rearrange("p (k n) -> p k n", k=KC)

    win_v = win_d.rearrange("(k p) n -> p k n", p=128)
    rctr = [0]
    ring_extra = []
    rloads = []

    def queue_ring(colsets):
        def mk():
            s = rctr[0] % 3; rctr[0] += 1
            for i, c0 in enumerate(colsets):
                P.add("pool", lambda e, i=i, c0=c0, s=s: e.dma_start(out=RING[s][i], in_=win_v[:, :, c0:c0 + 128]),
                      writes=[("rg", s, i)] + ring_extra, chan=("rg", s, i))
            return s
        rloads.append(mk)
    rloaded = []

    def issue_ring():
        if rloads:
            rloaded.append(rloads.pop(0)())

    order = [(g, h) for h in range(4) for g in range(3)]
    for (g, h) in order:
        idx = g * 4 + h
        queue_ring([idx * 128, 1536 + idx * 128, 3072 + idx * 128])
    for fc in range(KC):
        queue_ring([4608 + fc * 128, 6656 + fc * 128, 5632 + fc * 128])
    for oc in range(KC):
        queue_ring([7680 + oc * 128, 8704 + oc * 128])

    F1 = (f1g_d, f1u_d, f1d_d)
    for j in (0, 1):
        q_mod(j)
    for q, (j0, nj) in enumerate(PIECES):
        q_piece(*F1, j0, nj)
        if q <= 2:
            q_mod(2 + q)
    for q, (j0, nj) in enumerate(PIECES):
        q_piece(*F1, j0, nj)
        if q < 4:
            q_mod(5 + q)

    load_x(xh_d, 0)
    for _ in range(3):
        issue_load()
    for tg in range(1, 4):
        load_x(xh_d, tg, after=[("wg", 0)])
    P.add("dve", lambda e: e.memset(ssb[:], 0.0), writes=["ssb"])
    for j in (0, 1):
        mod_block(j)

    def post_halo_b(tg):
        norm_back(tg, 16, 24, 16)
        P.add("sp", lambda e: e.dma_start(out=h2s_d[:, :, tg * 512:(tg + 1) * 512], in_=hT[:, :, tg * 512:(tg + 1) * 512]),
              reads=[("hT", 4 * tg + i) for i in range(4)], writes=[("h2s", tg)], chan=("h2s", tg))
        load_x(xo_d, tg)

    own_pre = (lambda tg: norm_front(tg, 32), lambda tg: norm_back(tg, 0, 8, 32))
    tail = ffn_pass(0, lambda tg: norm_front(tg, 0), lambda tg: norm_back(tg, 0, 8, 0),
                    lambda tg: norm_front(tg, 16), post_halo_b,
                    {1: [lambda: mod_block(3)], 2: [lambda: mod_block(4)]},
                    first_mid=lambda: mod_block(2, issue=False))

    def early_ring():
        ring_extra.extend([("wg", 0), ("wu", 0), ("wd", 0)])
        for _ in range(3):
            issue_ring()
        del ring_extra[:]

    def post_own_f(tg):
        P.add("sp", lambda e: e.dma_start(out=x1s_d[:, 4 * tg:4 * tg + 4, :], in_=xbuf[:, 4 * tg:4 * tg + 4, :]),
              reads=[("x", 4 * tg + i) for i in range(4)], writes=[("x1s", tg)], chan=("x1s", tg))
        norm_front(tg, 48)

    ffn_pass(0, own_pre[0], own_pre[1],
             post_own_f, lambda tg: norm_back(tg, 16, 24, 48),
             {0: [lambda: mod_block(5)], 1: [lambda: mod_block(6)], 2: [lambda: mod_block(7)], 3: [lambda: mod_block(8), early_ring]},
             head=None)
    P.barrier()

    h2o = hT
    h2h = C[:].rearrange("p (k t) -> p k t", k=KC)
    for tg in range(4):
        P.add("sp", lambda e, tg=tg: e.dma_start(out=h2h[:, :, tg * 512:(tg + 1) * 512], in_=h2s_d[:, :, tg * 512:(tg + 1) * 512]),
              reads=[("h2s", tg)], pwrites=["h2h"], chan=("h2h", tg))
    oT = A[:, 0:8192].rearrange("p (h t) -> p h t", h=4)
    QTg = A[:, 8192:10240].rearrange("p (b i) -> p b i", b=16)
    KTg = A[:, 10240:14336].rearrange("p (b i) -> p b i", b=32)
    VTg = A[:, 14336:18432].rearrange("p (b i) -> p b i", b=32)
    Vb = A[:, 18432:22528].rearrange("p (b i) -> p b i", b=32)
    num = A[:, 22528:26624].bitcast(F32)
    den = A[:, 26624:30720].bitcast(F32)
    PT = [A[:, 30720 + i * 512:30720 + (i + 1) * 512] for i in range(2)]
    PT4 = PT + [E[:, 5120:5632], E[:, 5632:6144]]
    sqb = [A[:, 31744 + i * 512:31744 + (i + 1) * 512] for i in range(2)]
    rrb = [E[:, i * 1024:(i + 1) * 1024].bitcast(F32) for i in range(2)]

    P.add("pool", lambda e: e.dma_start(out=wab, in_=wab_d.rearrange("(h p) n -> p h n", p=128)),
          writes=["wab"], chan="wab")

    def dst_src(buf, base_blk, g, tb):
        flat = buf.rearrange("p b i -> p (b i)")
        o = base_blk * 128
        if g == 2:
            return flat[:, o + tb * 512:o + (tb + 1) * 512], (lambda ap: ap)
        if g == 1:
            dst = flat[:, o:o + 2048].rearrange("p (r q a) -> p r q a", r=4, a=4)[:, :, :, tb]
            return dst, (lambda ap: ap.rearrange("p (r q) -> p r q", r=4))
        dst = flat[:, o:o + 2048].rearrange("p (q a) -> p a q", a=16)[:, 4 * tb:4 * tb + 4, :]
        return dst, (lambda ap: ap.rearrange("p (a q) -> p a q", a=4))

    pctr = [0]
    qctr = [0]
    tctr2 = [0]
    ptmp = [E[:, 2048 + i * 512:2048 + (i + 1) * 512] for i in range(6)]
    pending_fin = []

    def proj_block(w, rhs_fn, ncols, kind, gcol, dst, shape_fn, contig, src="h2o"):
        pb = pctr[0] % 3; pctr[0] += 1
        for k in range(KC):
            P.add("pe", lambda e, k=k: e.matmul(pss[pb][:, 0:ncols], lhsT=w[0][:, k, :], rhs=rhs_fn(k),
                                                start=(k == 0), stop=(k == KC - 1)),
                  reads=[w[1], src], writes=[("ps", pb)])
        key = {"q": "QTg", "k": "KTg", "v": "VTg"}[kind]

        def store(src_fn, reads, eng_kind):
            if contig:
                out_ap, wkey, pw = dst, None, [key]
            else:
                ti = tctr2[0] % 6; tctr2[0] += 1
                out_ap, wkey, pw = ptmp[ti][:, 0:ncols], ("ptmp", ti), []
            src_fn(out_ap if contig else out_ap, reads, [wkey] if wkey else [], pw)
            if not contig:
                P.add("pool", lambda e: e.tensor_copy(out=dst, in_=shape_fn(ptmp[ti][:, 0:ncols])),
                      reads=[("ptmp", ti)], pwrites=[key])

        if kind == "v":
            def src(out_ap, reads, wr, pw):
                P.add("dve", lambda e: e.tensor_copy(out=out_ap, in_=pss[pb][:, 0:ncols]),
                      reads=[("ps", pb)], writes=wr, pwrites=pw)
            store(src, None, None)
            while pending_fin:
                pending_fin.pop(0)()
            return
        i2 = qctr[0] % 2; qctr[0] += 1
        sq = sqb[i2]; rr = rrb[i2]
        P.add("act", lambda e: e.activation(out=sq[:, 0:ncols], in_=pss[pb][:, 0:ncols], func=AF.Square),
              reads=[("ps", pb)], writes=[("sq", i2)])

        def fin():
            P.add("pe", lambda e: e.matmul(pss[3 + i2][:, 0:ncols], lhsT=ones_bf[:], rhs=sq[:, 0:ncols], start=True, stop=True),
                  reads=[("sq", i2), "ones"], writes=[("ps", 3 + i2)])
            P.add("act", lambda e: e.activation(out=rr[:, 0:ncols], in_=pss[3 + i2][:, 0:ncols], func=AF.Ln,
                                                scale=(1.0 if kind == "q" else 1.0 / 128),
                                                bias=(cols[:, 57:58] if kind == "q" else cols[:, 56:57])),
                  reads=[("ps", 3 + i2), "cols"], writes=[("rr", i2)])
            P.add("act", lambda e: e.activation(out=rr[:, 0:ncols], in_=rr[:, 0:ncols], func=AF.Exp, scale=-0.5),
                  reads=[("rr", i2)], writes=[("rr", i2)])

            def src(out_ap, reads, wr, pw):
                P.add("dve", lambda e: e.scalar_tensor_tensor(out=out_ap, in0=pss[pb][:, 0:ncols], scalar=gcol,
                                                              in1=rr[:, 0:ncols], op0=ALU.mult, op1=ALU.mult),
                      reads=[("ps", pb), ("rr", i2), "qk"], writes=wr, pwrites=pw)
            store(src, None, None)
        prev = list(pending_fin)
        del pending_fin[:]
        pending_fin.append(fin)
        for f in prev:
            f()

    def flush_fin():
        while pending_fin:
            pending_fin.pop(0)()

    P.add("dve", lambda e: e.memset(cols[:, 56:57], EPS), writes=["cols"])
    P.add("dve", lambda e: e.memset(cols[:, 57:58], 128 * EPS), writes=["cols"])

    h2h4 = h2h.rearrange("p k (n q) -> p k n q", n=16)
    sctr = [0]
    for (g, h) in order:
        s = rloaded.pop(0)
        wq = (RING[s][0], ("rg", s, 0)); wk = (RING[s][1], ("rg", s, 1)); wv = (RING[s][2], ("rg", s, 2))
        for tb in range(4):
            rf = lambda k, tb=tb: h2o[:, k, tb * 512:(tb + 1) * 512]
            d_, sf = dst_src(VTg, 16, g, tb)
            proj_block(wv, rf, 512, "v", None, d_, sf, g == 2)
            d_, sf = dst_src(QTg, 0, g, tb)
            proj_block(wq, rf, 512, "q", qk[:, 0:1], d_, sf, g == 2)
            d_, sf = dst_src(KTg, 16, g, tb)
            proj_block(wk, rf, 512, "k", qk[:, 1:2], d_, sf, g == 2)
        if g == 2:
            for tb in range(4):
                rf = lambda k, tb=tb: h2h[:, k, tb * 512:(tb + 1) * 512]
                d_, sf = dst_src(VTg, 0, 2, tb)
                proj_block(wv, rf, 512, "v", None, d_, sf, True, src="h2h")
            for tb in range(4):
                rf = lambda k, tb=tb: h2h[:, k, tb * 512:(tb + 1) * 512]
                d_, sf = dst_src(KTg, 0, 2, tb)
                proj_block(wk, rf, 512, "k", qk[:, 1:2], d_, sf, True, src="h2h")
            nhalo = 16
        elif g == 1:
            rf = lambda k: h2h4[:, k, :, 96:128]
            sf = lambda ap: ap.rearrange("p (a rb) -> p a rb", a=4)
            dv = lambda buf: buf.rearrange("p b i -> p (b i)")[:, 0:512].rearrange("p (rb a) -> p a rb", a=4)
            proj_block(wv, rf, 512, "v", None, dv(VTg), sf, False, src="h2h")
            proj_block(wk, rf, 512, "k", qk[:, 1:2], dv(KTg), sf, False, src="h2h")
            nhalo = 4
        else:
            rf = lambda k: h2h4[:, k, :, 120:128]
            sf = lambda ap: ap.rearrange("p (a b) -> p a b", a=16)
            dv = lambda buf: buf[:, 0, :].rearrange("p (b a) -> p a b", a=16)
            proj_block(wv, rf, 128, "v", None, dv(VTg), sf, False, src="h2h")
            proj_block(wk, rf, 128, "k", qk[:, 1:2], dv(KTg), sf, False, src="h2h")
            nhalo = 1
        flush_fin()
        issue_ring()
        grps = [list(range(i0, min(i0 + 8, nhalo))) for i0 in range(0, nhalo, 8)] + [list(range(16, 24)), list(range(24, 32))]
        for grp in grps:
            pb = 6 + (sctr[0] % 2); sctr[0] += 1
            for ii, blk in enumerate(grp):
                P.add("pe", lambda e, ii=ii, blk=blk, pb=pb: e.transpose(
                    out=psb(pb)[:, ii * 128:(ii + 1) * 128], in_=VTg[:, blk, :], identity=ident[:]),
                    reads=["VTg", "ident"], writes=[("ps", pb)])
            b0 = grp[0]; nb = len(grp)
            P.add("dve", lambda e, b0=b0, nb=nb, pb=pb: e.tensor_copy(
                out=Vb[:, b0:b0 + nb, :], in_=psb(pb)[:, 0:nb * 128].rearrange("p (b i) -> p b i", b=nb)),
                reads=[("ps", pb)], pwrites=["Vb"])
        def batch_info(qb4):
            qblks = [4 * qb4 + i for i in range(4)]
            info = []
            for qb in qblks:
                cur = 16 + qb
                if g == 2:
                    prev, halo = qb, True
                elif g == 1:
                    r4, n1 = qb // 4, qb % 4
                    prev, halo = (r4, True) if n1 == 0 else (16 + qb - 1, False)
                else:
                    prev, halo = (0, True) if qb == 0 else (16 + qb - 1, False)
                info.append((prev, halo, cur))
            return qblks, info

        def scores(qb4):
            qblks, info = batch_info(qb4)
            par = qb4 % 2
            for kbi in range(2):
                ps_s = 4 + 2 * par + kbi
                pt = PT4[2 * par + kbi]
                for bi, qb in enumerate(qblks):
                    prev, halo, cur = info[bi]
                    kb = prev if kbi == 0 else cur
                    if kbi == 0:
                        mk_ap = masksh[:, g, :] if halo else masks[:, 2 * g, :]
                    else:
                        mk_ap = masks[:, 2 * g + 1, :]
                    P.add("pe", lambda e, bi=bi, qb=qb, kb=kb: e.matmul(
                        pss[ps_s][:, bi * 128:(bi + 1) * 128], lhsT=KTg[:, kb, :], rhs=QTg[:, qb, :], start=True, stop=False),
                        reads=["KTg", "QTg"], writes=[("ps", ps_s)])
                    P.add("pe", lambda e, bi=bi, mk_ap=mk_ap: e.matmul(
                        pss[ps_s][:, bi * 128:(bi + 1) * 128], lhsT=ident[:], rhs=mk_ap, start=False, stop=True),
                        reads=["ident", "masks", "masksh"], writes=[("ps", ps_s)])
                P.add("act", lambda e: e.activation(out=pt, in_=pss[ps_s][:], func=AF.Exp),
                      reads=[("ps", ps_s)], writes=[("PT", 2 * par + kbi)])

        def pv(qb4):
            qblks, info = batch_info(qb4)
            par = qb4 % 2
            pn = 0 + par; pd = 2 + par
            for bi, qb in enumerate(qblks):
                prev, halo, cur = info[bi]
                for kbi in range(2):
                    kb = prev if kbi == 0 else cur
                    pt = PT4[2 * par + kbi]
                    P.add("pe", lambda e, bi=bi, kb=kb, kbi=kbi, pt=pt: e.matmul(
                        pss[pn][:, bi * 128:(bi + 1) * 128], lhsT=Vb[:, kb, :], rhs=pt[:, bi * 128:(bi + 1) * 128],
                        start=(kbi == 0), stop=(kbi == 1)),
                        reads=["Vb", ("PT", 2 * par + kbi)], writes=[("ps", pn)])
                for kbi in range(2):
                    pt = PT4[2 * par + kbi]
                    P.add("pe", lambda e, bi=bi, kbi=kbi, pt=pt: e.matmul(
                        pss[pd][:, bi * 128:(bi + 1) * 128], lhsT=ones_bf[:], rhs=pt[:, bi * 128:(bi + 1) * 128],
                        start=(kbi == 0), stop=(kbi == 1)),
                        reads=["ones", ("PT", 2 * par + kbi)], writes=[("ps", pd)])
            def canon(buf):
                if g == 2:
                    return buf[:, qb4 * 512:(qb4 + 1) * 512], (lambda ap: ap)
                if g == 1:
                    v = buf.rearrange("p (a r q) -> p r a q", a=4, r=4)[:, qb4]
                    return v, (lambda ap: ap.rearrange("p (q a) -> p a q", a=4))
                v = buf.rearrange("p (a q) -> p a q", a=16)[:, :, 32 * qb4:32 * qb4 + 32]
                return v, (lambda ap: ap.rearrange("p (q a) -> p a q", a=16))
            dn, shp = canon(num)
            dd, shp2 = canon(den)
            if g == 0:
                P.add("dve", lambda e: e.tensor_copy(out=dn, in_=shp(pss[pn][:])), reads=[("ps", pn)], writes=["num"])
                P.add("dve", lambda e: e.tensor_copy(out=dd, in_=shp2(pss[pd][:])), reads=[("ps", pd)], writes=["den"])
            else:
                P.add("dve", lambda e: e.tensor_tensor(out=dn, in0=shp(pss[pn][:]), in1=dn, op=ALU.add),
                      reads=[("ps", pn), "num"], writes=["num"])
                P.add("dve", lambda e: e.tensor_tensor(out=dd, in0=shp2(pss[pd][:]), in1=dd, op=ALU.add),
                      reads=[("ps", pd), "den"], writes=["den"])

        for qb4 in range(4):
            scores(qb4)
            if qb4 >= 1:
                pv(qb4 - 1)
        pv(3)
        if g == 2:
            P.add("act", lambda e: e.activation(out=den, in_=den, func=AF.Ln), reads=["den"], writes=["den"])
            P.add("act", lambda e: e.activation(out=den, in_=den, func=AF.Exp, scale=-1.0), reads=["den"], writes=["den"])
            P.add("dve", lambda e, h=h: e.tensor_tensor(out=oT[:, h, :], in0=num, in1=den, op=ALU.mult),
                  reads=["num", "den"], writes=["oT"])
    P.barrier()

    cvT = A[:, 8192:24576].rearrange("p (k t) -> p k t", k=KC)
    xc = A[:, 24576:28704].bitcast(F32).rearrange("p (n q) -> p n q", n=16)
    caccs = [E[:, i * 4096:(i + 1) * 4096].bitcast(F32).rearrange("p (n q) -> p n q", n=16) for i in range(2)]
    ctmp = [A[:, 28704 + i * 1024:28704 + (i + 1) * 1024].bitcast(F32) for i in range(2)]
    P.add("pool", lambda e: e.dma_start(out=wcb, in_=wcb_d.rearrange("(k p) n -> p k n", p=128)),
          writes=["wcb"], chan="wcb")
    h2h_t = h2h.rearrange("p k (n q) -> p k n q", n=16)[:, :, 14:16, 127]
    cctr = [0]

    def conv_uc(fc, s):
        wu_ = RING[s][0]; wc_ = RING[s][1]
        for tb in range(4):
            i2 = cctr[0] % 2; cctr[0] += 1
            pu = 0 + i2; pc = 2 + i2
            for k in range(KC):
                P.add("pe", lambda e, k=k: e.matmul(pss[pu][:], lhsT=wu_[:, k, :], rhs=h2o[:, k, tb * 512:(tb + 1) * 512],
                                                    start=(k == 0), stop=(k == KC - 1)),
                      reads=[("rg", s, 0), "h2o"], writes=[("ps", pu)])
            for k in range(KC):
                P.add("pe", lambda e, k=k: e.matmul(pss[pc][:], lhsT=wc_[:, k, :], rhs=h2o[:, k, tb * 512:(tb + 1) * 512],
                                                    start=(k == 0), stop=(k == KC - 1)),
                      reads=[("rg", s, 1), "h2o"], writes=[("ps", pc)])
            P.add("act", lambda e: e.activation(out=ctmp[i2], in_=pss[pc][:], func=AF.Copy),
                  reads=[("ps", pc)], writes=[("ctmp", i2)])
            P.add("dve", lambda e: e.tensor_tensor(
                out=xc[:, 4 * tb:4 * tb + 4, 1:129], in0=pss[pu][:].rearrange("p (n q) -> p n q", n=4),
                in1=ctmp[i2].rearrange("p (n q) -> p n q", n=4), op=ALU.mult),
                reads=[("ps", pu), ("ctmp", i2)], pwrites=["xc"])
        for k in range(KC):
            P.add("pe", lambda e, k=k: e.matmul(pss[6][:, 0:2], lhsT=wu_[:, k, :], rhs=h2h_t[:, k, :],
                                                start=(k == 0), stop=(k == KC - 1)),
                  reads=[("rg", s, 0), "h2h"], writes=[("ps", 6)])
        for k in range(KC):
            P.add("pe", lambda e, k=k: e.matmul(pss[7][:, 0:2], lhsT=wc_[:, k, :], rhs=h2h_t[:, k, :],
                                                start=(k == 0), stop=(k == KC - 1)),
                  reads=[("rg", s, 1), "h2h"], writes=[("ps", 7)])
        P.add("act", lambda e: e.activation(out=cols[:, 58:60], in_=pss[7][:, 0:2], func=AF.Copy),
              reads=[("ps", 7)], writes=["cols"])
        P.add("dve", lambda e: e.scalar_tensor_tensor(out=xc[:, 14:16, 0], in0=pss[6][:, 0:2], scalar=hm[:, 1:2],
                                                      in1=cols[:, 58:60], op0=ALU.mult, op1=ALU.mult),
              reads=[("ps", 6), "cols", "hm"], writes=["xc"])

    def conv_taps(fc):
        cacc = caccs[fc % 2]; ck = ("cacc", fc % 2)
        w0 = cw[:, fc, 0:1]; w1 = cw[:, fc, 1:2]; w2 = cw[:, fc, 2:3]
        P.add("dve", lambda e: e.tensor_scalar(out=cacc[:, :, :], in0=xc[:, :, 1:129], scalar1=w2, scalar2=None, op0=ALU.mult),
              reads=["xc", "cw"], writes=[ck])
        P.add("dve", lambda e: e.scalar_tensor_tensor(out=cacc[:, 1:16, :], in0=xc[:, 0:15, 1:129], scalar=w1,
                                                      in1=cacc[:, 1:16, :], op0=ALU.mult, op1=ALU.add),
              reads=["xc", "cw", ck], writes=[ck])
        P.add("dve", lambda e: e.scalar_tensor_tensor(out=cacc[:, 0, :], in0=xc[:, 15, 0:128], scalar=w1,
                                                      in1=cacc[:, 0, :], op0=ALU.mult, op1=ALU.add),
              reads=["xc", "cw", ck], writes=[ck])
        P.add("dve", lambda e: e.scalar_tensor_tensor(out=cacc[:, 2:16, :], in0=xc[:, 0:14, 1:129], scalar=w0,
                                                      in1=cacc[:, 2:16, :], op0=ALU.mult, op1=ALU.add),
              reads=["xc", "cw", ck], writes=[ck])
        P.add("dve", lambda e: e.scalar_tensor_tensor(out=cacc[:, 0:2, :], in0=xc[:, 14:16, 0:128], scalar=w0,
                                                      in1=cacc[:, 0:2, :], op0=ALU.mult, op1=ALU.add),
              reads=["xc", "cw", ck], writes=[ck])

    def conv_b(fc, s):
        wb_ = RING[s][2]
        cacc = caccs[fc % 2]; ck = ("cacc", fc % 2)
        for tb in range(4):
            pbk = 4 + tb
            for k in range(KC):
                P.add("pe", lambda e, k=k: e.matmul(pss[pbk][:], lhsT=wb_[:, k, :], rhs=h2o[:, k, tb * 512:(tb + 1) * 512],
                                                    start=(k == 0), stop=(k == KC - 1)),
                      reads=[("rg", s, 2), "h2o"], writes=[("ps", pbk)])
            P.add("dve", lambda e: e.tensor_tensor(
                out=cvT[:, fc, tb * 512:(tb + 1) * 512], in0=pss[pbk][:],
                in1=cacc[:, 4 * tb:4 * tb + 4, :].rearrange("p n q -> p (n q)"), op=ALU.mult),
                reads=[("ps", pbk), ck], pwrites=["cvT"])

    cslots = {}
    for fc in range(KC):
        cslots[fc] = rloaded.pop(0)
        conv_uc(fc, cslots[fc])
        conv_taps(fc)
        if fc >= 1:
            conv_b(fc - 1, cslots[fc - 1])
            issue_ring()
    conv_b(KC - 1, cslots[KC - 1])
    issue_ring()
    P.barrier()

    wout_st = A[:, 24576:32768].rearrange("p (k n) -> p k n", k=KC)
    P.add("pool", lambda e: e.dma_start(out=wout_st, in_=wout_d.rearrange("(k p) n -> p k n", p=128)),
          writes=["wout_st"], chan="wout")
    mT = C[:].rearrange("p (k t) -> p k t", k=KC)
    tab = [E[:, i * 1024:(i + 1) * 1024].bitcast(F32) for i in range(4)]
    m12 = [E[:, 4096 + i * 1024:4096 + (i + 1) * 1024].bitcast(F32) for i in range(4)]
    gctr = [0]
    for oc in range(KC):
        s = rloaded.pop(0)
        wga = RING[s][0]; wgc = RING[s][1]
        for tb in range(4):
            i2 = gctr[0] % 2; gctr[0] += 1
            pya, pyc, pga, pgc = 4 * i2, 4 * i2 + 1, 4 * i2 + 2, 4 * i2 + 3
            tk = tb * 512
            for hh in range(4):
                P.add("pe", lambda e, hh=hh, tk=tk, pya=pya: e.matmul(pss[pya][:], lhsT=wab[:, hh, oc * 128:(oc + 1) * 128],
                                                                       rhs=oT[:, hh, tk:tk + 512], start=(hh == 0), stop=(hh == 3)),
                      reads=["wab", "oT"], writes=[("ps", pya)])
            for k in range(KC):
                P.add("pe", lambda e, k=k, tk=tk, pyc=pyc: e.matmul(pss[pyc][:], lhsT=wcb[:, k, oc * 128:(oc + 1) * 128],
                                                                     rhs=cvT[:, k, tk:tk + 512], start=(k == 0), stop=(k == KC - 1)),
                      reads=["wcb", "cvT"], writes=[("ps", pyc)])
            for k in range(KC):
                P.add("pe", lambda e, k=k, tk=tk, pga=pga: e.matmul(pss[pga][:], lhsT=wga[:, k, :], rhs=h2o[:, k, tk:tk + 512],
                                                                     start=(k == 0), stop=(k == KC - 1)),
                      reads=[("rg", s, 0), "h2o"], writes=[("ps", pga)])
            for k in range(KC):
                P.add("pe", lambda e, k=k, tk=tk, pgc=pgc: e.matmul(pss[pgc][:], lhsT=wgc[:, k, :], rhs=h2o[:, k, tk:tk + 512],
                                                                     start=(k == 0), stop=(k == KC - 1)),
                      reads=[("rg", s, 1), "h2o"], writes=[("ps", pgc)])
            ta = tab[2 * i2]; tc_ = tab[2 * i2 + 1]; m1 = m12[2 * i2]; m2 = m12[2 * i2 + 1]
            P.add("act", lambda e, ta=ta, pga=pga: e.activation(out=ta, in_=pss[pga][:], func=AF.Tanh, scale=0.5),
                  reads=[("ps", pga)], writes=[("ta", i2)])
            P.add("act", lambda e, tc_=tc_, pgc=pgc: e.activation(out=tc_, in_=pss[pgc][:], func=AF.Tanh, scale=0.5),
                  reads=[("ps", pgc)], writes=[("tc", i2)])
            P.add("dve", lambda e, ta=ta, m1=m1, pya=pya: e.scalar_tensor_tensor(out=m1, in0=ta, scalar=1.0, in1=pss[pya][:],
                                                                                 op0=ALU.add, op1=ALU.mult),
                  reads=[("ta", i2), ("ps", pya)], writes=[("m1", i2)])
            P.add("dve", lambda e, tc_=tc_, m2=m2, pyc=pyc: e.scalar_tensor_tensor(out=m2, in0=tc_, scalar=1.0, in1=pss[pyc][:],
                                                                                   op0=ALU.add, op1=ALU.mult),
                  reads=[("tc", i2), ("ps", pyc)], writes=[("m2", i2)])
            P.add("dve", lambda e, m1=m1, m2=m2, tk=tk: e.tensor_tensor(out=mT[:, oc, tk:tk + 512], in0=m1, in1=m2, op=ALU.add),
                  reads=[("m1", i2), ("m2", i2)], pwrites=["mT"])
        issue_ring()
    P.barrier()

    for tg in range(3):
        P.add("sp", lambda e, tg=tg: e.dma_start(out=xbuf[:, 4 * tg:4 * tg + 4, :], in_=x1s_d[:, 4 * tg:4 * tg + 4, :]),
              reads=[("x1s", tg)], writes=[("x", 4 * tg + i) for i in range(4)], chan=("xg", tg))
    for k in range(KC):
        P.add("dve", lambda e, k=k: e.tensor_tensor(out=wout[:, k, :], in0=wout_st[:, k, :], in1=gtrows[:, 1, :], op=ALU.mult),
              reads=["wout_st", ("gt", 1)], pwrites=["wout"])
    P.add("sp", lambda e: e.dma_start(out=xbuf[:, 12:16, :], in_=x1s_d[:, 12:16, :]),
          reads=[("x1s", 3)], writes=[("x", 12 + i) for i in range(4)] + ["wout_st"], chan=("xg", 3))
    for (j0, nj) in PIECES:
        q_piece(f2g_d, f2u_d, f2d_d, j0, nj)
    assert piece_ctr[0] % 3 == 0
    issue_load(); issue_load()
    for n in range(NT):
        for hf in range(2):
            pa = (2 * n + hf) % 8
            for k in range(KC):
                P.add("pe", lambda e, k=k, n=n, hf=hf, pa=pa: e.matmul(pss[pa][:], lhsT=mT[:, k, n * 128:(n + 1) * 128],
                                                                        rhs=wout[:, k, hf * 512:(hf + 1) * 512],
                                                                        start=(k == 0), stop=(k == KC - 1)),
                      reads=["mT", "wout"], writes=[("ps", pa)])
            P.add("dve", lambda e, n=n, hf=hf, pa=pa: e.tensor_tensor(
                out=xbuf[:, n, hf * 512:(hf + 1) * 512], in0=pss[pa][:], in1=xbuf[:, n, hf * 512:(hf + 1) * 512], op=ALU.add),
                reads=[("ps", pa), ("x", n)], writes=[("x", n)])
    P.barrier()

    issue_load()

    def post_out(tg):
        P.add("sp", lambda e: e.dma_start(out=out_d[:, 4 * tg:4 * tg + 4, :], in_=xbuf[:, 4 * tg:4 * tg + 4, :]),
              reads=[("x", 4 * tg + i) for i in range(4)], writes=[("out", tg)], chan=("out", tg))

    ffn_pass(2, lambda tg: norm_front(tg, 64), lambda tg: norm_back(tg, 32, 40, 64), lambda tg: None, post_out, {})
    P.add("sp", None, reads=[("out", tg) for tg in range(4)])
    P.emit(nc)
    st.close()
    return nc


def _masks():
    m = np.zeros((128, 6, 128), np.float32)
    i = np.arange(128)
    mloc = [i, i, i]
    for g in range(3):
        ml = mloc[g]
        k = ml[:, None]; q = ml[None, :]
        m[:, 2 * g, :] = np.where(k >= q, 0.0, NEG)
        m[:, 2 * g + 1, :] = np.where(k <= q, 0.0, NEG)
    return m.astype(ml_dtypes.bfloat16)


def _host_inputs(inputs):
    x = np.ascontiguousarray(np.asarray(inputs["x"], dtype=np.float32))
    c = np.asarray(inputs["c"], dtype=np.float32)
    sq = lambda name: np.ascontiguousarray(np.asarray(inputs[name], dtype=np.float32)[0])
    col = lambda v: np.ascontiguousarray(v.reshape(KC, 128).T)
    shared = {
        "w_ada": sq("w_ada"), "b_ada": sq("b_ada"),
        "gcols": np.ascontiguousarray(np.concatenate([col(sq("norm_ffn1")), col(sq("norm_mix")), col(sq("norm_ffn2"))], axis=1)),
        "qkn": np.ascontiguousarray(np.stack([sq("q_norm"), sq("k_norm")], axis=1)),
        "cw": np.ascontiguousarray(sq("conv_w").reshape(3, KC, 128).transpose(2, 1, 0)),
        "bcol": np.ascontiguousarray(sq("b_ada").reshape(9 * KC, 128).T),
        "ffn1_w_gate": sq("ffn1_w_gate"), "ffn1_w_up": sq("ffn1_w_up"), "ffn1_w_down": sq("ffn1_w_down"),
        "ffn2_w_gate": sq("ffn2_w_gate"), "ffn2_w_up": sq("ffn2_w_up"), "ffn2_w_down": sq("ffn2_w_down"),
        "w_in": sq("w_in"), "w_attn_branch": sq("w_attn_branch"), "w_conv_branch": sq("w_conv_branch"),
        "w_out": sq("w_out"),
        "ident": np.eye(128).astype(ml_dtypes.bfloat16), "identf": np.eye(128, dtype=np.float32),
        "masks": _masks(),
    }
    in_maps = []
    for core in range(8):
        b, ch = core // 4, core % 4
        xo = x[b, ch * T:(ch + 1) * T].reshape(128, NT, D)
        xh = x[b, (ch - 1) * T:ch * T].reshape(128, NT, D) if ch > 0 else np.zeros((128, NT, D), np.float32)
        hm = np.zeros((128, 2), np.float32)
        hm[:, 0] = 0.0 if ch > 0 else NEG
        hm[:, 1] = 1.0 if ch > 0 else 0.0
        m = dict(shared)
        m.update({"xo": np.ascontiguousarray(xo), "xh": np.ascontiguousarray(xh), "cT": col(c[b]), "hm": hm})
        in_maps.append(m)
    return in_maps


_NC_CACHE = {}


def kernel(**inputs):
    in_maps = _host_inputs(inputs)
    if "nc" not in _NC_CACHE:
        _NC_CACHE["nc"] = build_nc()
    res = run_bass_kernel_spmd(_NC_CACHE["nc"], in_maps, core_ids=list(range(8)))
    out = np.empty((2, 4 * T, D), np.float32)
    for core in range(8):
        b, ch = core // 4, core % 4
        out[b, ch * T:(ch + 1) * T] = np.asarray(res.results[core]["out"]).reshape(T, D)
    return out
```

```python
from contextlib import ExitStack
import numpy as np
import ml_dtypes
import concourse.bass as bass
import concourse.mybir as mybir
from concourse.bass_utils import run_bass_kernel_spmd

F32 = mybir.dt.float32
BF16 = mybir.dt.bfloat16
AF = mybir.ActivationFunctionType
ALU = mybir.AluOpType
AX = mybir.AxisListType

NEG = -30000.0
EPS = 1e-6
D = 1024
KC = 8
DFF = 2816
NJ = 22
T = 2048
NT = 16
INW = 9728


class Op:
    __slots__ = ("eng", "fn", "deps", "chan", "chan_val", "needs_inc", "inc_val")

    def __init__(self, eng, fn, chan):
        self.eng = eng
        self.fn = fn
        self.chan = chan
        self.chan_val = 0
        self.deps = ()
        self.needs_inc = False
        self.inc_val = 0


class _Rec:
    def __init__(self):
        self.call = None

    def __getattr__(self, name):
        def f(*a, **kw):
            self.call = (name, a, kw)
        return f


class Prog:
    ENGS = ("pe", "act", "dve", "pool", "sp")
    BLK = {"pe": "tensor", "act": "scalar", "dve": "vector", "pool": "gpsimd", "sp": "sync"}

    def __init__(self):
        self.ops = []
        self.last_write = {}
        self.readers = {}
        self.chan_count = {}
        self.last_on_eng = {}
        self.last_on_chan = {}
        self.par_epoch = {}
        self.epoch_base = {}
        self.new_epoch = set()

    def add(self, eng, fn, reads=(), writes=(), chan=None, extra_deps=(), pwrites=()):
        if fn is not None:
            rec = _Rec()
            fn(rec)
            assert rec.call is not None
            fn = rec.call
        op = Op(eng, fn, chan)
        deps = set(extra_deps)
        for r in reads:
            deps.update(self.last_write.get(r, ()))
            if isinstance(r, tuple) and r[0] == "ps":
                deps.update(o for o in self.readers.get(r, ()) if o.eng != eng)
        for w in writes:
            deps.update(self.last_write.get(w, ()))
            deps.update(self.readers.get(w, ()))
        for w in pwrites:
            rd = self.readers.get(w, ())
            if rd or not self.par_epoch.get(w, False):
                base = set(rd) | set(self.last_write.get(w, ()))
                self.epoch_base[w] = base
                self.new_epoch.add(w)
            deps.update(self.epoch_base.get(w, ()))
        if eng == "pe":
            deps = {d for d in deps if not (d.eng == "pe" and d.chan is None)}
        deps.discard(op)
        for r in reads:
            self.readers.setdefault(r, []).append(op)
        for w in writes:
            self.last_write[w] = [op]
            self.readers[w] = []
            self.par_epoch[w] = False
        for w in pwrites:
            if w in self.new_epoch:
                self.new_epoch.discard(w)
                self.last_write[w] = [op]
                self.readers[w] = []
                self.par_epoch[w] = True
            else:
                self.last_write[w].append(op)
        if chan is not None:
            n = self.chan_count.get(chan, 0) + 1
            self.chan_count[chan] = n
            op.chan_val = 16 * n
            self.last_on_chan[chan] = op
        elif fn is not None:
            self.last_on_eng[eng] = op
        for d in deps:
            if d.chan is None:
                d.needs_inc = True
        op.deps = tuple(deps)
        self.ops.append(op)
        return op

    def barrier(self):
        deps = list(self.last_on_eng.values()) + list(self.last_on_chan.values())
        for e in self.ENGS:
            self.add(e, None, extra_deps=deps)

    def emit(self, nc):
        cnt = {e: 0 for e in self.ENGS}
        for op in self.ops:
            if op.chan is None and op.needs_inc:
                cnt[op.eng] += 1
                op.inc_val = cnt[op.eng]
        with ExitStack() as st:
            sem_eng = {e: st.enter_context(nc.semaphore("s_" + e)) for e in self.ENGS}
            sem_chan = {c: st.enter_context(nc.semaphore("c_%d" % i))
                        for i, c in enumerate(self.chan_count)}
            block = st.enter_context(nc.Block())
            for e in self.ENGS:
                ops_e = [op for op in self.ops if op.eng == e]
                if not ops_e:
                    continue

                def body(engine, ops_e=ops_e, e=e):
                    waited = {}
                    for op in ops_e:
                        need = {}
                        for d in op.deps:
                            if d.chan is not None:
                                key = ("c", d.chan)
                                s, v = sem_chan[d.chan], d.chan_val
                            else:
                                key = ("e", d.eng)
                                s, v = sem_eng[d.eng], d.inc_val
                            if need.get(key, (None, 0))[1] < v:
                                need[key] = (s, v)
                        for key, (s, v) in need.items():
                            if waited.get(key, 0) < v:
                                engine.wait_ge(s, v)
                                waited[key] = v
                        if op.fn is None:
                            continue
                        name, a, kw = op.fn
                        ins = getattr(engine, name)(*a, **kw)
                        if op.chan is not None:
                            ins.then_inc(sem_chan[op.chan], 16)
                        elif op.needs_inc:
                            ins.then_inc(sem_eng[e], 1)

                getattr(block, self.BLK[e])(body)


def build_nc(debug_stop=None):
    nc = bass.Bass("TRN2", target_bir_lowering=False)

    def din(name, shape, dt=F32):
        return nc.dram_tensor(name, list(shape), dt, kind="ExternalInput").ap()

    xo_d = din("xo", [128, NT, D])
    xh_d = din("xh", [128, NT, D])
    cT_d = din("cT", [128, KC])
    hm_d = din("hm", [128, 2])
    wada_d = din("w_ada", [D, 9 * D])
    bada_d = din("b_ada", [9 * D])
    gcol_d = din("gcols", [128, 3 * KC])
    qk_d = din("qkn", [128, 2])
    cw_d = din("cw", [128, KC, 3])
    bcol_d = din("bcol", [128, 9 * KC])
    f1g_d = din("ffn1_w_gate", [D, DFF]); f1u_d = din("ffn1_w_up", [D, DFF]); f1d_d = din("ffn1_w_down", [DFF, D])
    f2g_d = din("ffn2_w_gate", [D, DFF]); f2u_d = din("ffn2_w_up", [D, DFF]); f2d_d = din("ffn2_w_down", [DFF, D])
    win_d = din("w_in", [D, INW])
    wab_d = din("w_attn_branch", [512, D])
    wcb_d = din("w_conv_branch", [D, D])
    wout_d = din("w_out", [D, D])
    ident_d = din("ident", [128, 128], BF16)
    identf_d = din("identf", [128, 128], F32)
    masks_d = din("masks", [128, 6, 128], BF16)
    out_d = nc.dram_tensor("out", [128, NT, D], F32, kind="ExternalOutput").ap()
    x1s_d = nc.dram_tensor("x1s", [128, NT, D], F32).ap()
    h2s_d = nc.dram_tensor("h2s", [128, KC, T], BF16).ap()

    P = Prog()
    st = ExitStack()
    sb = lambda name, shape, dt: st.enter_context(nc.sbuf_tensor(name, list(shape), dt))
    A = sb("A", [128, 32768], BF16)
    B = sb("B", [128, 16384], BF16)
    C = sb("C", [128, 16384], BF16)
    Dw = sb("Dw", [128, 24576], BF16)
    E = sb("E", [128, 8192], BF16)
    ident = sb("ident_s", [128, 128], BF16)
    identf = sb("identf_s", [128, 128], F32)
    masks = sb("masks_s", [128, 8, 128], BF16)
    masksh = sb("masksh_s", [128, 3, 128], BF16)
    ones_bf = sb("ones_s", [128, 128], BF16)
    gtrows = sb("gtrows", [128, 3, D], BF16)
    cols = sb("cols", [128, 64], F32)
    gcols = sb("gcols_s", [128, 3 * KC], F32)
    qk = sb("qk_s", [128, 4], F32)
    cw = sb("cw_s", [128, KC, 3], F32)
    hm = sb("hm_s", [128, 2], F32)
    cact = sb("cact", [128, KC], F32)
    cactb = sb("cactb", [128, KC], BF16)
    bcol = sb("bcol_s", [128, 9 * KC], F32)
    crep = sb("crep", [128, KC, 128], BF16)
    ssb = sb("ssb", [128, 6 * NT], F32)
    rstd = sb("rstd", [128, 6 * NT], F32)
    mhalf = sb("mhalf", [128, NT], F32)
    pss = [st.enter_context(nc.psum_tensor("ps%d" % i, [128, 512], F32)) for i in range(8)]

    def psb(i):
        return pss[i][:].bitcast(BF16)

    xbuf = A[:].bitcast(F32).rearrange("p (n d) -> p n d", n=NT)
    hT = B[:].rearrange("p (k t) -> p k t", k=KC)

    cload = []
    def cl(dst, src, key):
        cload.append(key)
        P.add("sp", lambda e: e.dma_start(out=dst, in_=src), writes=[key], chan="const")
    cl(ident[:], ident_d, "ident"); cl(identf[:], identf_d, "identf")
    cl(masks[:, 0:6, :], masks_d, "masks"); cl(gcols[:], gcol_d, "gcols")
    cl(qk[:, 0:2], qk_d, "qk"); cl(cw[:], cw_d, "cw"); cl(hm[:], hm_d, "hm"); cl(cact[:], cT_d, "cact"); cl(bcol[:], bcol_d, "bcol")
    P.add("dve", lambda e: e.memset(ones_bf[:], 1.0), reads=cload, writes=cload + ["ones"])
    P.add("dve", lambda e: e.memset(mhalf[:], -0.5), writes=["mhalf"])
    for g in range(3):
        P.add("dve", lambda e, g=g: e.tensor_scalar(out=masksh[:, g, :], in0=masks[:, 2 * g, :],
                                                    scalar1=hm[:, 0:1], scalar2=None, op0=ALU.add),
              reads=["masks", "hm"], writes=["masksh"])
    P.add("dve", lambda e: e.tensor_scalar(out=qk[:, 2:3], in0=qk[:, 1:2], scalar1=float(128 ** -0.5),
                                           scalar2=None, op0=ALU.mult), reads=["qk"], writes=["qk"])
    P.add("act", lambda e: e.activation(out=cact[:], in_=cact[:], func=AF.Silu), reads=["cact"], writes=["cact"])
    P.add("dve", lambda e: e.tensor_copy(out=cactb[:], in_=cact[:]), reads=["cact"], writes=["cactb"])
    for k in range(KC):
        P.add("dve", lambda e, k=k: e.tensor_scalar(out=crep[:, k, :], in0=ones_bf[:], scalar1=cact[:, k:k + 1],
                                                    scalar2=None, op0=ALU.mult),
              reads=["cact", "ones"], writes=["crep"])

    def slot_views(s):
        base = Dw[:, s * 12288:(s + 1) * 12288] if s < 2 else C[:, 0:12288]
        wg = base[:, 0:4096].rearrange("p (k n) -> p k n", k=KC)
        wu = base[:, 4096:8192].rearrange("p (k n) -> p k n", k=KC)
        wd = base[:, 8192:12288].rearrange("p (j n) -> p j n", j=4)
        wa = base[:, 0:8192].rearrange("p (k n) -> p k n", k=KC)
        return wg, wu, wd, wa
    SLOTS = [slot_views(s) for s in range(3)]

    hid = [E[:, i * 2048:(i + 1) * 2048].rearrange("p (j t) -> p j t", j=4) for i in range(2)]
    sgb = [E[:, 4096 + i * 1024:4096 + (i + 1) * 1024].bitcast(F32) for i in range(2)]
    xnb = [E[:, 6144:7168], E[:, 7168:8192], C[:, 13312:14336], C[:, 14336:15360]]
    junk = C[:, 12288:13312]
    brow = C[0:1, 15360:16384]
    PIECES = [(0, 4), (4, 4), (8, 2), (10, 4), (14, 4), (18, 4)]

    pending_loads = []
    piece_ctr = [0]

    def q_piece(wg_d, wu_d, wd_d, j0, nj):
        def mk():
            idx = piece_ctr[0]; piece_ctr[0] += 1
            s = idx % 3
            wg, wu, wd, _ = SLOTS[s]
            nc_ = nj * 128
            P.add("pool", lambda e: e.dma_start(out=wg[:, :, 0:nc_],
                  in_=wg_d[:, j0 * 128:j0 * 128 + nc_].rearrange("(k p) n -> p k n", p=128)),
                  writes=[("wg", s)], chan=("wg", s))
            P.add("pool", lambda e: e.dma_start(out=wu[:, :, 0:nc_],
                  in_=wu_d[:, j0 * 128:j0 * 128 + nc_].rearrange("(k p) n -> p k n", p=128)),
                  writes=[("wu", s)], chan=("wu", s))
            P.add("pool", lambda e: e.dma_start(out=wd[:, 0:nj, :],
                  in_=wd_d[j0 * 128:j0 * 128 + nc_, :].rearrange("(j p) n -> p j n", p=128)),
                  writes=[("wd", s)], chan=("wd", s))
            return s
        pending_loads.append(mk)

    def q_mod(j):
        def mk():
            idx = piece_ctr[0]; piece_ctr[0] += 1
            s = idx % 3
            wa = SLOTS[s][3]
            P.add("pool", lambda e: e.dma_start(
                out=wa, in_=wada_d[:, j * D:(j + 1) * D].rearrange("(k p) n -> p k n", p=128)),
                writes=[("wg", s), ("wu", s)], chan=("wg", s))
            if j % 3 == 2:
                P.add("pool", lambda e: e.dma_start(out=brow, in_=bada_d[j * D:(j + 1) * D].rearrange("(o n) -> o n", o=1)),
                      writes=["brow"], chan="brow")
            return s
        pending_loads.append(mk)

    loaded = []

    def issue_load():
        if pending_loads:
            loaded.append(pending_loads.pop(0)())

    def mod_block(j, issue=True):
        s = loaded.pop(0)
        wa = SLOTS[s][3]
        role, li = j % 3, j // 3
        if role == 2:
            for hf in range(2):
                pb = 6 + hf
                for k in range(KC):
                    P.add("pe", lambda e, k=k: e.matmul(pss[pb][:], lhsT=crep[:, k, :], rhs=wa[:, k, hf * 512:(hf + 1) * 512],
                                                        start=(k == 0), stop=False),
                          reads=["crep", ("wg", s), ("wu", s)], writes=[("ps", pb)])
                P.add("pe", lambda e: e.matmul(pss[pb][:], lhsT=ones_bf[0:1, :], rhs=brow[:, hf * 512:(hf + 1) * 512],
                                               start=False, stop=True),
                      reads=["ones", "brow"], writes=[("ps", pb)])
                P.add("dve", lambda e: e.tensor_scalar(out=gtrows[:, li, hf * 512:(hf + 1) * 512], in0=pss[pb][:], scalar1=0.5,
                                                       scalar2=None, op0=ALU.mult),
                      reads=[("ps", pb)], writes=[("gt", li)])
        else:
            pb = 7
            for kc in range(KC):
                for k in range(KC):
                    P.add("pe", lambda e, k=k, kc=kc: e.matmul(pss[pb][:, kc:kc + 1], lhsT=wa[:, k, kc * 128:(kc + 1) * 128],
                                                               rhs=cactb[:, k:k + 1], start=(k == 0), stop=(k == KC - 1)),
                          reads=["cactb", ("wg", s), ("wu", s)], writes=[("ps", pb)])
            if role == 0:
                P.add("dve", lambda e: e.tensor_tensor(out=cols[:, 16 * li + 8:16 * li + 16], in0=pss[pb][:, 0:KC],
                                                       in1=bcol[:, j * KC:(j + 1) * KC], op=ALU.add),
                      reads=[("ps", pb), "bcol"], writes=["cols"])
            else:
                P.add("dve", lambda e: e.tensor_tensor(out=cols[:, 48:56], in0=pss[pb][:, 0:KC],
                                                       in1=bcol[:, j * KC:(j + 1) * KC], op=ALU.add),
                      reads=[("ps", pb), "bcol"], writes=["cols"])
                P.add("dve", lambda e: e.scalar_tensor_tensor(
                    out=cols[:, 16 * li:16 * li + 8], in0=cols[:, 48:56], scalar=1.0, in1=gcols[:, li * KC:(li + 1) * KC],
                    op0=ALU.add, op1=ALU.mult), reads=["cols", "gcols"], writes=["cols"])
        if issue:
            issue_load()

    def load_x(src_d, tg, after=()):
        P.add("sp", lambda e: e.dma_start(out=xbuf[:, 4 * tg:4 * tg + 4, :], in_=src_d[:, 4 * tg:4 * tg + 4, :]),
              reads=list(after), writes=[("x", 4 * tg + i) for i in range(4)], chan=("xg", tg))

    tctr = [0]

    ng_state = {}

    def norm_front(tg, off=0):
        c0 = off + 4 * tg
        for i in range(4):
            n = 4 * tg + i
            P.add("act", lambda e, n=n, i=i: e.activation(out=junk, in_=xbuf[:, n, :], func=AF.Square,
                                                          accum_out=ssb[:, c0 + i:c0 + i + 1]),
                  reads=[("x", n), "ssb"], writes=[("ss", off + n), "junk"])
        P.add("dve", lambda e: e.tensor_scalar(out=rstd[:, c0:c0 + 4], in0=ssb[:, c0:c0 + 4],
                                               scalar1=1.0 / D, scalar2=EPS, op0=ALU.mult, op1=ALU.add),
              reads=["ssb"] + [("ss", off + 4 * tg + i) for i in range(4)], writes=[("rs", off + tg)])
        P.add("pool", lambda e: e.tensor_tensor(out=rstd[:, c0:c0 + 4], in0=rstd[:, c0:c0 + 4],
                                                in1=mhalf[:, 0:4], op=ALU.pow),
              reads=[("rs", off + tg), "mhalf"], writes=[("rs", off + tg)])
        tiles = []
        for i in range(4):
            n = 4 * tg + i
            xi = tctr[0] % 4
            pb = 6 + (tctr[0] % 2); tctr[0] += 1
            xb = xnb[xi]
            tiles.append((n, i, xi, pb, xb))
            P.add("pool", lambda e, n=n, i=i, xb=xb: e.tensor_scalar(out=xb, in0=xbuf[:, n, :], scalar1=rstd[:, c0 + i:c0 + i + 1],
                                                                     scalar2=0.0, op0=ALU.mult, op1=ALU.add),
                  reads=[("x", n), ("rs", off + tg)], writes=[("xn", xi)])
        ng_state[(off, tg)] = tiles

    def norm_back(tg, acol, scol, off=0):
        tiles = ng_state.pop((off, tg))

        def transp(t):
            n, i, xi, pb, xb = t
            for c in range(KC):
                P.add("pe", lambda e, c=c: e.transpose(
                    out=psb(pb)[:, c * 128:(c + 1) * 128], in_=xb[:, c * 128:(c + 1) * 128], identity=ident[:]),
                    reads=[("xn", xi), "ident"], writes=[("ps", pb)])

        def evac(t):
            n, i, xi, pb, xb = t
            for c in range(KC):
                if pb == 6:
                    P.add("dve", lambda e, c=c: e.tensor_scalar(
                        out=hT[:, c, n * 128:(n + 1) * 128], in0=psb(pb)[:, c * 128:(c + 1) * 128],
                        scalar1=cols[:, acol + c:acol + c + 1], scalar2=cols[:, scol + c:scol + c + 1],
                        op0=ALU.mult, op1=ALU.add), reads=[("ps", pb), "cols"], pwrites=[("hT", n)])
                else:
                    P.add("act", lambda e, c=c: e.activation(
                        out=hT[:, c, n * 128:(n + 1) * 128], in_=psb(pb)[:, c * 128:(c + 1) * 128],
                        func=AF.Identity, scale=cols[:, acol + c:acol + c + 1], bias=cols[:, scol + c:scol + c + 1]),
                        reads=[("ps", pb), "cols"], pwrites=[("hT", n)])

        transp(tiles[0]); transp(tiles[1])
        evac(tiles[0]); transp(tiles[2])
        evac(tiles[1]); transp(tiles[3])
        evac(tiles[2]); evac(tiles[3])

    mctr = [0]
    actr = [0]

    def unit_gu(s, nj, tg, hb, js):
        wg, wu, wd, _ = SLOTS[s]
        hd = hid[hb]
        for j in js:
            pg = j % 2
            for k in range(KC):
                P.add("pe", lambda e, j=j, k=k: e.matmul(
                    pss[pg][:], lhsT=wg[:, k, j * 128:(j + 1) * 128], rhs=hT[:, k, tg * 512:(tg + 1) * 512],
                    start=(k == 0), stop=(k == KC - 1)),
                    reads=[("wg", s)] + [("hT", 4 * tg + i) for i in range(4)], writes=[("ps", pg)])
            for k in range(KC):
                P.add("pe", lambda e, j=j, k=k: e.matmul(
                    pss[2 + pg][:], lhsT=wu[:, k, j * 128:(j + 1) * 128], rhs=hT[:, k, tg * 512:(tg + 1) * 512],
                    start=(k == 0), stop=(k == KC - 1)),
                    reads=[("wu", s)] + [("hT", 4 * tg + i) for i in range(4)], writes=[("ps", 2 + pg)])
            P.add("act", lambda e: e.activation(out=sgb[pg], in_=pss[pg][:], func=AF.Silu),
                  reads=[("ps", pg)], writes=[("sg", pg)])
            P.add("dve", lambda e, j=j: e.tensor_tensor(out=hd[:, j, :], in0=pss[2 + pg][:], in1=sgb[pg], op=ALU.mult),
                  reads=[("ps", 2 + pg), ("sg", pg)], pwrites=[("hid", hb)])

    def unit_down(s, nj, tg, hb):
        wg, wu, wd, _ = SLOTS[s]
        hd = hid[hb]
        for t in range(4):
            n = 4 * tg + t
            for hf in range(2):
                pa = 4 + (actr[0] % 4); actr[0] += 1
                for j in range(nj):
                    P.add("pe", lambda e, j=j: e.matmul(
                        pss[pa][:], lhsT=hd[:, j, t * 128:(t + 1) * 128], rhs=wd[:, j, hf * 512:(hf + 1) * 512],
                        start=(j == 0), stop=(j == nj - 1)),
                        reads=[("hid", hb), ("wd", s)], writes=[("ps", pa)])
                P.add("dve", lambda e: e.tensor_tensor(
                    out=xbuf[:, n, hf * 512:(hf + 1) * 512], in0=pss[pa][:],
                    in1=xbuf[:, n, hf * 512:(hf + 1) * 512], op=ALU.add),
                    reads=[("ps", pa), ("x", n)], writes=[("x", n)])

    def new_hb():
        hb = mctr[0] % 2; mctr[0] += 1
        return hb

    def ffn_unit(s, nj, tg, mid=None):
        hb = new_hb()
        unit_gu(s, nj, tg, hb, range(nj))
        if mid is not None:
            mid()
        unit_down(s, nj, tg, hb)

    def ffn_pass(gi, pre_f, pre_b, post_f, post_b, extras, first_mid=None, nxt=None, head=None):
        if head is None:
            pre_f(0); pre_b(0); pre_f(1)
        deferred = None
        for q, (j0, nj) in enumerate(PIECES):
            s = loaded.pop(0)
            wd = SLOTS[s][2]

            def scale_wd(s=s, wd=wd, nj=nj):
                for j in range(nj):
                    P.add("dve", lambda e, j=j: e.tensor_tensor(out=wd[:, j, :], in0=wd[:, j, :], in1=gtrows[:, gi, :], op=ALU.mult),
                          reads=[("wd", s), ("gt", gi)], writes=[("wd", s)])
            mid0 = None
            if q == 0 and first_mid is not None:
                def mid0(scale_wd=scale_wd):
                    first_mid()
                    scale_wd()
            else:
                scale_wd()
            last = (q == len(PIECES) - 1)
            hbs = [None] * 4
            for tg in range(4):
                if q == 0:
                    ffn_unit(s, nj, tg, mid=(mid0 if tg == 0 else None))
                else:
                    if tg == 0:
                        hbs[0] = new_hb()
                        unit_gu(s, nj, 0, hbs[0], range(nj))
                    if tg < 3:
                        hbs[tg + 1] = new_hb()
                        unit_gu(s, nj, tg + 1, hbs[tg + 1], [0])
                    unit_down(s, nj, tg, hbs[tg])
                    if tg < 3:
                        unit_gu(s, nj, tg + 1, hbs[tg + 1], range(1, nj))
                if q == 0:
                    if tg == 0 and head is not None:
                        head()
                    if tg + 1 < 4:
                        pre_b(tg + 1)
                    if tg + 2 < 4:
                        pre_f(tg + 2)
                if last:
                    if tg >= 1:
                        post_b(tg - 1)
                    post_f(tg)
                    if nxt is not None and tg == 2:
                        nxt[0](0)
                    if nxt is not None and tg == 3:
                        nxt[1](0)
                        nxt[0](1)
            issue_load()
            if q == 0 and first_mid is not None:
                issue_load()
            for fn in extras.get(q, ()):
                fn()
        if nxt is not None:
            return lambda: post_b(3)
        post_b(3)
        return None

    def ring(s):
        base = Dw[:, s * 3072:(s + 1) * 3072]
        return [base[:, i * 1024:(i + 1) * 1024].rearrange("p (k n) -> p k n", k=KC) for i in range(3)]
    RING = [ring(s) for s in range(3)]
    wab = Dw[:, 9216:13312].rearrange("p (h n) -> p h n", h=4)
    wcb = Dw[:, 13312:21504].rearrange("p (k n) -> p k n", k=KC)
    wout = E[:, 0:8192].rearrange("p (k n) -> p k n", k=KC)

    win_v = win_d.rearrange("(k p) n -> p k n", p=128)
    rctr = [0]
    ring_extra = []
    rloads = []

    def queue_ring(colsets):
        def mk():
            s = rctr[0] % 3; rctr[0] += 1
            for i, c0 in enumerate(colsets):
                P.add("pool", lambda e, i=i, c0=c0, s=s: e.dma_start(out=RING[s][i], in_=win_v[:, :, c0:c0 + 128]),
                      writes=[("rg", s, i)] + ring_extra, chan=("rg", s, i))
            return s
        rloads.append(mk)
    rloaded = []

    def issue_ring():
        if rloads:
            rloaded.append(rloads.pop(0)())

    order = [(g, h) for h in range(4) for g in range(3)]
    for (g, h) in order:
        idx = g * 4 + h
        queue_ring([idx * 128, 1536 + idx * 128, 3072 + idx * 128])
    for fc in range(KC):
        queue_ring([4608 + fc * 128, 6656 + fc * 128, 5632 + fc * 128])
    for oc in range(KC):
        queue_ring([7680 + oc * 128, 8704 + oc * 128])

    F1 = (f1g_d, f1u_d, f1d_d)
    for j in (0, 1):
        q_mod(j)
    for q, (j0, nj) in enumerate(PIECES):
        q_piece(*F1, j0, nj)
        if q <= 2:
            q_mod(2 + q)
    for q, (j0, nj) in enumerate(PIECES):
        q_piece(*F1, j0, nj)
        if q < 4:
            q_mod(5 + q)

    load_x(xh_d, 0)
    for _ in range(3):
        issue_load()
    for tg in range(1, 4):
        load_x(xh_d, tg, after=[("wg", 0)])
    P.add("dve", lambda e: e.memset(ssb[:], 0.0), writes=["ssb"])
    for j in (0, 1):
        mod_block(j)

    def post_halo_b(tg):
        norm_back(tg, 16, 24, 16)
        P.add("sp", lambda e: e.dma_start(out=h2s_d[:, :, tg * 512:(tg + 1) * 512], in_=hT[:, :, tg * 512:(tg + 1) * 512]),
              reads=[("hT", 4 * tg + i) for i in range(4)], writes=[("h2s", tg)], chan=("h2s", tg))
        load_x(xo_d, tg)

    own_pre = (lambda tg: norm_front(tg, 32), lambda tg: norm_back(tg, 0, 8, 32))
    tail = ffn_pass(0, lambda tg: norm_front(tg, 0), lambda tg: norm_back(tg, 0, 8, 0),
                    lambda tg: norm_front(tg, 16), post_halo_b,
                    {1: [lambda: mod_block(3)], 2: [lambda: mod_block(4)]},
                    first_mid=lambda: mod_block(2, issue=False))

    def early_ring():
        ring_extra.extend([("wg", 0), ("wu", 0), ("wd", 0)])
        for _ in range(3):
            issue_ring()
        del ring_extra[:]

    def post_own_f(tg):
        P.add("sp", lambda e: e.dma_start(out=x1s_d[:, 4 * tg:4 * tg + 4, :], in_=xbuf[:, 4 * tg:4 * tg + 4, :]),
              reads=[("x", 4 * tg + i) for i in range(4)], writes=[("x1s", tg)], chan=("x1s", tg))
        norm_front(tg, 48)

    ffn_pass(0, own_pre[0], own_pre[1],
             post_own_f, lambda tg: norm_back(tg, 16, 24, 48),
             {0: [lambda: mod_block(5)], 1: [lambda: mod_block(6)], 2: [lambda: mod_block(7)], 3: [lambda: mod_block(8), early_ring]},
             head=None)
    P.barrier()

    h2o = hT
    h2h = C[:].rearrange("p (k t) -> p k t", k=KC)
    for tg in range(4):
        P.add("sp", lambda e, tg=tg: e.dma_start(out=h2h[:, :, tg * 512:(tg + 1) * 512], in_=h2s_d[:, :, tg * 512:(tg + 1) * 512]),
              reads=[("h2s", tg)], pwrites=["h2h"], chan=("h2h", tg))
    oT = A[:, 0:8192].rearrange("p (h t) -> p h t", h=4)
    QTg = A[:, 8192:10240].rearrange("p (b i) -> p b i", b=16)
    KTg = A[:, 10240:14336].rearrange("p (b i) -> p b i", b=32)
    VTg = A[:, 14336:18432].rearrange("p (b i) -> p b i", b=32)
    Vb = A[:, 18432:22528].rearrange("p (b i) -> p b i", b=32)
    num = A[:, 22528:26624].bitcast(F32)
    den = A[:, 26624:30720].bitcast(F32)
    PT = [A[:, 30720 + i * 512:30720 + (i + 1) * 512] for i in range(2)]
    PT4 = PT + [E[:, 5120:5632], E[:, 5632:6144]]
    sqb = [A[:, 31744 + i * 512:31744 + (i + 1) * 512] for i in range(2)]
    rrb = [E[:, i * 1024:(i + 1) * 1024].bitcast(F32) for i in range(2)]

    P.add("pool", lambda e: e.dma_start(out=wab, in_=wab_d.rearrange("(h p) n -> p h n", p=128)),
          writes=["wab"], chan="wab")

    def dst_src(buf, base_blk, g, tb):
        flat = buf.rearrange("p b i -> p (b i)")
        o = base_blk * 128
        if g == 2:
            return flat[:, o + tb * 512:o + (tb + 1) * 512], (lambda ap: ap)
        if g == 1:
            dst = flat[:, o:o + 2048].rearrange("p (r q a) -> p r q a", r=4, a=4)[:, :, :, tb]
            return dst, (lambda ap: ap.rearrange("p (r q) -> p r q", r=4))
        dst = flat[:, o:o + 2048].rearrange("p (q a) -> p a q", a=16)[:, 4 * tb:4 * tb + 4, :]
        return dst, (lambda ap: ap.rearrange("p (a q) -> p a q", a=4))

    pctr = [0]
    qctr = [0]
    tctr2 = [0]
    ptmp = [E[:, 2048 + i * 512:2048 + (i + 1) * 512] for i in range(6)]
    pending_fin = []

    def proj_block(w, rhs_fn, ncols, kind, gcol, dst, shape_fn, contig, src="h2o"):
        pb = (0, 1, 2, 5)[pctr[0] % 4]; pctr[0] += 1
        for k in range(KC):
            P.add("pe", lambda e, k=k: e.matmul(pss[pb][:, 0:ncols], lhsT=w[0][:, k, :], rhs=rhs_fn(k),
                                                start=(k == 0), stop=(k == KC - 1)),
                  reads=[w[1], src], writes=[("ps", pb)])
        key = {"q": "QTg", "k": "KTg", "v": "VTg"}[kind]

        def store(src_fn, reads, eng_kind):
            if contig:
                out_ap, wkey, pw = dst, None, [key]
            else:
                ti = tctr2[0] % 6; tctr2[0] += 1
                out_ap, wkey, pw = ptmp[ti][:, 0:ncols], ("ptmp", ti), []
            src_fn(out_ap if contig else out_ap, reads, [wkey] if wkey else [], pw)
            if not contig:
                P.add("pool", lambda e: e.tensor_copy(out=dst, in_=shape_fn(ptmp[ti][:, 0:ncols])),
                      reads=[("ptmp", ti)], pwrites=[key])

        if kind == "v":
            def src(out_ap, reads, wr, pw):
                P.add("dve", lambda e: e.tensor_copy(out=out_ap, in_=pss[pb][:, 0:ncols]),
                      reads=[("ps", pb)], writes=wr, pwrites=pw)
            store(src, None, None)
            while pending_fin:
                pending_fin.pop(0)()
            return
        i2 = qctr[0] % 2; qctr[0] += 1
        sq = sqb[i2]; rr = rrb[i2]
        P.add("act", lambda e: e.activation(out=sq[:, 0:ncols], in_=pss[pb][:, 0:ncols], func=AF.Square),
              reads=[("ps", pb)], writes=[("sq", i2)])

        def fin():
            P.add("pe", lambda e: e.matmul(pss[3 + i2][:, 0:ncols], lhsT=ones_bf[:], rhs=sq[:, 0:ncols], start=True, stop=True),
                  reads=[("sq", i2), "ones"], writes=[("ps", 3 + i2)])
            P.add("act", lambda e: e.activation(out=rr[:, 0:ncols], in_=pss[3 + i2][:, 0:ncols], func=AF.Ln,
                                                scale=(1.0 if kind == "q" else 1.0 / 128),
                                                bias=(cols[:, 57:58] if kind == "q" else cols[:, 56:57])),
                  reads=[("ps", 3 + i2), "cols"], writes=[("rr", i2)])
            P.add("act", lambda e: e.activation(out=rr[:, 0:ncols], in_=rr[:, 0:ncols], func=AF.Exp, scale=-0.5),
                  reads=[("rr", i2)], writes=[("rr", i2)])

            def src(out_ap, reads, wr, pw):
                P.add("dve", lambda e: e.scalar_tensor_tensor(out=out_ap, in0=pss[pb][:, 0:ncols], scalar=gcol,
                                                              in1=rr[:, 0:ncols], op0=ALU.mult, op1=ALU.mult),
                      reads=[("ps", pb), ("rr", i2), "qk"], writes=wr, pwrites=pw)
            store(src, None, None)
        prev = list(pending_fin)
        del pending_fin[:]
        pending_fin.append(fin)
        for f in prev:
            f()

    def flush_fin():
        while pending_fin:
            pending_fin.pop(0)()

    P.add("dve", lambda e: e.memset(cols[:, 56:57], EPS), writes=["cols"])
    P.add("dve", lambda e: e.memset(cols[:, 57:58], 128 * EPS), writes=["cols"])

    h2h4 = h2h.rearrange("p k (n q) -> p k n q", n=16)
    sctr = [0]
    for (g, h) in order:
        s = rloaded.pop(0)
        wq = (RING[s][0], ("rg", s, 0)); wk = (RING[s][1], ("rg", s, 1)); wv = (RING[s][2], ("rg", s, 2))
        for tb in range(4):
            rf = lambda k, tb=tb: h2o[:, k, tb * 512:(tb + 1) * 512]
            d_, sf = dst_src(VTg, 16, g, tb)
            proj_block(wv, rf, 512, "v", None, d_, sf, g == 2)
            d_, sf = dst_src(QTg, 0, g, tb)
            proj_block(wq, rf, 512, "q", qk[:, 0:1], d_, sf, g == 2)
            d_, sf = dst_src(KTg, 16, g, tb)
            proj_block(wk, rf, 512, "k", qk[:, 1:2], d_, sf, g == 2)
        if g == 2:
            for tb in range(4):
                rf = lambda k, tb=tb: h2h[:, k, tb * 512:(tb + 1) * 512]
                d_, sf = dst_src(VTg, 0, 2, tb)
                proj_block(wv, rf, 512, "v", None, d_, sf, True, src="h2h")
            for tb in range(4):
                rf = lambda k, tb=tb: h2h[:, k, tb * 512:(tb + 1) * 512]
                d_, sf = dst_src(KTg, 0, 2, tb)
                proj_block(wk, rf, 512, "k", qk[:, 1:2], d_, sf, True, src="h2h")
            nhalo = 16
        elif g == 1:
            rf = lambda k: h2h4[:, k, :, 96:128]
            sf = lambda ap: ap.rearrange("p (a rb) -> p a rb", a=4)
            dv = lambda buf: buf.rearrange("p b i -> p (b i)")[:, 0:512].rearrange("p (rb a) -> p a rb", a=4)
            proj_block(wv, rf, 512, "v", None, dv(VTg), sf, False, src="h2h")
            proj_block(wk, rf, 512, "k", qk[:, 1:2], dv(KTg), sf, False, src="h2h")
            nhalo = 4
        else:
            rf = lambda k: h2h4[:, k, :, 120:128]
            sf = lambda ap: ap.rearrange("p (a b) -> p a b", a=16)
            dv = lambda buf: buf[:, 0, :].rearrange("p (b a) -> p a b", a=16)
            proj_block(wv, rf, 128, "v", None, dv(VTg), sf, False, src="h2h")
            proj_block(wk, rf, 128, "k", qk[:, 1:2], dv(KTg), sf, False, src="h2h")
            nhalo = 1
        flush_fin()
        issue_ring()
        grps = [list(range(i0, min(i0 + 8, nhalo))) for i0 in range(0, nhalo, 8)] + [list(range(16, 24)), list(range(24, 32))]
        for grp in grps:
            pb = 6 + (sctr[0] % 2); sctr[0] += 1
            for ii, blk in enumerate(grp):
                P.add("pe", lambda e, ii=ii, blk=blk, pb=pb: e.transpose(
                    out=psb(pb)[:, ii * 128:(ii + 1) * 128], in_=VTg[:, blk, :], identity=ident[:]),
                    reads=["VTg", "ident"], writes=[("ps", pb)])
            b0 = grp[0]; nb = len(grp)
            P.add("dve", lambda e, b0=b0, nb=nb, pb=pb: e.tensor_copy(
                out=Vb[:, b0:b0 + nb, :], in_=psb(pb)[:, 0:nb * 128].rearrange("p (b i) -> p b i", b=nb)),
                reads=[("ps", pb)], pwrites=["Vb"])
        def batch_info(qb4):
            qblks = [4 * qb4 + i for i in range(4)]
            info = []
            for qb in qblks:
                cur = 16 + qb
                if g == 2:
                    prev, halo = qb, True
                elif g == 1:
                    r4, n1 = qb // 4, qb % 4
                    prev, halo = (r4, True) if n1 == 0 else (16 + qb - 1, False)
                else:
                    prev, halo = (0, True) if qb == 0 else (16 + qb - 1, False)
                info.append((prev, halo, cur))
            return qblks, info

        def scores(qb4):
            qblks, info = batch_info(qb4)
            par = qb4 % 2
            for kbi in range(2):
                ps_s = 4 + 2 * par + kbi
                pt = PT4[2 * par + kbi]
                for bi, qb in enumerate(qblks):
                    prev, halo, cur = info[bi]
                    kb = prev if kbi == 0 else cur
                    if kbi == 0:
                        mk_ap = masksh[:, g, :] if halo else masks[:, 2 * g, :]
                    else:
                        mk_ap = masks[:, 2 * g + 1, :]
                    P.add("pe", lambda e, bi=bi, qb=qb, kb=kb: e.matmul(
                        pss[ps_s][:, bi * 128:(bi + 1) * 128], lhsT=KTg[:, kb, :], rhs=QTg[:, qb, :], start=True, stop=False),
                        reads=["KTg", "QTg"], writes=[("ps", ps_s)])
                    P.add("pe", lambda e, bi=bi, mk_ap=mk_ap: e.matmul(
                        pss[ps_s][:, bi * 128:(bi + 1) * 128], lhsT=ident[:], rhs=mk_ap, start=False, stop=True),
                        reads=["ident", "masks", "masksh"], writes=[("ps", ps_s)])
                P.add("act", lambda e: e.activation(out=pt, in_=pss[ps_s][:], func=AF.Exp),
                      reads=[("ps", ps_s)], writes=[("PT", 2 * par + kbi)])

        def pv(qb4):
            qblks, info = batch_info(qb4)
            par = qb4 % 2
            pn = 0 + par; pd = 2 + par
            for bi, qb in enumerate(qblks):
                prev, halo, cur = info[bi]
                for kbi in range(2):
                    kb = prev if kbi == 0 else cur
                    pt = PT4[2 * par + kbi]
                    P.add("pe", lambda e, bi=bi, kb=kb, kbi=kbi, pt=pt: e.matmul(
                        pss[pn][:, bi * 128:(bi + 1) * 128], lhsT=Vb[:, kb, :], rhs=pt[:, bi * 128:(bi + 1) * 128],
                        start=(kbi == 0), stop=(kbi == 1)),
                        reads=["Vb", ("PT", 2 * par + kbi)], writes=[("ps", pn)])
                for kbi in range(2):
                    pt = PT4[2 * par + kbi]
                    P.add("pe", lambda e, bi=bi, kbi=kbi, pt=pt: e.matmul(
                        pss[pd][:, bi * 128:(bi + 1) * 128], lhsT=ones_bf[:], rhs=pt[:, bi * 128:(bi + 1) * 128],
                        start=(kbi == 0), stop=(kbi == 1)),
                        reads=["ones", ("PT", 2 * par + kbi)], writes=[("ps", pd)])
            def canon(buf):
                if g == 2:
                    return buf[:, qb4 * 512:(qb4 + 1) * 512], (lambda ap: ap)
                if g == 1:
                    v = buf.rearrange("p (a r q) -> p r a q", a=4, r=4)[:, qb4]
                    return v, (lambda ap: ap.rearrange("p (q a) -> p a q", a=4))
                v = buf.rearrange("p (a q) -> p a q", a=16)[:, :, 32 * qb4:32 * qb4 + 32]
                return v, (lambda ap: ap.rearrange("p (q a) -> p a q", a=16))
            dn, shp = canon(num)
            dd, shp2 = canon(den)
            if g == 0:
                P.add("dve", lambda e: e.tensor_copy(out=dn, in_=shp(pss[pn][:])), reads=[("ps", pn)], writes=["num"])
                P.add("dve", lambda e: e.tensor_copy(out=dd, in_=shp2(pss[pd][:])), reads=[("ps", pd)], writes=["den"])
            else:
                P.add("dve", lambda e: e.tensor_tensor(out=dn, in0=shp(pss[pn][:]), in1=dn, op=ALU.add),
                      reads=[("ps", pn), "num"], writes=["num"])
                P.add("dve", lambda e: e.tensor_tensor(out=dd, in0=shp2(pss[pd][:]), in1=dd, op=ALU.add),
                      reads=[("ps", pd), "den"], writes=["den"])

        for qb4 in range(4):
            scores(qb4)
            if qb4 >= 1:
                pv(qb4 - 1)
        pv(3)
        if g == 2:
            P.add("act", lambda e: e.activation(out=den, in_=den, func=AF.Ln), reads=["den"], writes=["den"])
            P.add("act", lambda e: e.activation(out=den, in_=den, func=AF.Exp, scale=-1.0), reads=["den"], writes=["den"])
            P.add("dve", lambda e, h=h: e.tensor_tensor(out=oT[:, h, :], in0=num, in1=den, op=ALU.mult),
                  reads=["num", "den"], writes=["oT"])
    P.barrier()

    cvT = A[:, 8192:24576].rearrange("p (k t) -> p k t", k=KC)
    xc = A[:, 24576:28704].bitcast(F32).rearrange("p (n q) -> p n q", n=16)
    caccs = [E[:, i * 4096:(i + 1) * 4096].bitcast(F32).rearrange("p (n q) -> p n q", n=16) for i in range(2)]
    ctmp = [A[:, 28704 + i * 1024:28704 + (i + 1) * 1024].bitcast(F32) for i in range(2)]
    P.add("pool", lambda e: e.dma_start(out=wcb, in_=wcb_d.rearrange("(k p) n -> p k n", p=128)),
          writes=["wcb"], chan="wcb")
    h2h_t = h2h.rearrange("p k (n q) -> p k n q", n=16)[:, :, 14:16, 127]
    cctr = [0]

    def conv_uc(fc, s):
        wu_ = RING[s][0]; wc_ = RING[s][1]
        for tb in range(4):
            i2 = cctr[0] % 2; cctr[0] += 1
            pu = 0 + i2; pc = 2 + i2
            for k in range(KC):
                P.add("pe", lambda e, k=k: e.matmul(pss[pu][:], lhsT=wu_[:, k, :], rhs=h2o[:, k, tb * 512:(tb + 1) * 512],
                                                    start=(k == 0), stop=(k == KC - 1)),
                      reads=[("rg", s, 0), "h2o"], writes=[("ps", pu)])
            for k in range(KC):
                P.add("pe", lambda e, k=k: e.matmul(pss[pc][:], lhsT=wc_[:, k, :], rhs=h2o[:, k, tb * 512:(tb + 1) * 512],
                                                    start=(k == 0), stop=(k == KC - 1)),
                      reads=[("rg", s, 1), "h2o"], writes=[("ps", pc)])
            P.add("act", lambda e: e.activation(out=ctmp[i2], in_=pss[pc][:], func=AF.Copy),
                  reads=[("ps", pc)], writes=[("ctmp", i2)])
            P.add("dve", lambda e: e.tensor_tensor(
                out=xc[:, 4 * tb:4 * tb + 4, 1:129], in0=pss[pu][:].rearrange("p (n q) -> p n q", n=4),
                in1=ctmp[i2].rearrange("p (n q) -> p n q", n=4), op=ALU.mult),
                reads=[("ps", pu), ("ctmp", i2)], pwrites=["xc"])
        for k in range(KC):
            P.add("pe", lambda e, k=k: e.matmul(pss[6][:, 0:2], lhsT=wu_[:, k, :], rhs=h2h_t[:, k, :],
                                                start=(k == 0), stop=(k == KC - 1)),
                  reads=[("rg", s, 0), "h2h"], writes=[("ps", 6)])
        for k in range(KC):
            P.add("pe", lambda e, k=k: e.matmul(pss[7][:, 0:2], lhsT=wc_[:, k, :], rhs=h2h_t[:, k, :],
                                                start=(k == 0), stop=(k == KC - 1)),
                  reads=[("rg", s, 1), "h2h"], writes=[("ps", 7)])
        P.add("act", lambda e: e.activation(out=cols[:, 58:60], in_=pss[7][:, 0:2], func=AF.Copy),
              reads=[("ps", 7)], writes=["cols"])
        P.add("dve", lambda e: e.scalar_tensor_tensor(out=xc[:, 14:16, 0], in0=pss[6][:, 0:2], scalar=hm[:, 1:2],
                                                      in1=cols[:, 58:60], op0=ALU.mult, op1=ALU.mult),
              reads=[("ps", 6), "cols", "hm"], writes=["xc"])

    def conv_taps(fc):
        cacc = caccs[fc % 2]; ck = ("cacc", fc % 2)
        w0 = cw[:, fc, 0:1]; w1 = cw[:, fc, 1:2]; w2 = cw[:, fc, 2:3]
        P.add("dve", lambda e: e.tensor_scalar(out=cacc[:, :, :], in0=xc[:, :, 1:129], scalar1=w2, scalar2=None, op0=ALU.mult),
              reads=["xc", "cw"], writes=[ck])
        P.add("dve", lambda e: e.scalar_tensor_tensor(out=cacc[:, 1:16, :], in0=xc[:, 0:15, 1:129], scalar=w1,
                                                      in1=cacc[:, 1:16, :], op0=ALU.mult, op1=ALU.add),
              reads=["xc", "cw", ck], writes=[ck])
        P.add("dve", lambda e: e.scalar_tensor_tensor(out=cacc[:, 0, :], in0=xc[:, 15, 0:128], scalar=w1,
                                                      in1=cacc[:, 0, :], op0=ALU.mult, op1=ALU.add),
              reads=["xc", "cw", ck], writes=[ck])
        P.add("dve", lambda e: e.scalar_tensor_tensor(out=cacc[:, 2:16, :], in0=xc[:, 0:14, 1:129], scalar=w0,
                                                      in1=cacc[:, 2:16, :], op0=ALU.mult, op1=ALU.add),
              reads=["xc", "cw", ck], writes=[ck])
        P.add("dve", lambda e: e.scalar_tensor_tensor(out=cacc[:, 0:2, :], in0=xc[:, 14:16, 0:128], scalar=w0,
                                                      in1=cacc[:, 0:2, :], op0=ALU.mult, op1=ALU.add),
              reads=["xc", "cw", ck], writes=[ck])

    def conv_b(fc, s):
        wb_ = RING[s][2]
        cacc = caccs[fc % 2]; ck = ("cacc", fc % 2)
        for tb in range(4):
            pbk = 4 + tb
            for k in range(KC):
                P.add("pe", lambda e, k=k: e.matmul(pss[pbk][:], lhsT=wb_[:, k, :], rhs=h2o[:, k, tb * 512:(tb + 1) * 512],
                                                    start=(k == 0), stop=(k == KC - 1)),
                      reads=[("rg", s, 2), "h2o"], writes=[("ps", pbk)])
            P.add("dve", lambda e: e.tensor_tensor(
                out=cvT[:, fc, tb * 512:(tb + 1) * 512], in0=pss[pbk][:],
                in1=cacc[:, 4 * tb:4 * tb + 4, :].rearrange("p n q -> p (n q)"), op=ALU.mult),
                reads=[("ps", pbk), ck], pwrites=["cvT"])

    cslots = {}
    for fc in range(KC):
        cslots[fc] = rloaded.pop(0)
        conv_uc(fc, cslots[fc])
        conv_taps(fc)
        if fc >= 1:
            conv_b(fc - 1, cslots[fc - 1])
            issue_ring()
    conv_b(KC - 1, cslots[KC - 1])
    issue_ring()
    P.barrier()

    wout_st = A[:, 24576:32768].rearrange("p (k n) -> p k n", k=KC)
    P.add("pool", lambda e: e.dma_start(out=wout_st, in_=wout_d.rearrange("(k p) n -> p k n", p=128)),
          writes=["wout_st"], chan="wout")
    mT = C[:].rearrange("p (k t) -> p k t", k=KC)
    tab = [E[:, i * 1024:(i + 1) * 1024].bitcast(F32) for i in range(4)]
    m12 = [E[:, 4096 + i * 1024:4096 + (i + 1) * 1024].bitcast(F32) for i in range(4)]
    gctr = [0]
    for oc in range(KC):
        s = rloaded.pop(0)
        wga = RING[s][0]; wgc = RING[s][1]
        for tb in range(4):
            i2 = gctr[0] % 2; gctr[0] += 1
            pya, pyc, pga, pgc = 4 * i2, 4 * i2 + 1, 4 * i2 + 2, 4 * i2 + 3
            tk = tb * 512
            for hh in range(4):
                P.add("pe", lambda e, hh=hh, tk=tk, pya=pya: e.matmul(pss[pya][:], lhsT=wab[:, hh, oc * 128:(oc + 1) * 128],
                                                                       rhs=oT[:, hh, tk:tk + 512], start=(hh == 0), stop=(hh == 3)),
                      reads=["wab", "oT"], writes=[("ps", pya)])
            for k in range(KC):
                P.add("pe", lambda e, k=k, tk=tk, pyc=pyc: e.matmul(pss[pyc][:], lhsT=wcb[:, k, oc * 128:(oc + 1) * 128],
                                                                     rhs=cvT[:, k, tk:tk + 512], start=(k == 0), stop=(k == KC - 1)),
                      reads=["wcb", "cvT"], writes=[("ps", pyc)])
            for k in range(KC):
                P.add("pe", lambda e, k=k, tk=tk, pga=pga: e.matmul(pss[pga][:], lhsT=wga[:, k, :], rhs=h2o[:, k, tk:tk + 512],
                                                                     start=(k == 0), stop=(k == KC - 1)),
                      reads=[("rg", s, 0), "h2o"], writes=[("ps", pga)])
            for k in range(KC):
                P.add("pe", lambda e, k=k, tk=tk, pgc=pgc: e.matmul(pss[pgc][:], lhsT=wgc[:, k, :], rhs=h2o[:, k, tk:tk + 512],
                                                                     start=(k == 0), stop=(k == KC - 1)),
                      reads=[("rg", s, 1), "h2o"], writes=[("ps", pgc)])
            ta = tab[2 * i2]; tc_ = tab[2 * i2 + 1]; m1 = m12[2 * i2]; m2 = m12[2 * i2 + 1]
            P.add("act", lambda e, ta=ta, pga=pga: e.activation(out=ta, in_=pss[pga][:], func=AF.Tanh, scale=0.5),
                  reads=[("ps", pga)], writes=[("ta", i2)])
            P.add("act", lambda e, tc_=tc_, pgc=pgc: e.activation(out=tc_, in_=pss[pgc][:], func=AF.Tanh, scale=0.5),
                  reads=[("ps", pgc)], writes=[("tc", i2)])
            P.add("dve", lambda e, ta=ta, m1=m1, pya=pya: e.scalar_tensor_tensor(out=m1, in0=ta, scalar=1.0, in1=pss[pya][:],
                                                                                 op0=ALU.add, op1=ALU.mult),
                  reads=[("ta", i2), ("ps", pya)], writes=[("m1", i2)])
            P.add("dve", lambda e, tc_=tc_, m2=m2, pyc=pyc: e.scalar_tensor_tensor(out=m2, in0=tc_, scalar=1.0, in1=pss[pyc][:],
                                                                                   op0=ALU.add, op1=ALU.mult),
                  reads=[("tc", i2), ("ps", pyc)], writes=[("m2", i2)])
            P.add("dve", lambda e, m1=m1, m2=m2, tk=tk: e.tensor_tensor(out=mT[:, oc, tk:tk + 512], in0=m1, in1=m2, op=ALU.add),
                  reads=[("m1", i2), ("m2", i2)], pwrites=["mT"])
        issue_ring()
    P.barrier()

    for tg in range(3):
        P.add("sp", lambda e, tg=tg: e.dma_start(out=xbuf[:, 4 * tg:4 * tg + 4, :], in_=x1s_d[:, 4 * tg:4 * tg + 4, :]),
              reads=[("x1s", tg)], writes=[("x", 4 * tg + i) for i in range(4)], chan=("xg", tg))
    for k in range(KC):
        P.add("dve", lambda e, k=k: e.tensor_tensor(out=wout[:, k, :], in0=wout_st[:, k, :], in1=gtrows[:, 1, :], op=ALU.mult),
              reads=["wout_st", ("gt", 1)], pwrites=["wout"])
    P.add("sp", lambda e: e.dma_start(out=xbuf[:, 12:16, :], in_=x1s_d[:, 12:16, :]),
          reads=[("x1s", 3)], writes=[("x", 12 + i) for i in range(4)] + ["wout_st"], chan=("xg", 3))
    for (j0, nj) in PIECES:
        q_piece(f2g_d, f2u_d, f2d_d, j0, nj)
    assert piece_ctr[0] % 3 == 0
    issue_load(); issue_load()
    for n in range(NT):
        for hf in range(2):
            pa = 2 * (n % 2) + hf
            for k in range(KC):
                P.add("pe", lambda e, k=k, n=n, hf=hf, pa=pa: e.matmul(pss[pa][:], lhsT=mT[:, k, n * 128:(n + 1) * 128],
                                                                        rhs=wout[:, k, hf * 512:(hf + 1) * 512],
                                                                        start=(k == 0), stop=(k == KC - 1)),
                      reads=["mT", "wout"], writes=[("ps", pa)])
            P.add("dve", lambda e, n=n, hf=hf, pa=pa: e.tensor_tensor(
                out=xbuf[:, n, hf * 512:(hf + 1) * 512], in0=pss[pa][:], in1=xbuf[:, n, hf * 512:(hf + 1) * 512], op=ALU.add),
                reads=[("ps", pa), ("x", n)], writes=[("x", n)])
    P.barrier()

    issue_load()

    def post_out(tg):
        P.add("sp", lambda e: e.dma_start(out=out_d[:, 4 * tg:4 * tg + 4, :], in_=xbuf[:, 4 * tg:4 * tg + 4, :]),
              reads=[("x", 4 * tg + i) for i in range(4)], writes=[("out", tg)], chan=("out", tg))

    ffn_pass(2, lambda tg: norm_front(tg, 64), lambda tg: norm_back(tg, 32, 40, 64), lambda tg: None, post_out, {})
    P.add("sp", None, reads=[("out", tg) for tg in range(4)])
    P.emit(nc)
    st.close()
    return nc


def _masks():
    m = np.zeros((128, 6, 128), np.float32)
    i = np.arange(128)
    mloc = [i, i, i]
    for g in range(3):
        ml = mloc[g]
        k = ml[:, None]; q = ml[None, :]
        m[:, 2 * g, :] = np.where(k >= q, 0.0, NEG)
        m[:, 2 * g + 1, :] = np.where(k <= q, 0.0, NEG)
    return m.astype(ml_dtypes.bfloat16)


def _host_inputs(inputs):
    x = np.ascontiguousarray(np.asarray(inputs["x"], dtype=np.float32))
    c = np.asarray(inputs["c"], dtype=np.float32)
    sq = lambda name: np.ascontiguousarray(np.asarray(inputs[name], dtype=np.float32)[0])
    col = lambda v: np.ascontiguousarray(v.reshape(KC, 128).T)
    shared = {
        "w_ada": sq("w_ada"), "b_ada": sq("b_ada"),
        "gcols": np.ascontiguousarray(np.concatenate([col(sq("norm_ffn1")), col(sq("norm_mix")), col(sq("norm_ffn2"))], axis=1)),
        "qkn": np.ascontiguousarray(np.stack([sq("q_norm"), sq("k_norm")], axis=1)),
        "cw": np.ascontiguousarray(sq("conv_w").reshape(3, KC, 128).transpose(2, 1, 0)),
        "bcol": np.ascontiguousarray(sq("b_ada").reshape(9 * KC, 128).T),
        "ffn1_w_gate": sq("ffn1_w_gate"), "ffn1_w_up": sq("ffn1_w_up"), "ffn1_w_down": sq("ffn1_w_down"),
        "ffn2_w_gate": sq("ffn2_w_gate"), "ffn2_w_up": sq("ffn2_w_up"), "ffn2_w_down": sq("ffn2_w_down"),
        "w_in": sq("w_in"), "w_attn_branch": sq("w_attn_branch"), "w_conv_branch": sq("w_conv_branch"),
        "w_out": sq("w_out"),
        "ident": np.eye(128).astype(ml_dtypes.bfloat16), "identf": np.eye(128, dtype=np.float32),
        "masks": _masks(),
    }
    in_maps = []
    for core in range(8):
        b, ch = core // 4, core % 4
        xo = x[b, ch * T:(ch + 1) * T].reshape(128, NT, D)
        xh = x[b, (ch - 1) * T:ch * T].reshape(128, NT, D) if ch > 0 else np.zeros((128, NT, D), np.float32)
        hm = np.zeros((128, 2), np.float32)
        hm[:, 0] = 0.0 if ch > 0 else NEG
        hm[:, 1] = 1.0 if ch > 0 else 0.0
        m = dict(shared)
        m.update({"xo": np.ascontiguousarray(xo), "xh": np.ascontiguousarray(xh), "cT": col(c[b]), "hm": hm})
        in_maps.append(m)
    return in_maps


_NC_CACHE = {}


def kernel(**inputs):
    in_maps = _host_inputs(inputs)
    if "nc" not in _NC_CACHE:
        _NC_CACHE["nc"] = build_nc()
    res = run_bass_kernel_spmd(_NC_CACHE["nc"], in_maps, core_ids=list(range(8)))
    out = np.empty((2, 4 * T, D), np.float32)
    for core in range(8):
        b, ch = core // 4, core % 4
        out[b, ch * T:(ch + 1) * T] = np.asarray(res.results[core]["out"]).reshape(T, D)
    return out
```

```python
from contextlib import ExitStack
import numpy as np
import ml_dtypes
import concourse.bass as bass
import concourse.mybir as mybir
from concourse.bass_utils import run_bass_kernel_spmd

F32 = mybir.dt.float32
BF16 = mybir.dt.bfloat16
AF = mybir.ActivationFunctionType
ALU = mybir.AluOpType
AX = mybir.AxisListType

NEG = -30000.0
EPS = 1e-6
D = 1024
KC = 8
DFF = 2816
NJ = 22
T = 2048
NT = 16
INW = 9728


class Op:
    __slots__ = ("eng", "fn", "deps", "chan", "chan_val", "needs_inc", "inc_val")

    def __init__(self, eng, fn, chan):
        self.eng = eng
        self.fn = fn
        self.chan = chan
        self.chan_val = 0
        self.deps = ()
        self.needs_inc = False
        self.inc_val = 0


class _Rec:
    def __init__(self):
        self.call = None

    def __getattr__(self, name):
        def f(*a, **kw):
            self.call = (name, a, kw)
        return f


class Prog:
    ENGS = ("pe", "act", "dve", "pool", "sp")
    BLK = {"pe": "tensor", "act": "scalar", "dve": "vector", "pool": "gpsimd", "sp": "sync"}

    def __init__(self):
        self.ops = []
        self.last_write = {}
        self.readers = {}
        self.chan_count = {}
        self.last_on_eng = {}
        self.last_on_chan = {}
        self.par_epoch = {}
        self.epoch_base = {}
        self.new_epoch = set()

    def add(self, eng, fn, reads=(), writes=(), chan=None, extra_deps=(), pwrites=()):
        if fn is not None:
            rec = _Rec()
            fn(rec)
            assert rec.call is not None
            fn = rec.call
        op = Op(eng, fn, chan)
        deps = set(extra_deps)
        for r in reads:
            deps.update(self.last_write.get(r, ()))
            if isinstance(r, tuple) and r[0] == "ps":
                deps.update(o for o in self.readers.get(r, ()) if o.eng != eng)
        for w in writes:
            deps.update(self.last_write.get(w, ()))
            deps.update(self.readers.get(w, ()))
        for w in pwrites:
            rd = self.readers.get(w, ())
            if rd or not self.par_epoch.get(w, False):
                base = set(rd) | set(self.last_write.get(w, ()))
                self.epoch_base[w] = base
                self.new_epoch.add(w)
            deps.update(self.epoch_base.get(w, ()))
        if eng == "pe":
            deps = {d for d in deps if not (d.eng == "pe" and d.chan is None)}
        deps.discard(op)
        for r in reads:
            self.readers.setdefault(r, []).append(op)
        for w in writes:
            self.last_write[w] = [op]
            self.readers[w] = []
            self.par_epoch[w] = False
        for w in pwrites:
            if w in self.new_epoch:
                self.new_epoch.discard(w)
                self.last_write[w] = [op]
                self.readers[w] = []
                self.par_epoch[w] = True
            else:
                self.last_write[w].append(op)
        if chan is not None:
            n = self.chan_count.get(chan, 0) + 1
            self.chan_count[chan] = n
            op.chan_val = 16 * n
            self.last_on_chan[chan] = op
        elif fn is not None:
            self.last_on_eng[eng] = op
        for d in deps:
            if d.chan is None:
                d.needs_inc = True
        op.deps = tuple(deps)
        self.ops.append(op)
        return op

    def barrier(self):
        deps = list(self.last_on_eng.values()) + list(self.last_on_chan.values())
        for e in self.ENGS:
            self.add(e, None, extra_deps=deps)

    def emit(self, nc):
        cnt = {e: 0 for e in self.ENGS}
        for op in self.ops:
            if op.chan is None and op.needs_inc:
                cnt[op.eng] += 1
                op.inc_val = cnt[op.eng]
        with ExitStack() as st:
            sem_eng = {e: st.enter_context(nc.semaphore("s_" + e)) for e in self.ENGS}
            sem_chan = {c: st.enter_context(nc.semaphore("c_%d" % i))
                        for i, c in enumerate(self.chan_count)}
            block = st.enter_context(nc.Block())
            for e in self.ENGS:
                ops_e = [op for op in self.ops if op.eng == e]
                if not ops_e:
                    continue

                def body(engine, ops_e=ops_e, e=e):
                    waited = {}
                    for op in ops_e:
                        need = {}
                        for d in op.deps:
                            if d.chan is not None:
                                key = ("c", d.chan)
                                s, v = sem_chan[d.chan], d.chan_val
                            else:
                                key = ("e", d.eng)
                                s, v = sem_eng[d.eng], d.inc_val
                            if need.get(key, (None, 0))[1] < v:
                                need[key] = (s, v)
                        for key, (s, v) in need.items():
                            if waited.get(key, 0) < v:
                                engine.wait_ge(s, v)
                                waited[key] = v
                        if op.fn is None:
                            continue
                        name, a, kw = op.fn
                        ins = getattr(engine, name)(*a, **kw)
                        if op.chan is not None:
                            ins.then_inc(sem_chan[op.chan], 16)
                        elif op.needs_inc:
                            ins.then_inc(sem_eng[e], 1)

                getattr(block, self.BLK[e])(body)


def build_nc(debug_stop=None):
    nc = bass.Bass("TRN2", target_bir_lowering=False)

    def din(name, shape, dt=F32):
        return nc.dram_tensor(name, list(shape), dt, kind="ExternalInput").ap()

    xo_d = din("xo", [128, NT, D])
    xh_d = din("xh", [128, NT, D])
    cT_d = din("cT", [128, KC])
    hm_d = din("hm", [128, 2])
    wada_d = din("w_ada", [D, 9 * D])
    bada_d = din("b_ada", [9 * D])
    gcol_d = din("gcols", [128, 3 * KC])
    qk_d = din("qkn", [128, 2])
    cw_d = din("cw", [128, KC, 3])
    bcol_d = din("bcol", [128, 9 * KC])
    f1g_d = din("ffn1_w_gate", [D, DFF]); f1u_d = din("ffn1_w_up", [D, DFF]); f1d_d = din("ffn1_w_down", [DFF, D])
    f2g_d = din("ffn2_w_gate", [D, DFF]); f2u_d = din("ffn2_w_up", [D, DFF]); f2d_d = din("ffn2_w_down", [DFF, D])
    win_d = din("w_in", [D, INW])
    wab_d = din("w_attn_branch", [512, D])
    wcb_d = din("w_conv_branch", [D, D])
    wout_d = din("w_out", [D, D])
    ident_d = din("ident", [128, 128], BF16)
    identf_d = din("identf", [128, 128], F32)
    masks_d = din("masks", [128, 6, 128], BF16)
    out_d = nc.dram_tensor("out", [128, NT, D], F32, kind="ExternalOutput").ap()
    x1s_d = nc.dram_tensor("x1s", [128, NT, D], F32).ap()
    h2s_d = nc.dram_tensor("h2s", [128, KC, T], BF16).ap()

    P = Prog()
    st = ExitStack()
    sb = lambda name, shape, dt: st.enter_context(nc.sbuf_tensor(name, list(shape), dt))
    A = sb("A", [128, 32768], BF16)
    B = sb("B", [128, 16384], BF16)
    C = sb("C", [128, 16384], BF16)
    Dw = sb("Dw", [128, 24576], BF16)
    E = sb("E", [128, 8192], BF16)
    ident = sb("ident_s", [128, 128], BF16)
    identf = sb("identf_s", [128, 128], F32)
    masks = sb("masks_s", [128, 8, 128], BF16)
    masksh = sb("masksh_s", [128, 3, 128], BF16)
    ones_bf = sb("ones_s", [128, 128], BF16)
    gtrows = sb("gtrows", [128, 3, D], BF16)
    cols = sb("cols", [128, 64], F32)
    gcols = sb("gcols_s", [128, 3 * KC], F32)
    qk = sb("qk_s", [128, 4], F32)
    cw = sb("cw_s", [128, KC, 3], F32)
    hm = sb("hm_s", [128, 2], F32)
    cact = sb("cact", [128, KC], F32)
    cactb = sb("cactb", [128, KC], BF16)
    bcol = sb("bcol_s", [128, 9 * KC], F32)
    crep = sb("crep", [128, KC, 128], BF16)
    ssb = sb("ssb", [128, 6 * NT], F32)
    rstd = sb("rstd", [128, 6 * NT], F32)
    mhalf = sb("mhalf", [128, NT], F32)
    pss = [st.enter_context(nc.psum_tensor("ps%d" % i, [128, 512], F32)) for i in range(8)]

    def psb(i):
        return pss[i][:].bitcast(BF16)

    xbuf = A[:].bitcast(F32).rearrange("p (n d) -> p n d", n=NT)
    hT = B[:].rearrange("p (k t) -> p k t", k=KC)

    cload = []
    def cl(dst, src, key):
        cload.append(key)
        P.add("sp", lambda e: e.dma_start(out=dst, in_=src), writes=[key], chan="const")
    cl(ident[:], ident_d, "ident"); cl(identf[:], identf_d, "identf")
    cl(masks[:, 0:6, :], masks_d, "masks"); cl(gcols[:], gcol_d, "gcols")
    cl(qk[:, 0:2], qk_d, "qk"); cl(cw[:], cw_d, "cw"); cl(hm[:], hm_d, "hm"); cl(cact[:], cT_d, "cact"); cl(bcol[:], bcol_d, "bcol")
    P.add("dve", lambda e: e.memset(ones_bf[:], 1.0), reads=cload, writes=cload + ["ones"])
    P.add("dve", lambda e: e.memset(mhalf[:], -0.5), writes=["mhalf"])
    for g in range(3):
        P.add("dve", lambda e, g=g: e.tensor_scalar(out=masksh[:, g, :], in0=masks[:, 2 * g, :],
                                                    scalar1=hm[:, 0:1], scalar2=None, op0=ALU.add),
              reads=["masks", "hm"], writes=["masksh"])
    P.add("dve", lambda e: e.tensor_scalar(out=qk[:, 2:3], in0=qk[:, 1:2], scalar1=float(128 ** -0.5),
                                           scalar2=None, op0=ALU.mult), reads=["qk"], writes=["qk"])
    P.add("act", lambda e: e.activation(out=cact[:], in_=cact[:], func=AF.Silu), reads=["cact"], writes=["cact"])
    P.add("dve", lambda e: e.tensor_copy(out=cactb[:], in_=cact[:]), reads=["cact"], writes=["cactb"])
    for k in range(KC):
        P.add("dve", lambda e, k=k: e.tensor_scalar(out=crep[:, k, :], in0=ones_bf[:], scalar1=cact[:, k:k + 1],
                                                    scalar2=None, op0=ALU.mult),
              reads=["cact", "ones"], writes=["crep"])

    def slot_views(s):
        base = Dw[:, s * 12288:(s + 1) * 12288] if s < 2 else C[:, 0:12288]
        wg = base[:, 0:4096].rearrange("p (k n) -> p k n", k=KC)
        wu = base[:, 4096:8192].rearrange("p (k n) -> p k n", k=KC)
        wd = base[:, 8192:12288].rearrange("p (j n) -> p j n", j=4)
        wa = base[:, 0:8192].rearrange("p (k n) -> p k n", k=KC)
        return wg, wu, wd, wa
    SLOTS = [slot_views(s) for s in range(3)]

    hid = [E[:, i * 2048:(i + 1) * 2048].rearrange("p (j t) -> p j t", j=4) for i in range(2)]
    sgb = [E[:, 4096 + i * 1024:4096 + (i + 1) * 1024].bitcast(F32) for i in range(2)]
    xnb = [E[:, 6144:7168], E[:, 7168:8192], C[:, 13312:14336], C[:, 14336:15360]]
    junk = C[:, 12288:13312]
    brow = C[0:1, 15360:16384]
    PIECES = [(0, 4), (4, 4), (8, 2), (10, 4), (14, 4), (18, 4)]

    pending_loads = []
    piece_ctr = [0]

    def q_piece(wg_d, wu_d, wd_d, j0, nj):
        def mk():
            idx = piece_ctr[0]; piece_ctr[0] += 1
            s = idx % 3
            wg, wu, wd, _ = SLOTS[s]
            nc_ = nj * 128
            P.add("pool", lambda e: e.dma_start(out=wg[:, :, 0:nc_],
                  in_=wg_d[:, j0 * 128:j0 * 128 + nc_].rearrange("(k p) n -> p k n", p=128)),
                  writes=[("wg", s)], chan=("wg", s))
            P.add("pool", lambda e: e.dma_start(out=wu[:, :, 0:nc_],
                  in_=wu_d[:, j0 * 128:j0 * 128 + nc_].rearrange("(k p) n -> p k n", p=128)),
                  writes=[("wu", s)], chan=("wu", s))
            P.add("pool", lambda e: e.dma_start(out=wd[:, 0:nj, :],
                  in_=wd_d[j0 * 128:j0 * 128 + nc_, :].rearrange("(j p) n -> p j n", p=128)),
                  writes=[("wd", s)], chan=("wd", s))
            return s
        pending_loads.append(mk)

    def q_mod(j):
        def mk():
            idx = piece_ctr[0]; piece_ctr[0] += 1
            s = idx % 3
            wa = SLOTS[s][3]
            P.add("pool", lambda e: e.dma_start(
                out=wa, in_=wada_d[:, j * D:(j + 1) * D].rearrange("(k p) n -> p k n", p=128)),
                writes=[("wg", s), ("wu", s)], chan=("wg", s))
            if j % 3 == 2:
                P.add("pool", lambda e: e.dma_start(out=brow, in_=bada_d[j * D:(j + 1) * D].rearrange("(o n) -> o n", o=1)),
                      writes=["brow"], chan="brow")
            return s
        pending_loads.append(mk)

    loaded = []

    def issue_load():
        if pending_loads:
            loaded.append(pending_loads.pop(0)())

    def mod_block(j, issue=True):
        s = loaded.pop(0)
        wa = SLOTS[s][3]
        role, li = j % 3, j // 3
        if role == 2:
            for hf in range(2):
                pb = 6 + hf
                for k in range(KC):
                    P.add("pe", lambda e, k=k: e.matmul(pss[pb][:], lhsT=crep[:, k, :], rhs=wa[:, k, hf * 512:(hf + 1) * 512],
                                                        start=(k == 0), stop=False),
                          reads=["crep", ("wg", s), ("wu", s)], writes=[("ps", pb)])
                P.add("pe", lambda e: e.matmul(pss[pb][:], lhsT=ones_bf[0:1, :], rhs=brow[:, hf * 512:(hf + 1) * 512],
                                               start=False, stop=True),
                      reads=["ones", "brow"], writes=[("ps", pb)])
                P.add("dve", lambda e: e.tensor_scalar(out=gtrows[:, li, hf * 512:(hf + 1) * 512], in0=pss[pb][:], scalar1=0.5,
                                                       scalar2=None, op0=ALU.mult),
                      reads=[("ps", pb)], writes=[("gt", li)])
        else:
            pb = 7
            for kc in range(KC):
                for k in range(KC):
                    P.add("pe", lambda e, k=k, kc=kc: e.matmul(pss[pb][:, kc:kc + 1], lhsT=wa[:, k, kc * 128:(kc + 1) * 128],
                                                               rhs=cactb[:, k:k + 1], start=(k == 0), stop=(k == KC - 1)),
                          reads=["cactb", ("wg", s), ("wu", s)], writes=[("ps", pb)])
            if role == 0:
                P.add("dve", lambda e: e.tensor_tensor(out=cols[:, 16 * li + 8:16 * li + 16], in0=pss[pb][:, 0:KC],
                                                       in1=bcol[:, j * KC:(j + 1) * KC], op=ALU.add),
                      reads=[("ps", pb), "bcol"], writes=["cols"])
            else:
                P.add("dve", lambda e: e.tensor_tensor(out=cols[:, 48:56], in0=pss[pb][:, 0:KC],
                                                       in1=bcol[:, j * KC:(j + 1) * KC], op=ALU.add),
                      reads=[("ps", pb), "bcol"], writes=["cols"])
                P.add("dve", lambda e: e.scalar_tensor_tensor(
                    out=cols[:, 16 * li:16 * li + 8], in0=cols[:, 48:56], scalar=1.0, in1=gcols[:, li * KC:(li + 1) * KC],
                    op0=ALU.add, op1=ALU.mult), reads=["cols", "gcols"], writes=["cols"])
        if issue:
            issue_load()

    def load_x(src_d, tg, after=()):
        P.add("sp", lambda e: e.dma_start(out=xbuf[:, 4 * tg:4 * tg + 4, :], in_=src_d[:, 4 * tg:4 * tg + 4, :]),
              reads=list(after), writes=[("x", 4 * tg + i) for i in range(4)], chan=("xg", tg))

    tctr = [0]

    ng_state = {}

    def norm_front(tg, off=0):
        c0 = off + 4 * tg
        for i in range(4):
            n = 4 * tg + i
            P.add("act", lambda e, n=n, i=i: e.activation(out=junk, in_=xbuf[:, n, :], func=AF.Square,
                                                          accum_out=ssb[:, c0 + i:c0 + i + 1]),
                  reads=[("x", n), "ssb"], writes=[("ss", off + n), "junk"])
        P.add("dve", lambda e: e.tensor_scalar(out=rstd[:, c0:c0 + 4], in0=ssb[:, c0:c0 + 4],
                                               scalar1=1.0 / D, scalar2=EPS, op0=ALU.mult, op1=ALU.add),
              reads=["ssb"] + [("ss", off + 4 * tg + i) for i in range(4)], writes=[("rs", off + tg)])
        P.add("pool", lambda e: e.tensor_tensor(out=rstd[:, c0:c0 + 4], in0=rstd[:, c0:c0 + 4],
                                                in1=mhalf[:, 0:4], op=ALU.pow),
              reads=[("rs", off + tg), "mhalf"], writes=[("rs", off + tg)])
        tiles = []
        for i in range(4):
            n = 4 * tg + i
            xi = tctr[0] % 4
            pb = 6 + (tctr[0] % 2); tctr[0] += 1
            xb = xnb[xi]
            tiles.append((n, i, xi, pb, xb))
            P.add("pool", lambda e, n=n, i=i, xb=xb: e.tensor_scalar(out=xb, in0=xbuf[:, n, :], scalar1=rstd[:, c0 + i:c0 + i + 1],
                                                                     scalar2=0.0, op0=ALU.mult, op1=ALU.add),
                  reads=[("x", n), ("rs", off + tg)], writes=[("xn", xi)])
        ng_state[(off, tg)] = tiles

    def norm_back(tg, acol, scol, off=0):
        tiles = ng_state.pop((off, tg))

        def transp(t):
            n, i, xi, pb, xb = t
            for c in range(KC):
                P.add("pe", lambda e, c=c: e.transpose(
                    out=psb(pb)[:, c * 128:(c + 1) * 128], in_=xb[:, c * 128:(c + 1) * 128], identity=ident[:]),
                    reads=[("xn", xi), "ident"], writes=[("ps", pb)])

        def evac(t):
            n, i, xi, pb, xb = t
            for c in range(KC):
                if pb == 6:
                    P.add("dve", lambda e, c=c: e.tensor_scalar(
                        out=hT[:, c, n * 128:(n + 1) * 128], in0=psb(pb)[:, c * 128:(c + 1) * 128],
                        scalar1=cols[:, acol + c:acol + c + 1], scalar2=cols[:, scol + c:scol + c + 1],
                        op0=ALU.mult, op1=ALU.add), reads=[("ps", pb), "cols"], pwrites=[("hT", n)])
                else:
                    P.add("act", lambda e, c=c: e.activation(
                        out=hT[:, c, n * 128:(n + 1) * 128], in_=psb(pb)[:, c * 128:(c + 1) * 128],
                        func=AF.Identity, scale=cols[:, acol + c:acol + c + 1], bias=cols[:, scol + c:scol + c + 1]),
                        reads=[("ps", pb), "cols"], pwrites=[("hT", n)])

        transp(tiles[0]); transp(tiles[1])
        evac(tiles[0]); transp(tiles[2])
        evac(tiles[1]); transp(tiles[3])
        evac(tiles[2]); evac(tiles[3])

    mctr = [0]
    actr = [0]

    def unit_gu(s, nj, tg, hb, js):
        wg, wu, wd, _ = SLOTS[s]
        hd = hid[hb]
        for j in js:
            pg = j % 2
            for k in range(KC):
                P.add("pe", lambda e, j=j, k=k: e.matmul(
                    pss[pg][:], lhsT=wg[:, k, j * 128:(j + 1) * 128], rhs=hT[:, k, tg * 512:(tg + 1) * 512],
                    start=(k == 0), stop=(k == KC - 1)),
                    reads=[("wg", s)] + [("hT", 4 * tg + i) for i in range(4)], writes=[("ps", pg)])
            for k in range(KC):
                P.add("pe", lambda e, j=j, k=k: e.matmul(
                    pss[2 + pg][:], lhsT=wu[:, k, j * 128:(j + 1) * 128], rhs=hT[:, k, tg * 512:(tg + 1) * 512],
                    start=(k == 0), stop=(k == KC - 1)),
                    reads=[("wu", s)] + [("hT", 4 * tg + i) for i in range(4)], writes=[("ps", 2 + pg)])
            P.add("act", lambda e: e.activation(out=sgb[pg], in_=pss[pg][:], func=AF.Silu),
                  reads=[("ps", pg)], writes=[("sg", pg)])
            P.add("dve", lambda e, j=j: e.tensor_tensor(out=hd[:, j, :], in0=pss[2 + pg][:], in1=sgb[pg], op=ALU.mult),
                  reads=[("ps", 2 + pg), ("sg", pg)], pwrites=[("hid", hb)])

    def unit_down(s, nj, tg, hb):
        wg, wu, wd, _ = SLOTS[s]
        hd = hid[hb]
        for t in range(4):
            n = 4 * tg + t
            for hf in range(2):
                pa = 4 + (actr[0] % 4); actr[0] += 1
                for j in range(nj):
                    P.add("pe", lambda e, j=j: e.matmul(
                        pss[pa][:], lhsT=hd[:, j, t * 128:(t + 1) * 128], rhs=wd[:, j, hf * 512:(hf + 1) * 512],
                        start=(j == 0), stop=(j == nj - 1)),
                        reads=[("hid", hb), ("wd", s)], writes=[("ps", pa)])
                P.add("dve", lambda e: e.tensor_tensor(
                    out=xbuf[:, n, hf * 512:(hf + 1) * 512], in0=pss[pa][:],
                    in1=xbuf[:, n, hf * 512:(hf + 1) * 512], op=ALU.add),
                    reads=[("ps", pa), ("x", n)], writes=[("x", n)])

    def new_hb():
        hb = mctr[0] % 2; mctr[0] += 1
        return hb

    def ffn_unit(s, nj, tg, mid=None):
        hb = new_hb()
        unit_gu(s, nj, tg, hb, range(nj))
        if mid is not None:
            mid()
        unit_down(s, nj, tg, hb)

    def ffn_pass(gi, pre_f, pre_b, post_f, post_b, extras, first_mid=None, nxt=None, head=None):
        if head is None:
            pre_f(0); pre_b(0); pre_f(1)
        deferred = None
        for q, (j0, nj) in enumerate(PIECES):
            s = loaded.pop(0)
            wd = SLOTS[s][2]

            def scale_wd(s=s, wd=wd, nj=nj):
                for j in range(nj):
                    P.add("dve", lambda e, j=j: e.tensor_tensor(out=wd[:, j, :], in0=wd[:, j, :], in1=gtrows[:, gi, :], op=ALU.mult),
                          reads=[("wd", s), ("gt", gi)], writes=[("wd", s)])
            mid0 = None
            if q == 0 and first_mid is not None:
                def mid0(scale_wd=scale_wd):
                    first_mid()
                    scale_wd()
            else:
                scale_wd()
            last = (q == len(PIECES) - 1)
            hbs = [None] * 4
            for tg in range(4):
                if q == 0:
                    ffn_unit(s, nj, tg, mid=(mid0 if tg == 0 else None))
                else:
                    if tg == 0:
                        hbs[0] = new_hb()
                        unit_gu(s, nj, 0, hbs[0], range(nj))
                    if tg < 3:
                        hbs[tg + 1] = new_hb()
                        unit_gu(s, nj, tg + 1, hbs[tg + 1], [0])
                    unit_down(s, nj, tg, hbs[tg])
                    if tg < 3:
                        unit_gu(s, nj, tg + 1, hbs[tg + 1], range(1, nj))
                if q == 0:
                    if tg == 0 and head is not None:
                        head()
                    if tg + 1 < 4:
                        pre_b(tg + 1)
                    if tg + 2 < 4:
                        pre_f(tg + 2)
                if last:
                    if tg >= 1:
                        post_b(tg - 1)
                    post_f(tg)
                    if nxt is not None and tg == 2:
                        nxt[0](0)
                    if nxt is not None and tg == 3:
                        nxt[1](0)
                        nxt[0](1)
            issue_load()
            if q == 0 and first_mid is not None:
                issue_load()
            for fn in extras.get(q, ()):
                fn()
        if nxt is not None:
            return lambda: post_b(3)
        post_b(3)
        return None

    def ring(s):
        base = Dw[:, s * 3072:(s + 1) * 3072]
        return [base[:, i * 1024:(i + 1) * 1024].rearrange("p (k n) -> p k n", k=KC) for i in range(3)]
    RING = [ring(s) for s in range(3)]
    wab = Dw[:, 9216:13312].rearrange("p (h n) -> p h n", h=4)
    wcb = Dw[:, 13312:21504].rearrange("p (k n) -> p k n", k=KC)
    wout = E[:, 0:8192].rearrange("p (k n) -> p k n", k=KC)

    win_v = win_d.rearrange("(k p) n -> p k n", p=128)
    rctr = [0]
    ring_extra = []
    rloads = []

    def queue_ring(colsets):
        def mk():
            s = rctr[0] % 3; rctr[0] += 1
            for i, c0 in enumerate(colsets):
                P.add("pool", lambda e, i=i, c0=c0, s=s: e.dma_start(out=RING[s][i], in_=win_v[:, :, c0:c0 + 128]),
                      writes=[("rg", s, i)] + ring_extra, chan=("rg", s, i))
            return s
        rloads.append(mk)
    rloaded = []

    def issue_ring():
        if rloads:
            rloaded.append(rloads.pop(0)())

    order = [(g, h) for h in range(4) for g in range(3)]
    for (g, h) in order:
        idx = g * 4 + h
        queue_ring([idx * 128, 1536 + idx * 128, 3072 + idx * 128])
    for fc in range(KC):
        queue_ring([4608 + fc * 128, 6656 + fc * 128, 5632 + fc * 128])
    for oc in range(KC):
        queue_ring([7680 + oc * 128, 8704 + oc * 128])

    F1 = (f1g_d, f1u_d, f1d_d)
    for j in (0, 1):
        q_mod(j)
    for q, (j0, nj) in enumerate(PIECES):
        q_piece(*F1, j0, nj)
        if q <= 2:
            q_mod(2 + q)
    for q, (j0, nj) in enumerate(PIECES):
        q_piece(*F1, j0, nj)
        if q < 4:
            q_mod(5 + q)

    load_x(xh_d, 0)
    for _ in range(3):
        issue_load()
    for tg in range(1, 4):
        load_x(xh_d, tg, after=[("wg", 0)])
    P.add("dve", lambda e: e.memset(ssb[:], 0.0), writes=["ssb"])
    for j in (0, 1):
        mod_block(j)

    def post_halo_b(tg):
        norm_back(tg, 16, 24, 16)
        P.add("sp", lambda e: e.dma_start(out=h2s_d[:, :, tg * 512:(tg + 1) * 512], in_=hT[:, :, tg * 512:(tg + 1) * 512]),
              reads=[("hT", 4 * tg + i) for i in range(4)], writes=[("h2s", tg)], chan=("h2s", tg))
        load_x(xo_d, tg)

    own_pre = (lambda tg: norm_front(tg, 32), lambda tg: norm_back(tg, 0, 8, 32))
    tail = ffn_pass(0, lambda tg: norm_front(tg, 0), lambda tg: norm_back(tg, 0, 8, 0),
                    lambda tg: norm_front(tg, 16), post_halo_b,
                    {1: [lambda: mod_block(3)], 2: [lambda: mod_block(4)]},
                    first_mid=lambda: mod_block(2, issue=False))

    def early_ring():
        ring_extra.extend([("wg", 0), ("wu", 0), ("wd", 0)])
        for _ in range(3):
            issue_ring()
        del ring_extra[:]

    def post_own_f(tg):
        P.add("sp", lambda e: e.dma_start(out=x1s_d[:, 4 * tg:4 * tg + 4, :], in_=xbuf[:, 4 * tg:4 * tg + 4, :]),
              reads=[("x", 4 * tg + i) for i in range(4)], writes=[("x1s", tg)], chan=("x1s", tg))
        norm_front(tg, 48)

    ffn_pass(0, own_pre[0], own_pre[1],
             post_own_f, lambda tg: norm_back(tg, 16, 24, 48),
             {0: [lambda: mod_block(5)], 1: [lambda: mod_block(6)], 2: [lambda: mod_block(7)], 3: [lambda: mod_block(8), early_ring]},
             head=None)
    P.barrier()

    h2o = hT
    h2h = C[:].rearrange("p (k t) -> p k t", k=KC)
    for tg in range(4):
        P.add("sp", lambda e, tg=tg: e.dma_start(out=h2h[:, :, tg * 512:(tg + 1) * 512], in_=h2s_d[:, :, tg * 512:(tg + 1) * 512]),
              reads=[("h2s", tg)], pwrites=["h2h"], chan=("h2h", tg))
    oT = A[:, 0:8192].rearrange("p (h t) -> p h t", h=4)
    QTg = A[:, 8192:10240].rearrange("p (b i) -> p b i", b=16)
    KTg = A[:, 10240:14336].rearrange("p (b i) -> p b i", b=32)
    VTg = A[:, 14336:18432].rearrange("p (b i) -> p b i", b=32)
    Vb = A[:, 18432:22528].rearrange("p (b i) -> p b i", b=32)
    num = A[:, 22528:26624].bitcast(F32)
    den = A[:, 26624:30720].bitcast(F32)
    PT = [A[:, 30720 + i * 512:30720 + (i + 1) * 512] for i in range(2)]
    PT4 = PT + [E[:, 5120:5632], E[:, 5632:6144]]
    sqb = [A[:, 31744 + i * 512:31744 + (i + 1) * 512] for i in range(2)]
    rrb = [E[:, i * 1024:(i + 1) * 1024].bitcast(F32) for i in range(2)]

    P.add("pool", lambda e: e.dma_start(out=wab, in_=wab_d.rearrange("(h p) n -> p h n", p=128)),
          writes=["wab"], chan="wab")

    def dst_src(buf, base_blk, g, tb):
        flat = buf.rearrange("p b i -> p (b i)")
        o = base_blk * 128
        if g == 2:
            return flat[:, o + tb * 512:o + (tb + 1) * 512], (lambda ap: ap)
        if g == 1:
            dst = flat[:, o:o + 2048].rearrange("p (r q a) -> p r q a", r=4, a=4)[:, :, :, tb]
            return dst, (lambda ap: ap.rearrange("p (r q) -> p r q", r=4))
        dst = flat[:, o:o + 2048].rearrange("p (q a) -> p a q", a=16)[:, 4 * tb:4 * tb + 4, :]
        return dst, (lambda ap: ap.rearrange("p (a q) -> p a q", a=4))

    pctr = [0]
    qctr = [0]
    tctr2 = [0]
    ptmp = [E[:, 2048 + i * 512:2048 + (i + 1) * 512] for i in range(6)]
    pending_fin = []

    def proj_block(w, rhs_fn, ncols, kind, gcol, dst, shape_fn, contig, src="h2o"):
        pb = (0, 1, 2, 5)[pctr[0] % 4]; pctr[0] += 1
        for k in range(KC):
            P.add("pe", lambda e, k=k: e.matmul(pss[pb][:, 0:ncols], lhsT=w[0][:, k, :], rhs=rhs_fn(k),
                                                start=(k == 0), stop=(k == KC - 1)),
                  reads=[w[1], src], writes=[("ps", pb)])
        key = {"q": "QTg", "k": "KTg", "v": "VTg"}[kind]

        def store(src_fn, reads, eng_kind):
            if contig:
                out_ap, wkey, pw = dst, None, [key]
            else:
                ti = tctr2[0] % 6; tctr2[0] += 1
                out_ap, wkey, pw = ptmp[ti][:, 0:ncols], ("ptmp", ti), []
            src_fn(out_ap if contig else out_ap, reads, [wkey] if wkey else [], pw)
            if not contig:
                P.add("pool", lambda e: e.tensor_copy(out=dst, in_=shape_fn(ptmp[ti][:, 0:ncols])),
                      reads=[("ptmp", ti)], pwrites=[key])

        if kind == "v":
            if contig:
                P.add("dve", lambda e: e.tensor_copy(out=dst, in_=pss[pb][:, 0:ncols]),
                      reads=[("ps", pb)], pwrites=[key])
            else:
                P.add("dve", lambda e: e.tensor_copy(out=dst, in_=shape_fn(pss[pb][:, 0:ncols])),
                      reads=[("ps", pb)], pwrites=[key])
            while pending_fin:
                pending_fin.pop(0)()
            return
        i2 = qctr[0] % 2; qctr[0] += 1
        sq = sqb[i2]; rr = rrb[i2]
        P.add("act", lambda e: e.activation(out=sq[:, 0:ncols], in_=pss[pb][:, 0:ncols], func=AF.Square),
              reads=[("ps", pb)], writes=[("sq", i2)])

        def fin():
            P.add("pe", lambda e: e.matmul(pss[3 + i2][:, 0:ncols], lhsT=ones_bf[:], rhs=sq[:, 0:ncols], start=True, stop=True),
                  reads=[("sq", i2), "ones"], writes=[("ps", 3 + i2)])
            P.add("act", lambda e: e.activation(out=rr[:, 0:ncols], in_=pss[3 + i2][:, 0:ncols], func=AF.Ln,
                                                scale=(1.0 if kind == "q" else 1.0 / 128),
                                                bias=(cols[:, 57:58] if kind == "q" else cols[:, 56:57])),
                  reads=[("ps", 3 + i2), "cols"], writes=[("rr", i2)])
            P.add("act", lambda e: e.activation(out=rr[:, 0:ncols], in_=rr[:, 0:ncols], func=AF.Exp, scale=-0.5),
                  reads=[("rr", i2)], writes=[("rr", i2)])

            def src(out_ap, reads, wr, pw):
                P.add("dve", lambda e: e.scalar_tensor_tensor(out=out_ap, in0=pss[pb][:, 0:ncols], scalar=gcol,
                                                              in1=rr[:, 0:ncols], op0=ALU.mult, op1=ALU.mult),
                      reads=[("ps", pb), ("rr", i2), "qk"], writes=wr, pwrites=pw)
            store(src, None, None)
        prev = list(pending_fin)
        del pending_fin[:]
        pending_fin.append(fin)
        for f in prev:
            f()

    def flush_fin():
        while pending_fin:
            pending_fin.pop(0)()

    P.add("dve", lambda e: e.memset(cols[:, 56:57], EPS), writes=["cols"])
    P.add("dve", lambda e: e.memset(cols[:, 57:58], 128 * EPS), writes=["cols"])

    h2h4 = h2h.rearrange("p k (n q) -> p k n q", n=16)
    sctr = [0]
    for (g, h) in order:
        s = rloaded.pop(0)
        wq = (RING[s][0], ("rg", s, 0)); wk = (RING[s][1], ("rg", s, 1)); wv = (RING[s][2], ("rg", s, 2))
        for tb in range(4):
            rf = lambda k, tb=tb: h2o[:, k, tb * 512:(tb + 1) * 512]
            d_, sf = dst_src(VTg, 16, g, tb)
            proj_block(wv, rf, 512, "v", None, d_, sf, g == 2)
            d_, sf = dst_src(QTg, 0, g, tb)
            proj_block(wq, rf, 512, "q", qk[:, 0:1], d_, sf, g == 2)
            d_, sf = dst_src(KTg, 16, g, tb)
            proj_block(wk, rf, 512, "k", qk[:, 1:2], d_, sf, g == 2)
        if g == 2:
            for tb in range(4):
                rf = lambda k, tb=tb: h2h[:, k, tb * 512:(tb + 1) * 512]
                d_, sf = dst_src(VTg, 0, 2, tb)
                proj_block(wv, rf, 512, "v", None, d_, sf, True, src="h2h")
            for tb in range(4):
                rf = lambda k, tb=tb: h2h[:, k, tb * 512:(tb + 1) * 512]
                d_, sf = dst_src(KTg, 0, 2, tb)
                proj_block(wk, rf, 512, "k", qk[:, 1:2], d_, sf, True, src="h2h")
            nhalo = 16
        elif g == 1:
            rf = lambda k: h2h4[:, k, :, 96:128]
            sf = lambda ap: ap.rearrange("p (a rb) -> p a rb", a=4)
            dv = lambda buf: buf.rearrange("p b i -> p (b i)")[:, 0:512].rearrange("p (rb a) -> p a rb", a=4)
            proj_block(wv, rf, 512, "v", None, dv(VTg), sf, False, src="h2h")
            proj_block(wk, rf, 512, "k", qk[:, 1:2], dv(KTg), sf, False, src="h2h")
            nhalo = 4
        else:
            rf = lambda k: h2h4[:, k, :, 120:128]
            sf = lambda ap: ap.rearrange("p (a b) -> p a b", a=16)
            dv = lambda buf: buf[:, 0, :].rearrange("p (b a) -> p a b", a=16)
            proj_block(wv, rf, 128, "v", None, dv(VTg), sf, False, src="h2h")
            proj_block(wk, rf, 128, "k", qk[:, 1:2], dv(KTg), sf, False, src="h2h")
            nhalo = 1
        flush_fin()
        issue_ring()
        grps = [list(range(i0, min(i0 + 8, nhalo))) for i0 in range(0, nhalo, 8)] + [list(range(16, 24)), list(range(24, 32))]
        for grp in grps:
            pb = 6 + (sctr[0] % 2); sctr[0] += 1
            for ii, blk in enumerate(grp):
                P.add("pe", lambda e, ii=ii, blk=blk, pb=pb: e.transpose(
                    out=psb(pb)[:, ii * 128:(ii + 1) * 128], in_=VTg[:, blk, :], identity=ident[:]),
                    reads=["VTg", "ident"], writes=[("ps", pb)])
            b0 = grp[0]; nb = len(grp)
            P.add("dve", lambda e, b0=b0, nb=nb, pb=pb: e.tensor_copy(
                out=Vb[:, b0:b0 + nb, :], in_=psb(pb)[:, 0:nb * 128].rearrange("p (b i) -> p b i", b=nb)),
                reads=[("ps", pb)], pwrites=["Vb"])
        def batch_info(qb4):
            qblks = [4 * qb4 + i for i in range(4)]
            info = []
            for qb in qblks:
                cur = 16 + qb
                if g == 2:
                    prev, halo = qb, True
                elif g == 1:
                    r4, n1 = qb // 4, qb % 4
                    prev, halo = (r4, True) if n1 == 0 else (16 + qb - 1, False)
                else:
                    prev, halo = (0, True) if qb == 0 else (16 + qb - 1, False)
                info.append((prev, halo, cur))
            return qblks, info

        def scores(qb4):
            qblks, info = batch_info(qb4)
            par = qb4 % 2
            for kbi in range(2):
                ps_s = 4 + 2 * par + kbi
                pt = PT4[2 * par + kbi]
                for bi, qb in enumerate(qblks):
                    prev, halo, cur = info[bi]
                    kb = prev if kbi == 0 else cur
                    if kbi == 0:
                        mk_ap = masksh[:, g, :] if halo else masks[:, 2 * g, :]
                    else:
                        mk_ap = masks[:, 2 * g + 1, :]
                    P.add("pe", lambda e, bi=bi, qb=qb, kb=kb: e.matmul(
                        pss[ps_s][:, bi * 128:(bi + 1) * 128], lhsT=KTg[:, kb, :], rhs=QTg[:, qb, :], start=True, stop=False),
                        reads=["KTg", "QTg"], writes=[("ps", ps_s)])
                    P.add("pe", lambda e, bi=bi, mk_ap=mk_ap: e.matmul(
                        pss[ps_s][:, bi * 128:(bi + 1) * 128], lhsT=ident[:], rhs=mk_ap, start=False, stop=True),
                        reads=["ident", "masks", "masksh"], writes=[("ps", ps_s)])
                P.add("act", lambda e: e.activation(out=pt, in_=pss[ps_s][:], func=AF.Exp),
                      reads=[("ps", ps_s)], writes=[("PT", 2 * par + kbi)])

        def pv(qb4):
            qblks, info = batch_info(qb4)
            par = qb4 % 2
            pn = 0 + par; pd = 2 + par
            for bi, qb in enumerate(qblks):
                prev, halo, cur = info[bi]
                for kbi in range(2):
                    kb = prev if kbi == 0 else cur
                    pt = PT4[2 * par + kbi]
                    P.add("pe", lambda e, bi=bi, kb=kb, kbi=kbi, pt=pt: e.matmul(
                        pss[pn][:, bi * 128:(bi + 1) * 128], lhsT=Vb[:, kb, :], rhs=pt[:, bi * 128:(bi + 1) * 128],
                        start=(kbi == 0), stop=(kbi == 1)),
                        reads=["Vb", ("PT", 2 * par + kbi)], writes=[("ps", pn)])
                for kbi in range(2):
                    pt = PT4[2 * par + kbi]
                    P.add("pe", lambda e, bi=bi, kbi=kbi, pt=pt: e.matmul(
                        pss[pd][:, bi * 128:(bi + 1) * 128], lhsT=ones_bf[:], rhs=pt[:, bi * 128:(bi + 1) * 128],
                        start=(kbi == 0), stop=(kbi == 1)),
                        reads=["ones", ("PT", 2 * par + kbi)], writes=[("ps", pd)])
            def canon(buf):
                if g == 2:
                    return buf[:, qb4 * 512:(qb4 + 1) * 512], (lambda ap: ap)
                if g == 1:
                    v = buf.rearrange("p (a r q) -> p r a q", a=4, r=4)[:, qb4]
                    return v, (lambda ap: ap.rearrange("p (q a) -> p a q", a=4))
                v = buf.rearrange("p (a q) -> p a q", a=16)[:, :, 32 * qb4:32 * qb4 + 32]
                return v, (lambda ap: ap.rearrange("p (q a) -> p a q", a=16))
            dn, shp = canon(num)
            dd, shp2 = canon(den)
            if g == 0:
                P.add("dve", lambda e: e.tensor_copy(out=dn, in_=shp(pss[pn][:])), reads=[("ps", pn)], writes=["num"])
                P.add("dve", lambda e: e.tensor_copy(out=dd, in_=shp2(pss[pd][:])), reads=[("ps", pd)], writes=["den"])
            else:
                P.add("dve", lambda e: e.tensor_tensor(out=dn, in0=shp(pss[pn][:]), in1=dn, op=ALU.add),
                      reads=[("ps", pn), "num"], writes=["num"])
                P.add("dve", lambda e: e.tensor_tensor(out=dd, in0=shp2(pss[pd][:]), in1=dd, op=ALU.add),
                      reads=[("ps", pd), "den"], writes=["den"])

        for qb4 in range(4):
            scores(qb4)
            if qb4 >= 1:
                pv(qb4 - 1)
        pv(3)
        if g == 2:
            P.add("act", lambda e: e.activation(out=den, in_=den, func=AF.Ln), reads=["den"], writes=["den"])
            P.add("act", lambda e: e.activation(out=den, in_=den, func=AF.Exp, scale=-1.0), reads=["den"], writes=["den"])
            P.add("dve", lambda e, h=h: e.tensor_tensor(out=oT[:, h, :], in0=num, in1=den, op=ALU.mult),
                  reads=["num", "den"], writes=["oT"])
    P.barrier()

    cvT = A[:, 8192:24576].rearrange("p (k t) -> p k t", k=KC)
    xc = A[:, 24576:28704].bitcast(F32).rearrange("p (n q) -> p n q", n=16)
    caccs = [E[:, i * 4096:(i + 1) * 4096].bitcast(F32).rearrange("p (n q) -> p n q", n=16) for i in range(2)]
    ctmp = [A[:, 28704 + i * 1024:28704 + (i + 1) * 1024].bitcast(F32) for i in range(2)]
    P.add("pool", lambda e: e.dma_start(out=wcb, in_=wcb_d.rearrange("(k p) n -> p k n", p=128)),
          writes=["wcb"], chan="wcb")
    h2h_t = h2h.rearrange("p k (n q) -> p k n q", n=16)[:, :, 14:16, 127]
    cctr = [0]

    def conv_uc(fc, s):
        wu_ = RING[s][0]; wc_ = RING[s][1]
        for tb in range(4):
            i2 = cctr[0] % 2; cctr[0] += 1
            pu = 0 + i2; pc = 2 + i2
            for k in range(KC):
                P.add("pe", lambda e, k=k: e.matmul(pss[pu][:], lhsT=wu_[:, k, :], rhs=h2o[:, k, tb * 512:(tb + 1) * 512],
                                                    start=(k == 0), stop=(k == KC - 1)),
                      reads=[("rg", s, 0), "h2o"], writes=[("ps", pu)])
            for k in range(KC):
                P.add("pe", lambda e, k=k: e.matmul(pss[pc][:], lhsT=wc_[:, k, :], rhs=h2o[:, k, tb * 512:(tb + 1) * 512],
                                                    start=(k == 0), stop=(k == KC - 1)),
                      reads=[("rg", s, 1), "h2o"], writes=[("ps", pc)])
            P.add("act", lambda e: e.activation(out=ctmp[i2], in_=pss[pc][:], func=AF.Copy),
                  reads=[("ps", pc)], writes=[("ctmp", i2)])
            P.add("dve", lambda e: e.tensor_tensor(
                out=xc[:, 4 * tb:4 * tb + 4, 1:129], in0=pss[pu][:].rearrange("p (n q) -> p n q", n=4),
                in1=ctmp[i2].rearrange("p (n q) -> p n q", n=4), op=ALU.mult),
                reads=[("ps", pu), ("ctmp", i2)], pwrites=["xc"])
        for k in range(KC):
            P.add("pe", lambda e, k=k: e.matmul(pss[6][:, 0:2], lhsT=wu_[:, k, :], rhs=h2h_t[:, k, :],
                                                start=(k == 0), stop=(k == KC - 1)),
                  reads=[("rg", s, 0), "h2h"], writes=[("ps", 6)])
        for k in range(KC):
            P.add("pe", lambda e, k=k: e.matmul(pss[7][:, 0:2], lhsT=wc_[:, k, :], rhs=h2h_t[:, k, :],
                                                start=(k == 0), stop=(k == KC - 1)),
                  reads=[("rg", s, 1), "h2h"], writes=[("ps", 7)])
        P.add("act", lambda e: e.activation(out=cols[:, 58:60], in_=pss[7][:, 0:2], func=AF.Copy),
              reads=[("ps", 7)], writes=["cols"])
        P.add("dve", lambda e: e.scalar_tensor_tensor(out=xc[:, 14:16, 0], in0=pss[6][:, 0:2], scalar=hm[:, 1:2],
                                                      in1=cols[:, 58:60], op0=ALU.mult, op1=ALU.mult),
              reads=[("ps", 6), "cols", "hm"], writes=["xc"])

    def conv_taps(fc):
        cacc = caccs[fc % 2]; ck = ("cacc", fc % 2)
        w0 = cw[:, fc, 0:1]; w1 = cw[:, fc, 1:2]; w2 = cw[:, fc, 2:3]
        P.add("dve", lambda e: e.tensor_scalar(out=cacc[:, :, :], in0=xc[:, :, 1:129], scalar1=w2, scalar2=None, op0=ALU.mult),
              reads=["xc", "cw"], writes=[ck])
        P.add("dve", lambda e: e.scalar_tensor_tensor(out=cacc[:, 1:16, :], in0=xc[:, 0:15, 1:129], scalar=w1,
                                                      in1=cacc[:, 1:16, :], op0=ALU.mult, op1=ALU.add),
              reads=["xc", "cw", ck], writes=[ck])
        P.add("dve", lambda e: e.scalar_tensor_tensor(out=cacc[:, 0, :], in0=xc[:, 15, 0:128], scalar=w1,
                                                      in1=cacc[:, 0, :], op0=ALU.mult, op1=ALU.add),
              reads=["xc", "cw", ck], writes=[ck])
        P.add("dve", lambda e: e.scalar_tensor_tensor(out=cacc[:, 2:16, :], in0=xc[:, 0:14, 1:129], scalar=w0,
                                                      in1=cacc[:, 2:16, :], op0=ALU.mult, op1=ALU.add),
              reads=["xc", "cw", ck], writes=[ck])
        P.add("dve", lambda e: e.scalar_tensor_tensor(out=cacc[:, 0:2, :], in0=xc[:, 14:16, 0:128], scalar=w0,
                                                      in1=cacc[:, 0:2, :], op0=ALU.mult, op1=ALU.add),
              reads=["xc", "cw", ck], writes=[ck])

    def conv_b(fc, s):
        wb_ = RING[s][2]
        cacc = caccs[fc % 2]; ck = ("cacc", fc % 2)
        for tb in range(4):
            pbk = 4 + tb
            for k in range(KC):
                P.add("pe", lambda e, k=k: e.matmul(pss[pbk][:], lhsT=wb_[:, k, :], rhs=h2o[:, k, tb * 512:(tb + 1) * 512],
                                                    start=(k == 0), stop=(k == KC - 1)),
                      reads=[("rg", s, 2), "h2o"], writes=[("ps", pbk)])
            P.add("dve", lambda e: e.tensor_tensor(
                out=cvT[:, fc, tb * 512:(tb + 1) * 512], in0=pss[pbk][:],
                in1=cacc[:, 4 * tb:4 * tb + 4, :].rearrange("p n q -> p (n q)"), op=ALU.mult),
                reads=[("ps", pbk), ck], pwrites=["cvT"])

    cslots = {}
    for fc in range(KC):
        cslots[fc] = rloaded.pop(0)
        conv_uc(fc, cslots[fc])
        conv_taps(fc)
        if fc >= 1:
            conv_b(fc - 1, cslots[fc - 1])
            issue_ring()
    conv_b(KC - 1, cslots[KC - 1])
    issue_ring()
    P.barrier()

    wout_st = A[:, 24576:32768].rearrange("p (k n) -> p k n", k=KC)
    P.add("pool", lambda e: e.dma_start(out=wout_st, in_=wout_d.rearrange("(k p) n -> p k n", p=128)),
          writes=["wout_st"], chan="wout")
    mT = C[:].rearrange("p (k t) -> p k t", k=KC)
    tab = [E[:, i * 1024:(i + 1) * 1024].bitcast(F32) for i in range(4)]
    m12 = [E[:, 4096 + i * 1024:4096 + (i + 1) * 1024].bitcast(F32) for i in range(4)]
    gctr = [0]
    for oc in range(KC):
        s = rloaded.pop(0)
        wga = RING[s][0]; wgc = RING[s][1]
        for tb in range(4):
            i2 = gctr[0] % 2; gctr[0] += 1
            pya, pyc, pga, pgc = 4 * i2, 4 * i2 + 1, 4 * i2 + 2, 4 * i2 + 3
            tk = tb * 512
            for hh in range(4):
                P.add("pe", lambda e, hh=hh, tk=tk, pya=pya: e.matmul(pss[pya][:], lhsT=wab[:, hh, oc * 128:(oc + 1) * 128],
                                                                       rhs=oT[:, hh, tk:tk + 512], start=(hh == 0), stop=(hh == 3)),
                      reads=["wab", "oT"], writes=[("ps", pya)])
            for k in range(KC):
                P.add("pe", lambda e, k=k, tk=tk, pyc=pyc: e.matmul(pss[pyc][:], lhsT=wcb[:, k, oc * 128:(oc + 1) * 128],
                                                                     rhs=cvT[:, k, tk:tk + 512], start=(k == 0), stop=(k == KC - 1)),
                      reads=["wcb", "cvT"], writes=[("ps", pyc)])
            for k in range(KC):
                P.add("pe", lambda e, k=k, tk=tk, pga=pga: e.matmul(pss[pga][:], lhsT=wga[:, k, :], rhs=h2o[:, k, tk:tk + 512],
                                                                     start=(k == 0), stop=(k == KC - 1)),
                      reads=[("rg", s, 0), "h2o"], writes=[("ps", pga)])
            for k in range(KC):
                P.add("pe", lambda e, k=k, tk=tk, pgc=pgc: e.matmul(pss[pgc][:], lhsT=wgc[:, k, :], rhs=h2o[:, k, tk:tk + 512],
                                                                     start=(k == 0), stop=(k == KC - 1)),
                      reads=[("rg", s, 1), "h2o"], writes=[("ps", pgc)])
            ta = tab[2 * i2]; tc_ = tab[2 * i2 + 1]; m1 = m12[2 * i2]; m2 = m12[2 * i2 + 1]
            P.add("act", lambda e, ta=ta, pga=pga: e.activation(out=ta, in_=pss[pga][:], func=AF.Tanh, scale=0.5),
                  reads=[("ps", pga)], writes=[("ta", i2)])
            P.add("act", lambda e, tc_=tc_, pgc=pgc: e.activation(out=tc_, in_=pss[pgc][:], func=AF.Tanh, scale=0.5),
                  reads=[("ps", pgc)], writes=[("tc", i2)])
            P.add("dve", lambda e, ta=ta, m1=m1, pya=pya: e.scalar_tensor_tensor(out=m1, in0=ta, scalar=1.0, in1=pss[pya][:],
                                                                                 op0=ALU.add, op1=ALU.mult),
                  reads=[("ta", i2), ("ps", pya)], writes=[("m1", i2)])
            P.add("dve", lambda e, tc_=tc_, m2=m2, pyc=pyc: e.scalar_tensor_tensor(out=m2, in0=tc_, scalar=1.0, in1=pss[pyc][:],
                                                                                   op0=ALU.add, op1=ALU.mult),
                  reads=[("tc", i2), ("ps", pyc)], writes=[("m2", i2)])
            P.add("dve", lambda e, m1=m1, m2=m2, tk=tk: e.tensor_tensor(out=mT[:, oc, tk:tk + 512], in0=m1, in1=m2, op=ALU.add),
                  reads=[("m1", i2), ("m2", i2)], pwrites=["mT"])
        issue_ring()
    P.barrier()

    for tg in range(3):
        P.add("sp", lambda e, tg=tg: e.dma_start(out=xbuf[:, 4 * tg:4 * tg + 4, :], in_=x1s_d[:, 4 * tg:4 * tg + 4, :]),
              reads=[("x1s", tg)], writes=[("x", 4 * tg + i) for i in range(4)], chan=("xg", tg))
    for k in range(KC):
        P.add("dve", lambda e, k=k: e.tensor_tensor(out=wout[:, k, :], in0=wout_st[:, k, :], in1=gtrows[:, 1, :], op=ALU.mult),
              reads=["wout_st", ("gt", 1)], pwrites=["wout"])
    P.add("sp", lambda e: e.dma_start(out=xbuf[:, 12:16, :], in_=x1s_d[:, 12:16, :]),
          reads=[("x1s", 3)], writes=[("x", 12 + i) for i in range(4)] + ["wout_st"], chan=("xg", 3))
    for (j0, nj) in PIECES:
        q_piece(f2g_d, f2u_d, f2d_d, j0, nj)
    assert piece_ctr[0] % 3 == 0
    issue_load(); issue_load()
    for n in range(NT):
        for hf in range(2):
            pa = 2 * (n % 2) + hf
            for k in range(KC):
                P.add("pe", lambda e, k=k, n=n, hf=hf, pa=pa: e.matmul(pss[pa][:], lhsT=mT[:, k, n * 128:(n + 1) * 128],
                                                                        rhs=wout[:, k, hf * 512:(hf + 1) * 512],
                                                                        start=(k == 0), stop=(k == KC - 1)),
                      reads=["mT", "wout"], writes=[("ps", pa)])
            P.add("dve", lambda e, n=n, hf=hf, pa=pa: e.tensor_tensor(
                out=xbuf[:, n, hf * 512:(hf + 1) * 512], in0=pss[pa][:], in1=xbuf[:, n, hf * 512:(hf + 1) * 512], op=ALU.add),
                reads=[("ps", pa), ("x", n)], writes=[("x", n)])
    P.barrier()

    issue_load()

    def post_out(tg):
        P.add("sp", lambda e: e.dma_start(out=out_d[:, 4 * tg:4 * tg + 4, :], in_=xbuf[:, 4 * tg:4 * tg + 4, :]),
              reads=[("x", 4 * tg + i) for i in range(4)], writes=[("out", tg)], chan=("out", tg))

    ffn_pass(2, lambda tg: norm_front(tg, 64), lambda tg: norm_back(tg, 32, 40, 64), lambda tg: None, post_out, {})
    P.add("sp", None, reads=[("out", tg) for tg in range(4)])
    P.emit(nc)
    st.close()
    return nc


def _masks():
    m = np.zeros((128, 6, 128), np.float32)
    i = np.arange(128)
    mloc = [i, i, i]
    for g in range(3):
        ml = mloc[g]
        k = ml[:, None]; q = ml[None, :]
        m[:, 2 * g, :] = np.where(k >= q, 0.0, NEG)
        m[:, 2 * g + 1, :] = np.where(k <= q, 0.0, NEG)
    return m.astype(ml_dtypes.bfloat16)


def _host_inputs(inputs):
    x = np.ascontiguousarray(np.asarray(inputs["x"], dtype=np.float32))
    c = np.asarray(inputs["c"], dtype=np.float32)
    sq = lambda name: np.ascontiguousarray(np.asarray(inputs[name], dtype=np.float32)[0])
    col = lambda v: np.ascontiguousarray(v.reshape(KC, 128).T)
    shared = {
        "w_ada": sq("w_ada"), "b_ada": sq("b_ada"),
        "gcols": np.ascontiguousarray(np.concatenate([col(sq("norm_ffn1")), col(sq("norm_mix")), col(sq("norm_ffn2"))], axis=1)),
        "qkn": np.ascontiguousarray(np.stack([sq("q_norm"), sq("k_norm")], axis=1)),
        "cw": np.ascontiguousarray(sq("conv_w").reshape(3, KC, 128).transpose(2, 1, 0)),
        "bcol": np.ascontiguousarray(sq("b_ada").reshape(9 * KC, 128).T),
        "ffn1_w_gate": sq("ffn1_w_gate"), "ffn1_w_up": sq("ffn1_w_up"), "ffn1_w_down": sq("ffn1_w_down"),
        "ffn2_w_gate": sq("ffn2_w_gate"), "ffn2_w_up": sq("ffn2_w_up"), "ffn2_w_down": sq("ffn2_w_down"),
        "w_in": sq("w_in"), "w_attn_branch": sq("w_attn_branch"), "w_conv_branch": sq("w_conv_branch"),
        "w_out": sq("w_out"),
        "ident": np.eye(128).astype(ml_dtypes.bfloat16), "identf": np.eye(128, dtype=np.float32),
        "masks": _masks(),
    }
    in_maps = []
    for core in range(8):
        b, ch = core // 4, core % 4
        xo = x[b, ch * T:(ch + 1) * T].reshape(128, NT, D)
        xh = x[b, (ch - 1) * T:ch * T].reshape(128, NT, D) if ch > 0 else np.zeros((128, NT, D), np.float32)
        hm = np.zeros((128, 2), np.float32)
        hm[:, 0] = 0.0 if ch > 0 else NEG
        hm[:, 1] = 1.0 if ch > 0 else 0.0
        m = dict(shared)
        m.update({"xo": np.ascontiguousarray(xo), "xh": np.ascontiguousarray(xh), "cT": col(c[b]), "hm": hm})
        in_maps.append(m)
    return in_maps


_NC_CACHE = {}


def kernel(**inputs):
    in_maps = _host_inputs(inputs)
    if "nc" not in _NC_CACHE:
        _NC_CACHE["nc"] = build_nc()
    res = run_bass_kernel_spmd(_NC_CACHE["nc"], in_maps, core_ids=list(range(8)))
    out = np.empty((2, 4 * T, D), np.float32)
    for core in range(8):
        b, ch = core // 4, core % 4
        out[b, ch * T:(ch + 1) * T] = np.asarray(res.results[core]["out"]).reshape(T, D)
    return out
```
